# Optimizing a Trainium2 kernel written in Bass

```python
import jax
import jax.numpy as jnp
from jax import lax
import numpy as np

D_MODEL = 1024
BATCH = 8
SEQ = 2048
DEPTH = 1

CTX_LEN = 256
GRID_W = 64
FOURIER_W = 512
FOURIER_GROUPS = 8
RWKV_W = D_MODEL - FOURIER_W
HEAD = 64
N_HEADS = RWKV_W // HEAD
N_DIR = 2
DECAY_RANK = 64
ICLR_RANK = 64
GATE_RANK = 128
D_FF = 2816
RWKV_PROJ_W = 3 * RWKV_W + N_DIR * DECAY_RANK + N_DIR * ICLR_RANK + GATE_RANK
PROJ_W = FOURIER_W + RWKV_PROJ_W
NORM_EPS = 1e-6
GN_EPS = 64e-5
KK_EPS = 1e-12

kernel_name = "hybrid_fourier_rwkv7_convffn_prefix"


def rms_norm(x, g):
    xf = x.astype(jnp.float32)
    y = xf * lax.rsqrt(jnp.mean(xf * xf, axis=-1, keepdims=True) + NORM_EPS)
    return (y * g.astype(jnp.float32)).astype(x.dtype)


def modulate(h, shift, scale):
    return h * (1 + scale) + shift


def conv1d_centred(u, w):
    up = jnp.pad(u, ((0, 0), (1, 1), (0, 0)))
    return up[:, :-2] * w[0] + up[:, 1:-1] * w[1] + up[:, 2:] * w[2]


def conv2d_grid(u, w, rows):
    b, t, ch = u.shape
    g = u.reshape(b, rows, GRID_W, ch)
    y = lax.conv_general_dilated(g, w[:, :, None, :].astype(u.dtype), (1, 1), "SAME",
                                 dimension_numbers=("NHWC", "HWIO", "NHWC"),
                                 feature_group_count=ch)
    return y.reshape(b, t, ch)


def fourier_mix(u, g):
    b, t, _ = u.shape
    uf = u.astype(jnp.float32).reshape(b, t, FOURIER_GROUPS, FOURIER_W // FOURIER_GROUPS)
    y = jnp.fft.fft2(uf, axes=(1, 3), norm="ortho").real
    return rms_norm(y.reshape(b, t, FOURIER_W), g).astype(u.dtype)


def rwkv_inputs(u, conv_w, w0, w2, a0, a2, k_k, k_a):
    b, t, _ = u.shape
    rkv = conv1d_centred(u[..., :3 * RWKV_W], conv_w)
    r, k, v = jnp.split(rkv, 3, axis=-1)
    o = 3 * RWKV_W
    wd = u[..., o:o + N_DIR * DECAY_RANK].reshape(b, t, N_DIR, DECAY_RANK)
    o += N_DIR * DECAY_RANK
    ad = u[..., o:o + N_DIR * ICLR_RANK].reshape(b, t, N_DIR, ICLR_RANK)
    o += N_DIR * ICLR_RANK
    gd = u[..., o:]
    w_lora = (w0 + jnp.einsum("btdr,drc->btdc", jnp.tanh(wd), w2)).astype(jnp.float32)
    decay = jnp.exp(-jnp.exp(-jax.nn.softplus(-w_lora) - 0.5))
    a = jax.nn.sigmoid((a0 + jnp.einsum("btdr,drc->btdc", ad, a2)).astype(jnp.float32))
    kk = (k * k_k).astype(jnp.float32).reshape(b, t, N_HEADS, HEAD)
    kk = kk * lax.rsqrt(jnp.sum(kk * kk, axis=-1, keepdims=True) + KK_EPS)
    kk = kk.reshape(b, t, RWKV_W)
    kd = k.astype(jnp.float32)[:, :, None] * (1 + (a - 1) * k_a)
    return r, kd, v, kk, a, decay, gd


def _heads_time_major(x2):
    d, b, t, _ = x2.shape
    return x2.astype(jnp.float32).reshape(d, b, t, N_HEADS, HEAD).transpose(2, 0, 1, 3, 4)


def shared_dirs(x):
    return _heads_time_major(jnp.stack([x, jnp.flip(x, axis=1)], axis=0))


def per_dir(x):
    return _heads_time_major(jnp.stack([x[:, :, 0], jnp.flip(x[:, :, 1], axis=1)], axis=0))


def merge_dirs(y):
    y = y.transpose(1, 2, 0, 3, 4)
    return y[0] + jnp.flip(y[1], axis=1)


def rwkv_scan(prep, state0, readout):
    r, kd, v, kk, a, decay, _ = prep
    xs = (shared_dirs(r), per_dir(kd), shared_dirs(v), shared_dirs(kk), per_dir(a), per_dir(decay))

    def step(s, inp):
        r_t, k_t, v_t, kk_t, a_t, w_t = inp
        sa = jnp.einsum("dbhij,dbhj->dbhi", s, kk_t)
        s = (s * w_t[..., None, :] - sa[..., :, None] * (kk_t * a_t)[..., None, :]
             + v_t[..., :, None] * k_t[..., None, :])
        y = jnp.einsum("dbhij,dbhj->dbhi", s, r_t) if readout else None
        return s, y

    return lax.scan(step, state0, xs)


def rwkv_output(y, prep, g2, r_k, gn_g, gn_b):
    r, kd, v, _, _, _, gd = prep
    b, t = r.shape[:2]
    mu = jnp.mean(y, axis=-1, keepdims=True)
    var = jnp.mean(jnp.square(y - mu), axis=-1, keepdims=True)
    yn = ((y - mu) * lax.rsqrt(var + GN_EPS)).reshape(b, t, RWKV_W)
    yn = yn * gn_g.astype(jnp.float32) + gn_b.astype(jnp.float32)
    rh = r.astype(jnp.float32).reshape(b, t, N_HEADS, HEAD)
    vh = v.astype(jnp.float32).reshape(b, t, N_HEADS, HEAD)
    kdh = kd.reshape(b, t, N_DIR, N_HEADS, HEAD)
    bonus = jnp.einsum("bthn,btdhn,hn->bth", rh, kdh, r_k.astype(jnp.float32))[..., None] * vh
    gate = jax.nn.sigmoid(gd.astype(jnp.float32)) @ g2.astype(jnp.float32)
    return (yn + bonus.reshape(b, t, RWKV_W)) * gate


def token_mixer(h, hc, w_in, conv_w, w0, w2, a0, a2, g2, k_k, k_a, r_k, gn_g, gn_b,
                f_g, w_out, ctx_out):
    u = h @ w_in
    uc = hc @ (w_in if ctx_out else w_in[:, FOURIER_W:])
    uc_r = uc[..., FOURIER_W:] if ctx_out else uc
    prep_l = rwkv_inputs(u[..., FOURIER_W:], conv_w, w0, w2, a0, a2, k_k, k_a)
    prep_c = rwkv_inputs(uc_r, conv_w, w0, w2, a0, a2, k_k, k_a)
    state0 = jnp.zeros((N_DIR, hc.shape[0], N_HEADS, HEAD, HEAD), jnp.float32)
    state_c, y_c = rwkv_scan(prep_c, state0, ctx_out)
    _, y_l = rwkv_scan(prep_l, state_c, True)
    rw_l = rwkv_output(merge_dirs(y_l), prep_l, g2, r_k, gn_g, gn_b).astype(h.dtype)
    out = jnp.concatenate([fourier_mix(u[..., :FOURIER_W], f_g).astype(h.dtype), rw_l], axis=-1) @ w_out
    out_c = None
    if ctx_out:
        rw_c = rwkv_output(merge_dirs(y_c), prep_c, g2, r_k, gn_g, gn_b).astype(hc.dtype)
        out_c = jnp.concatenate([fourier_mix(uc[..., :FOURIER_W], f_g).astype(hc.dtype), rw_c], axis=-1) @ w_out
    return out, out_c


def conv_ffn(h, w_up, conv_w, conv_b, w_down, rows):
    u = h @ w_up
    if rows is None:
        u = conv1d_centred(u, conv_w[1]) + conv_b
    else:
        u = conv2d_grid(u, conv_w, rows) + conv_b
    gate, val = jnp.split(u, 2, axis=-1)
    return (jax.nn.silu(gate) * val) @ w_down


def setup_inputs(seed: int = 0) -> dict:
    key = jax.random.key(seed)
    ks = jax.random.split(key, 27)
    L, D, C, F2 = DEPTH, D_MODEL, RWKV_W, 2 * D_FF

    def nrm(i, shape, s=1.0):
        return s * jax.random.normal(ks[i], shape, jnp.float32)

    def near_one(i, shape):
        return 1.0 + nrm(i, shape, 0.05)

    centre3 = jnp.zeros((3,), jnp.float32).at[1].set(1.0)
    centre33 = jnp.zeros((3, 3), jnp.float32).at[1, 1].set(1.0)
    return {
        "x": nrm(0, (BATCH, SEQ, D)),
        "c": nrm(1, (BATCH, D)),
        "ctx": nrm(2, (BATCH, CTX_LEN, D)),
        "c_ctx": nrm(3, (D,)),
        "ada_w": nrm(4, (L, D, 6 * D), 0.5 * D ** -0.5),
        "ada_b": nrm(5, (L, 6 * D), 0.02),
        "norm1_g": near_one(6, (L, D)),
        "norm2_g": near_one(7, (L, D)),
        "w_in": nrm(8, (L, D, PROJ_W), D ** -0.5),
        "rwkv_conv_w": centre3[None, :, None] + nrm(9, (L, 3, 3 * C), 0.1),
        "decay_w0": -1.0 + nrm(10, (L, N_DIR, C), 1.5),
        "decay_w2": nrm(11, (L, N_DIR, DECAY_RANK, C), 0.1 * DECAY_RANK ** -0.5),
        "iclr_a0": nrm(12, (L, N_DIR, C), 0.5),
        "iclr_a2": nrm(13, (L, N_DIR, ICLR_RANK, C), 0.1 * ICLR_RANK ** -0.5),
        "gate_g2": nrm(14, (L, GATE_RANK, C), GATE_RANK ** -0.5),
        "k_k": 0.85 + nrm(15, (L, C), 0.05),
        "k_a": near_one(16, (L, C)),
        "r_k": nrm(17, (L, N_HEADS, HEAD), 0.1),
        "gn_g": near_one(18, (L, C)),
        "gn_b": nrm(19, (L, C), 0.02),
        "fourier_g": near_one(20, (L, FOURIER_W)),
        "w_out": nrm(21, (L, D, D), D ** -0.5),
        "ffn_w_up": nrm(22, (L, D, F2), D ** -0.5),
        "ffn_conv_w": centre33[None, :, :, None] + nrm(23, (L, 3, 3, F2), 0.1),
        "ffn_conv_b": nrm(24, (L, F2), 0.02),
        "ffn_w_down": nrm(25, (L, D_FF, D), D_FF ** -0.5),
        "final_g": near_one(26, (D,)),
    }


def reference(x, c, ctx, c_ctx, ada_w, ada_b, norm1_g, norm2_g, w_in, rwkv_conv_w,
              decay_w0, decay_w2, iclr_a0, iclr_a2, gate_g2, k_k, k_a, r_k, gn_g, gn_b,
              fourier_g, w_out, ffn_w_up, ffn_conv_w, ffn_conv_b, ffn_w_down, final_g):
    rows = x.shape[1] // GRID_W
    for l in range(DEPTH):
        last = l == DEPTH - 1
        mod_x = jax.nn.silu(c) @ ada_w[l] + ada_b[l]
        mod_c = jax.nn.silu(c_ctx) @ ada_w[l] + ada_b[l]
        sh1, sc1, g1, sh2, sc2, g2 = [m[:, None, :] for m in jnp.split(mod_x, 6, axis=-1)]
        sh1c, sc1c, g1c, sh2c, sc2c, g2c = jnp.split(mod_c, 6, axis=-1)

        h = modulate(rms_norm(x, norm1_g[l]), sh1, sc1)
        hc = modulate(rms_norm(ctx, norm1_g[l]), sh1c, sc1c)
        y, yc = token_mixer(h, hc, w_in[l], rwkv_conv_w[l], decay_w0[l], decay_w2[l],
                            iclr_a0[l], iclr_a2[l], gate_g2[l], k_k[l], k_a[l], r_k[l],
                            gn_g[l], gn_b[l], fourier_g[l], w_out[l], not last)
        x = x + g1 * y
        h = modulate(rms_norm(x, norm2_g[l]), sh2, sc2)
        x = x + g2 * conv_ffn(h, ffn_w_up[l], ffn_conv_w[l], ffn_conv_b[l], ffn_w_down[l], rows)
        if not last:
            ctx = ctx + g1c * yc
            hc = modulate(rms_norm(ctx, norm2_g[l]), sh2c, sc2c)
            ctx = ctx + g2c * conv_ffn(hc, ffn_w_up[l], ffn_conv_w[l], ffn_conv_b[l],
                                       ffn_w_down[l], None)
    return rms_norm(x, final_g)
```

```python
import contextlib
import os
import numpy as np
import ml_dtypes
import concourse.bass as bass
import concourse.mybir as mybir
from concourse.bass_utils import run_bass_kernel_spmd

F32 = mybir.dt.float32
BF16 = mybir.dt.bfloat16
F32R = mybir.dt.float32r
AF = mybir.ActivationFunctionType
ALU = mybir.AluOpType
AX = mybir.AxisListType

N_DMA_SEMS = 24
LAM = float(np.exp(-0.5))
NORM_EPS = 1e-6
GN_EPS = 64e-5
KK_EPS = 1e-12
SAME_ENGINE_SYNC = True
SAME_ENGINE_RAW_ONLY = True
SAME_ENGINE_RAW_ENGINES = ("dve", "act", "pool")


class Buf:
    __slots__ = ("name", "lw", "rd")

    def __init__(self, name=""):
        self.name = name
        self.lw = None
        self.rd = {}


class Op:
    __slots__ = ("eng", "fn", "deps", "signal", "sig", "is_dma", "slot", "target", "prev_on_slot")

    def __init__(self, eng, fn):
        self.eng = eng
        self.fn = fn
        self.deps = []
        self.signal = False
        self.sig = 0
        self.is_dma = False
        self.slot = 0
        self.target = 0
        self.prev_on_slot = None


class Prog:
    ENGS = ("pe", "dve", "act", "pool", "sp")

    def __init__(self, nc, same_engine_sync=True):
        self.nc = nc
        self.ops = {e: [] for e in self.ENGS}
        self.same_engine_sync = same_engine_sync
        self.dma_rr = 0
        self.dma_last = [None] * N_DMA_SEMS
        self.dma_cnt = [0] * N_DMA_SEMS
        self.bank_rr = 0

    def _add_deps(self, op, reads, writes):
        deps = {}

        def add(d, raw):
            if d is None or d is op:
                return
            if d.eng == op.eng and not d.is_dma and not op.is_dma:
                if not self.same_engine_sync:
                    return
                if op.eng == "pe":
                    return
                if not raw and SAME_ENGINE_RAW_ONLY and op.eng != "pool":
                    return
                if op.eng not in SAME_ENGINE_RAW_ENGINES:
                    return
            deps[id(d)] = d

        for b in reads:
            add(b.lw, True)
        for b in writes:
            add(b.lw, False)
            for r in b.rd.values():
                add(r, False)
        op.deps = list(deps.values())
        for d in op.deps:
            d.signal = True
        for b in reads:
            if b.name.startswith("ps") and b.rd:
                assert all(k == op.eng for k in b.rd), "two engines reading PSUM bank %s" % b.name
            b.rd[op.eng if not op.is_dma else ("dma", id(op))] = op
        for b in writes:
            b.lw = op
            b.rd = {}

    def op(self, eng, fn, reads=(), writes=()):
        o = Op(eng, fn)
        self._add_deps(o, reads, writes)
        self.ops[eng].append(o)
        return o

    def dma(self, out, in_, reads=(), writes=(), eng="sp", **kw):
        o = Op(eng, lambda e: e.dma_start(out=out, in_=in_, **kw))
        o.is_dma = True
        o.signal = True
        s = self.dma_rr
        self.dma_rr = (self.dma_rr + 1) % N_DMA_SEMS
        o.slot = s
        self.dma_cnt[s] += 16
        o.target = self.dma_cnt[s]
        o.prev_on_slot = self.dma_last[s]
        self.dma_last[s] = o
        self._add_deps(o, reads, writes)
        self.ops[eng].append(o)
        return o

    def barrier(self, scratch_out, scratch_in):
        o = self.dma(scratch_out, scratch_in)
        deps = {}
        for e in ("pe", "dve", "act", "pool"):
            for q in reversed(self.ops[e]):
                if not q.is_dma and q.fn is not None:
                    deps[id(q)] = q
                    q.signal = True
                    break
        for d in self.dma_last:
            if d is not None and d is not o:
                deps[id(d)] = d
        o.deps = list(deps.values())
        for e in ("pe", "dve", "act", "pool"):
            w = Op(e, None)
            w.deps = [o]
            self.ops[e].append(w)

    def finish(self, ops):
        o = Op("sp", None)
        o.deps = list(ops)
        self.ops["sp"].append(o)

    def mm(self, out, lhsT, rhs, start=True, stop=True, reads=(), writes=()):
        return self.op("pe", lambda e: e.matmul(out, lhsT, rhs, start=start, stop=stop), reads, writes)

    def tr(self, out, in_, ident, reads=(), writes=()):
        return self.op("pe", lambda e: e.transpose(out, in_, ident), reads, writes)

    def act(self, out, in_, func, bias=None, scale=None, accum=None, reads=(), writes=()):
        kw = {}
        if bias is not None:
            kw["bias"] = bias
        if scale is not None:
            kw["scale"] = scale
        if accum is not None:
            kw["accum_out"] = accum
        return self.op("act", lambda e: e.activation(out=out, in_=in_, func=func, **kw), reads, writes)

    def tt(self, eng, out, in0, in1, op, reads=(), writes=()):
        return self.op(eng, lambda e: e.tensor_tensor(out=out, in0=in0, in1=in1, op=op), reads, writes)

    def ts(self, eng, out, in0, s1, op0, s2=None, op1=None, reads=(), writes=()):
        if op1 is None and eng == "pool" and op0 == ALU.mult:
            return self.op(eng, lambda e: e.tensor_scalar(out=out, in0=in0, scalar1=s1, scalar2=1.0, op0=ALU.mult, op1=ALU.mult), reads, writes)
        if op1 is None:
            return self.op(eng, lambda e: e.tensor_scalar(out=out, in0=in0, scalar1=s1, scalar2=None, op0=op0), reads, writes)
        return self.op(eng, lambda e: e.tensor_scalar(out=out, in0=in0, scalar1=s1, scalar2=s2, op0=op0, op1=op1), reads, writes)

    def stt(self, out, in0, scalar, in1, op0, op1, reads=(), writes=()):
        return self.op("dve", lambda e: e.scalar_tensor_tensor(out=out, in0=in0, scalar=scalar, in1=in1, op0=op0, op1=op1), reads, writes)

    def copy(self, eng, out, in_, reads=(), writes=()):
        if eng == "act":
            return self.op("act", lambda e: e.copy(out=out, in_=in_), reads, writes)
        return self.op(eng, lambda e: e.tensor_copy(out=out, in_=in_), reads, writes)

    def memset(self, eng, ap, val, writes=()):
        return self.op(eng, lambda e: e.memset(ap, val), (), writes)

    def emit(self):
        nc = self.nc
        for e in self.ENGS:
            c = 0
            for o in self.ops[e]:
                if o.is_dma or o.fn is None:
                    continue
                if o.signal:
                    c += 1
                    o.sig = c
        with contextlib.ExitStack() as st:
            sems = {e: st.enter_context(nc.semaphore("s_" + e)) for e in self.ENGS}
            dsems = [st.enter_context(nc.semaphore("d%d" % i)) for i in range(N_DMA_SEMS)]
            block = st.enter_context(nc.Block())

            def run(e, engobj):
                waited = {}
                for o in self.ops[e]:
                    deps = list(o.deps)
                    if o.is_dma and o.prev_on_slot is not None:
                        deps.append(o.prev_on_slot)
                    for d in deps:
                        if d.is_dma:
                            key = ("d", d.slot)
                            if waited.get(key, 0) < d.target:
                                engobj.wait_ge(dsems[d.slot], d.target)
                                waited[key] = d.target
                        else:
                            key = d.eng
                            if waited.get(key, 0) < d.sig:
                                engobj.wait_ge(sems[d.eng], d.sig)
                                waited[key] = d.sig
                    if o.fn is None:
                        continue
                    ins = o.fn(engobj)
                    if o.is_dma:
                        ins.then_inc(dsems[o.slot], 16)
                    elif o.signal:
                        ins.then_inc(sems[e], 1)

            @block.tensor
            def _(eng):
                run("pe", eng)

            @block.vector
            def _(eng):
                run("dve", eng)

            @block.scalar
            def _(eng):
                run("act", eng)

            @block.gpsimd
            def _(eng):
                run("pool", eng)

            @block.sync
            def _(eng):
                run("sp", eng)


class Alloc:
    def __init__(self, arena, lo_kib, hi_kib):
        self.a = arena
        self.off = lo_kib * 256
        self.hi = hi_kib * 256

    def _take(self, words):
        o = self.off
        self.off += words
        assert self.off <= self.hi, "arena overflow %d > %d" % (self.off, self.hi)
        return o

    @staticmethod
    def _shape(ap, shape):
        if len(shape) == 2:
            return ap
        if len(shape) == 3:
            return ap.rearrange("p (a b) -> p a b", a=shape[1])
        if len(shape) == 4:
            return ap.rearrange("p (a b c) -> p a b c", a=shape[1], b=shape[2])
        raise ValueError

    def f32(self, shape):
        n = int(np.prod(shape[1:]))
        o = self._take(n)
        return self._shape(self.a[0:shape[0], o:o + n], shape)

    def bf16(self, shape):
        n = int(np.prod(shape[1:]))
        w = (n + 1) // 2
        o = self._take(w)
        ap = self.a[0:shape[0], o:o + w].bitcast(BF16)
        if w * 2 != n:
            ap = ap[:, 0:n]
        return self._shape(ap, shape)


C_ID, C_ONE, C_BLK, C_SEL, C_IDF = 0, 128, 256, 384, 512
C_M4 = (576, 1088)
C_MP = (1600, 1856)
C_END = 2112


def build_program(dbg=None, stop_after=99):
    dbg = dbg or []
    nc = bass.Bass("TRN2", target_bir_lowering=False)
    p = Prog(nc, same_engine_sync=SAME_ENGINE_SYNC)

    def din(name, shape, dt=F32):
        return nc.dram_tensor(name, list(shape), dt, kind="ExternalInput").ap()

    x_d = din("x", [2048, 1024])
    ctx_d = din("ctx", [256, 1024])
    c2_d = din("c2", [2, 1024])
    adaw_d = din("ada_w", [1024, 6144])
    adab_d = din("ada_b", [1, 6144])
    v1024_d = din("v1024", [2, 1024])
    v1536_d = din("v1536", [3, 1536])
    v512_d = din("v512", [10, 512])
    v5632_d = din("v5632", [10, 5632])
    fg_d = din("final_g", [1, 1024])
    win_d = din("w_in", [1024, 2432])
    w2_d = din("decay_w2", [128, 512])
    a2_d = din("iclr_a2", [128, 512])
    g2_d = din("gate_g2", [128, 512])
    wout_d = din("w_out", [1024, 1024])
    wup_d = din("ffn_w_up", [22, 128, 2048])
    wdn_d = din("ffn_w_down", [2816, 1024])
    cst_d = din("cst", [128, C_END])
    cgs_d = din("cgs", [128, 256], BF16)
    dft_d = din("dft", [2, 2048, 2048], BF16)
    out_d = nc.dram_tensor("out", [2048, 1024], F32, kind="ExternalOutput").ap()
    dbg_outs = {}

    with contextlib.ExitStack() as st:
        def sb(name, shape, dt=F32):
            return st.enter_context(nc.sbuf_tensor(name, list(shape), dt))

        cst = sb("cst_sb", [128, C_END])
        ident = cst[:, C_ID:C_ID + 128]
        ones = cst[:, C_ONE:C_ONE + 128]
        blkones = cst[:, C_BLK:C_BLK + 128]
        sel = cst[:, C_SEL:C_SEL + 128].rearrange("p (h k) -> p h k", h=2)
        idfold = cst[:, C_IDF:C_IDF + 64]
        mask4 = [cst[:, C_M4[d]:C_M4[d] + 512] for d in range(2)]
        mP2 = [cst[:, C_MP[d]:C_MP[d] + 256] for d in range(2)]
        cgs = sb("cgs_sb", [128, 256], BF16)
        pc1024 = sb("pc1024", [128, 8, 2])
        pc1536 = sb("pc1536", [128, 12, 3])
        pc512 = sb("pc512", [128, 4, 10])
        pc5632 = sb("pc5632", [128, 44, 10])
        modcol = sb("modcol", [128, 64])
        AB = sb("AB", [128, 6, 8])
        omka = sb("omka", [128, 4])
        bc = sb("bc", [128, 3, 1024])
        scratch = sb("scratch", [128, 16])
        arena = sb("arena", [128, 180 * 256])
        psum = [st.enter_context(nc.psum_tensor("ps%d" % i, [128, 512], F32)) for i in range(8)]
        pbuf = [Buf("ps%d" % i) for i in range(8)]
        bcst, bcgs, bprm, bmod, bAB, bbc = Buf("cst"), Buf("cgs"), Buf("prm"), Buf("modcol"), Buf("AB"), Buf("bc")

        def bank():
            i = p.bank_rr
            p.bank_rr = (p.bank_rr + 1) % 8
            return psum[i], pbuf[i]

        def barrier():
            p.barrier(scratch[0:1, 0:8], cst_d[0:1, 0:8])

        def dump(name, ap, reads):
            if name not in dbg:
                return
            shape = list(ap.shape)
            t = nc.dram_tensor("dbg_" + name, shape, ap.dtype, kind="ExternalOutput").ap()
            dbg_outs[name] = p.dma(t, ap, reads=reads)

        p.dma(cst[:], cst_d[:, :], writes=[bcst])
        p.dma(cgs[:], cgs_d[:, :], writes=[bcgs])

        hT_al = Alloc(arena, 0, 52)
        hT = hT_al.bf16([128, 8, 2048])
        hcT = hT_al.bf16([128, 8, 256])
        rwT = hT_al.bf16([128, 4, 2048])
        hTb = [[Buf("hT%d_%d" % (k, i)) for i in range(16)] for k in range(8)]
        hcTb = [Buf("hcT%d" % k) for k in range(8)]
        rwTb = [Buf("rwT%d" % i) for i in range(4)]

        al = Alloc(arena, 52, 176)
        stage = [al.f32([128, 8, 512]) for _ in range(3)]
        bstage = [Buf("st0"), Buf("st1"), Buf("st2")]
        modrow = al.f32([33, 6144])
        adab = al.f32([33, 6144])
        prm = al.f32([10, 5632])
        c2t = al.f32([128, 2, 8])
        scT = al.f32([128, 8, 33])
        bmodrow, badab, bc2, bsc = Buf("modrow"), Buf("adab"), Buf("c2"), Buf("sc")

        p.dma(c2t, c2_d.rearrange("r (p k) -> p r k", k=8), writes=[bc2])
        p.memset("pool", scT, 0.0, writes=[bsc])
        p.memset("pool", adab[0:33, :], 0.0, writes=[badab])
        p.dma(adab[0:1, :], adab_d[:, :], reads=(), writes=[badab])
        p.dma(adab[32:33, :], adab_d[:, :], reads=(), writes=[badab])
        p.act(scT[:, :, 0], c2t[:, 0, :], AF.Silu, reads=[bc2], writes=[bsc])
        p.act(scT[:, :, 32], c2t[:, 1, :], AF.Silu, reads=[bc2, bsc], writes=[bsc])
        adaw_v = adaw_d.rearrange("(p k) n -> p k n", k=8)
        for nb in range(12):
            s = nb % 3
            p.dma(stage[s], adaw_v[:, :, nb * 512:(nb + 1) * 512], writes=[bstage[s]])
            ps, pb = bank()
            for kc in range(8):
                p.mm(ps[0:33, :], scT[:, kc, :], stage[s][:, kc, :], start=(kc == 0), stop=(kc == 7),
                     reads=[bsc, bstage[s]], writes=[pb])
            p.tt("dve", modrow[0:33, nb * 512:(nb + 1) * 512], ps[0:33, :], adab[0:33, nb * 512:(nb + 1) * 512],
                 ALU.add, reads=[pb, badab], writes=[bmodrow])
        dump("modrow", modrow[0:33, :], [bmodrow])
        ps, pb = bank()
        for v in range(6):
            for kc in range(8):
                j = v * 8 + kc
                p.mm(ps[:, 2 * j:2 * j + 2], modrow[0:1, v * 1024 + kc * 128:v * 1024 + (kc + 1) * 128],
                     ones[0:1, 0:2], reads=[bmodrow, bcst], writes=[pb])
        for v in range(2):
            for kc in range(8):
                j = 48 + v * 8 + kc
                p.mm(ps[:, 2 * j:2 * j + 2],
                     modrow[32:33, v * 1024 + kc * 128:v * 1024 + (kc + 1) * 128],
                     ones[32:33, 0:2], reads=[bmodrow, bcst], writes=[pb])
        p.copy("dve", modcol[:, :], ps[:, 0:128].rearrange("p (j t) -> p j t", t=2)[:, :, 0], reads=[pb], writes=[bmod])
        for (src, rows, width, dst) in ((v1024_d, 2, 1024, pc1024), (v1536_d, 3, 1536, pc1536),
                                        (v512_d, 10, 512, pc512), (v5632_d, 10, 5632, pc5632)):
            nch = width // 128
            re = rows + (rows % 2)
            if re != rows:
                p.memset("pool", prm[0:re, 0:width], 0.0, writes=[bprm])
            p.dma(prm[0:rows, 0:width], src[:, :], reads=[bprm], writes=[bprm])
            ps, pb = bank()
            for c in range(nch):
                p.tr(ps[:, c * re:(c + 1) * re], prm[0:re, c * 128:(c + 1) * 128], ident[0:re, 0:re],
                     reads=[bprm, bcst], writes=[pb])
            p.copy("dve", dst[:, :, :], ps[:, 0:nch * re].rearrange("p (c r) -> p c r", r=re)[:, :, 0:rows], reads=[pb], writes=[bprm])
        for (ai, sc_off, sh_off, gi) in ((0, 8, 0, 0), (2, 56, 48, 0), (4, 32, 24, 1)):
            p.ts("dve", AB[:, ai, :], modcol[:, sc_off:sc_off + 8], 1.0, ALU.add, reads=[bmod], writes=[bAB])
            p.tt("dve", AB[:, ai, :], AB[:, ai, :], pc1024[:, :, gi], ALU.mult, reads=[bAB, bprm], writes=[bAB])
            p.copy("dve", AB[:, ai + 1, :], modcol[:, sh_off:sh_off + 8], reads=[bmod], writes=[bAB])
        p.ts("dve", omka[:, :], pc512[:, :, 5], -1.0, ALU.mult, 1.0, ALU.add, reads=[bprm], writes=[bAB])
        p.dma(prm[0:1, 0:1024], fg_d[:, :], reads=[bprm], writes=[bprm])
        for (bi, row_ap) in ((0, modrow[0:1, 2048:3072]), (1, modrow[0:1, 5120:6144]), (2, prm[0:1, 0:1024])):
            for nh in range(2):
                ps, pb = bank()
                p.mm(ps[:, :], ones[0:1, 0:128], row_ap[:, nh * 512:(nh + 1) * 512],
                     reads=[bmodrow, bprm, bcst], writes=[pb])
                p.copy("act", bc[:, bi, nh * 512:(nh + 1) * 512], ps[:, :], reads=[pb], writes=[bbc])
        dump("AB", AB.rearrange("p a k -> p (a k)"), [bAB])
        dump("bc", bc.rearrange("p a k -> p (a k)"), [bbc])
        barrier()
        if stop_after <= 1:
            return _finish(nc, p, out_d, dbg_outs)

        def norm_to_T(al, src_rows, ntiles, A_ap, B_ap, dstT, dst_bufs_fn, src_is_sbuf=None, src_bufs=None):
            xt = [al.f32([128, 1024]) for _ in range(3)] if src_is_sbuf is None else None
            xs = [al.f32([128, 1024]) for _ in range(2)]
            junk = al.bf16([128, 1024])
            ss = al.f32([128, 32])
            bxt = [Buf() for _ in range(3)]
            bxs = [Buf() for _ in range(2)]
            bj = Buf()
            bss_l = [Buf() for _ in range(ntiles)]
            import os
            BIS = os.environ.get("BIS", "")
            for i in range(ntiles if "1" not in BIS else 1):
                bss = bss_l[i]
                if src_is_sbuf is None:
                    t = xt[i % 3]
                    bt = bxt[i % 3]
                    p.dma(t, src_rows(i), writes=[bt])
                else:
                    t = src_rows(i)
                    bt = src_bufs[i]
                p.act(junk, t, AF.Square, accum=ss[:, i:i + 1], reads=[bt], writes=[bj, bss])
                p.act(ss[:, i:i + 1], ss[:, i:i + 1], AF.Sqrt, bias=NORM_EPS, scale=1.0 / 1024.0, reads=[bss], writes=[bss])
                p.op("dve", lambda e, i=i: e.reciprocal(out=ss[:, i:i + 1], in_=ss[:, i:i + 1]), [bss], [bss])
                s2 = i % 2
                p.act(xs[s2], t, AF.Copy, scale=ss[:, i:i + 1], reads=[bt, bss], writes=[bxs[s2]])
                if "t" in BIS:
                    continue
                pa, pba = bank()
                pbk, pbb = bank()
                for kc in range(8):
                    pp, ppb = (pa, pba) if kc < 4 else (pbk, pbb)
                    p.tr(pp[:, (kc % 4) * 128:(kc % 4 + 1) * 128], xs[s2][:, kc * 128:(kc + 1) * 128], ident,
                         reads=[bxs[s2], bcst], writes=[ppb])
                for kc in range(8):
                    if "e" in BIS:
                        continue
                    pp, ppb = (pa, pba) if kc < 4 else (pbk, pbb)
                    src = pp[:, (kc % 4) * 128:(kc % 4 + 1) * 128]
                    dst = dstT[:, kc, i * 128:(i + 1) * 128]
                    if (kc < 4 and "D" not in BIS) or "A" in BIS:
                        p.act(dst, src, AF.Identity, bias=B_ap[:, kc:kc + 1], scale=A_ap[:, kc:kc + 1],
                              reads=[ppb, bAB], writes=[dst_bufs_fn(kc, i)])
                    else:
                        p.ts("dve", dst, src, A_ap[:, kc:kc + 1], ALU.mult, B_ap[:, kc:kc + 1], ALU.add,
                             reads=[ppb, bAB], writes=[dst_bufs_fn(kc, i)])

        al = Alloc(arena, 52, 176)
        xv = x_d.rearrange("(i p) d -> i p d", p=128)
        norm_to_T(al, lambda i: xv[i], 16, AB[:, 0, :], AB[:, 1, :], hT, lambda kc, i: hTb[kc][i])
        cv = ctx_d.rearrange("(i p) d -> i p d", p=128)
        al = Alloc(arena, 100, 176)
        norm_to_T(al, lambda i: cv[i], 2, AB[:, 2, :], AB[:, 3, :], hcT, lambda kc, i: hcTb[kc])
        dump("hT", hT.rearrange("p k t -> p (k t)"), [b for r in hTb for b in r])
        dump("hcT", hcT.rearrange("p k t -> p (k t)"), hcTb)
        barrier()
        if stop_after <= 2:
            return _finish(nc, p, out_d, dbg_outs)
        al = Alloc(arena, 52, 180)
        WID = 2320
        arr = {n: al.f32([128, WID]) for n in ("r", "k", "v", "kk", "sig", "a")}
        barr = {n: Buf(n) for n in arr}
        Ysum = al.f32([128, 16, 128])
        bY = [Buf("Y%d" % i) for i in range(16)]
        bsum = al.f32([128, 2048])
        bbs = Buf("bsum")
        twd = al.bf16([128, 2304])
        adT = al.bf16([128, 2304])
        sgd = al.bf16([128, 2048])
        btwd, badT, bsgd = Buf("twd"), Buf("adT"), Buf("sgd")
        w2b = al.bf16([128, 512])
        a2b = al.bf16([128, 512])
        g2b = al.bf16([128, 512])
        bw2 = Buf("w2")
        wrkv = arr["kk"][:, 0:1536].bitcast(BF16).rearrange("p (k x n) -> p k x n", k=8, x=3)
        bwrkv = Buf("wrkv")
        al_tmp = Alloc(arena, 52, 180)
        wlr = al_tmp.bf16([128, 8, 384])
        wlr_st = al_tmp.f32([128, 8, 384])
        assert al_tmp.off <= 52 * 256 + 2 * WID
        bwlr = Buf("wlr")

        class WK:
            pass
        GRP = int(os.environ.get("GRP", "3"))
        TS = []
        for s in range(2):
            w_ = WK()
            for nm in ("cum", "Dm", "Em", "eI", "eIn", "eE", "kdc", "bcc"):
                setattr(w_, nm, al.f32([128, 128]))
                setattr(w_, "b" + nm, Buf(nm + str(s)))
            TS.append(w_)
        PS = []
        for s in range(GRP + 1):
            w_ = WK()
            for nm in ("Kmg", "Bmg"):
                setattr(w_, nm, al.f32([128, 128]))
                setattr(w_, "b" + nm, Buf(nm + str(s)))
            w_.bKm = Buf("Km%d" % s)
            w_.bBm = Buf("Bm%d" % s)
            w_.KmH = [al.bf16([128, 128]) for _ in range(2)]
            w_.BmH = [al.bf16([128, 128]) for _ in range(2)]
            for t_ in w_.KmH + w_.BmH:
                p.memset("pool", t_, 0.0, writes=[w_.bKm, w_.bBm])
            w_.QR = al.bf16([128, 256])
            w_.QRf = al.f32([128, 128])
            w_.bQR = Buf("QR%d" % s)
            w_.cols = al.f32([128, 8])
            w_.bcols = Buf("cols%d" % s)
            w_.diagG = al.bf16([128, 64])
            w_.bdiag = Buf("diag%d" % s)
            PS.append(w_)

        def wview(j):
            w_ = WK()
            w_.__dict__.update(TS[j % 2].__dict__)
            w_.__dict__.update(PS[j % (GRP + 1)].__dict__)
            return w_
        slot_lo = al.off
        SLOT = []
        for s_ in range(GRP):
            SLOT.append(dict(tok=al.bf16([128, 384]), Xb=al.bf16([128, 2, 128]), gram=[al.bf16([128, 512]) for _ in range(2)],
                             P2=al.bf16([128, 256]), GP=[al.bf16([128, 512]) for _ in range(2)], small=al.bf16([128, 384]),
                             btok=Buf("tok%d" % s_), bX=Buf("X%d" % s_), bP2=Buf("P2%d" % s_), bsmall=Buf("small%d" % s_),
                             bgram=[Buf("gram0%d" % s_), Buf("gram1%d" % s_)], bGP=[Buf("GP0%d" % s_), Buf("GP1%d" % s_)]))
        Hs = [al.bf16([128, 2, 64]) for _ in range(2)]
        selb = al.bf16([128, 2, 64])
        p.copy("pool", selb.rearrange("p h k -> p (h k)"), cst[:, C_SEL:C_SEL + 128], reads=[bcst], writes=[bcst])
        identb3 = al.bf16([128, 128])
        p.copy("pool", identb3, ident, reads=[bcst], writes=[bcst])
        al_alias = Alloc(arena, 0, 180)
        al_alias.off = slot_lo
        tmpbc = al_alias.f32([128, 1024])
        tmpb = tmpbc[:, 0:512]
        tmpc = tmpbc[:, 512:1024]
        st3 = al_alias.f32([128, 96, 1])
        btmpb, btmpc, bst3 = Buf("tmpb"), Buf("tmpc"), Buf("st3")
        bHs = [Buf("H0"), Buf("H1")]

        def hT_reads(tb):
            return [hTb[kc][4 * tb + i] for kc in range(8) for i in range(4)]

        tblocks = [(0, 256, (lambda kc: hcT[:, kc, 0:256]), list(hcTb))]
        for tb in range(4):
            tblocks.append((256 + tb * 512, 512, (lambda kc, tb=tb: hT[:, kc, tb * 512:(tb + 1) * 512]), hT_reads(tb)))

        win_v = win_d.rearrange("(k p) n -> p k n", p=128)
        bwst = Buf("wlr_st")
        p.dma(wlr_st, win_v[:, :, 2048:2432], writes=[bwst])
        p.copy("pool", wlr, wlr_st, reads=[bwst], writes=[bwlr])
        for (dstw, srcw) in ((w2b, w2_d), (a2b, a2_d), (g2b, g2_d)):
            p.dma(wlr_st[:, 0, 0:512] if False else wlr_st.rearrange("p k n -> p (k n)")[:, 0:512], srcw[:, :], reads=[bwst], writes=[bwst])
            p.copy("pool", dstw, wlr_st.rearrange("p k n -> p (k n)")[:, 0:512], reads=[bwst], writes=[bw2])
        for (co, n, rhs_fn, rds) in tblocks:
            for jj in range(3):
                if jj == 2 and co == 0:
                    continue
                ps, pb = bank()
                for kc in range(8):
                    p.mm(ps[:, 0:n], wlr[:, kc, jj * 128:(jj + 1) * 128], rhs_fn(kc), start=(kc == 0), stop=(kc == 7),
                         reads=[bwlr] + rds, writes=[pb])
                if jj == 0:
                    p.act(twd[:, co:co + n], ps[:, 0:n], AF.Tanh, reads=[pb], writes=[btwd])
                elif jj == 1:
                    p.copy("act", adT[:, co:co + n], ps[:, 0:n], reads=[pb], writes=[badT])
                else:
                    p.act(sgd[:, co - 256:co - 256 + n], ps[:, 0:n], AF.Sigmoid, reads=[pb], writes=[bsgd])
        dump("twd", twd, [btwd])
        barrier()

        for pr in range(4 if stop_after > 3 else int(os.environ.get('NPAIRS', '1'))):
            for xi in range(3):
                sn = ("a", "a", "sig")[xi]
                so = (0, 1024, 0)[xi]
                stg = arr[sn][:, so:so + 1024].rearrange("p (k n) -> p k n", k=8)
                p.dma(stg, win_v[:, :, 512 + xi * 512 + pr * 128:512 + xi * 512 + (pr + 1) * 128],
                      writes=[barr[sn]])
                p.copy(("dve", "act", "pool")[xi], wrkv[:, :, xi, :], stg, reads=[barr[sn]], writes=[bwrkv, barr["kk"]])
            for xi, nm in enumerate(("r", "k", "v")):
                rawn = "sig" if xi != 1 else "a"
                raw = arr[rawn]
                braw = barr[rawn]
                p.memset("pool", raw[:, 0:1], 0.0, writes=[braw])
                p.memset("pool", raw[:, 257:259], 0.0, writes=[braw])
                p.memset("pool", raw[:, 2307:2308], 0.0, writes=[braw])
                for (co, n, rhs_fn, rds) in tblocks:
                    ps, pb = bank()
                    for kc in range(8):
                        p.mm(ps[:, 0:n], wrkv[:, kc, xi, :], rhs_fn(kc), start=(kc == 0), stop=(kc == 7),
                             reads=[bwrkv] + rds, writes=[pb])
                    ro = 1 if co == 0 else 259 + (co - 256)
                    p.copy("act", raw[:, ro:ro + n], ps[:, 0:n], reads=[pb], writes=[braw])
                cw = pc1536[:, xi * 4 + pr, :]
                o = arr[nm]
                bo = barr[nm]
                for (oo, n, rb) in ((0, 256, 1), (256, 2048, 259)):
                    p.ts("pool", o[:, oo:oo + n], raw[:, rb:rb + n], cw[:, 1:2], ALU.mult, reads=[braw, bprm], writes=[bo])
                    p.stt(o[:, oo:oo + n], raw[:, rb - 1:rb - 1 + n], cw[:, 0:1], o[:, oo:oo + n], ALU.mult, ALU.add,
                          reads=[braw, bprm, bo], writes=[bo])
                    p.stt(o[:, oo:oo + n], raw[:, rb + 1:rb + 1 + n], cw[:, 2:3], o[:, oo:oo + n], ALU.mult, ALU.add,
                          reads=[braw, bprm, bo], writes=[bo])
            p.ts("pool", arr["kk"][:, 0:2304], arr["k"][:, 0:2304], pc512[:, pr, 4:5], ALU.mult,
                 reads=[barr["k"], bprm], writes=[barr["kk"], bwrkv])
            p.act(arr["a"][:, 0:2304], arr["kk"][:, 0:2304], AF.Square, reads=[barr["kk"]], writes=[barr["a"]])
            for (co, n, _, _) in tblocks:
                ps, pb = bank()
                p.mm(ps[:, 0:n], blkones, arr["a"][:, co:co + n], reads=[bcst, barr["a"]], writes=[pb])
                p.act(arr["a"][:, co:co + n], ps[:, 0:n], AF.Ln, bias=KK_EPS, reads=[pb], writes=[barr["a"]])
            p.act(arr["a"][:, 0:2304], arr["a"][:, 0:2304], AF.Exp, scale=-0.5, reads=[barr["a"]], writes=[barr["a"]])
            p.tt("dve", arr["kk"][:, 0:2304], arr["kk"][:, 0:2304], arr["a"][:, 0:2304], ALU.mult,
                 reads=[barr["kk"], barr["a"]], writes=[barr["kk"]])
            if pr == 0:
                dump("r", arr["r"][:, 0:2304], [barr["r"]])
                dump("kk", arr["kk"][:, 0:2304], [barr["kk"]])
                dump("v", arr["v"][:, 0:2304], [barr["v"]])

            for d in range(2):
                hs_ = slice(64 * d, 64 * d + 64)
                for (co, n, _, _) in tblocks:
                    ps, pb = bank()
                    p.mm(ps[:, 0:n], w2b[hs_, pr * 128:(pr + 1) * 128], twd[hs_, co:co + n], reads=[bw2, btwd], writes=[pb])
                    p.act(arr["sig"][:, co:co + n], ps[:, 0:n], AF.Sigmoid, bias=pc512[:, pr, d:d + 1],
                          reads=[pb, bprm], writes=[barr["sig"]])
                    ps, pb = bank()
                    p.mm(ps[:, 0:n], a2b[hs_, pr * 128:(pr + 1) * 128], adT[hs_, co:co + n], reads=[bw2, badT], writes=[pb])
                    p.act(arr["a"][:, co:co + n], ps[:, 0:n], AF.Sigmoid, bias=pc512[:, pr, 2 + d:3 + d],
                          reads=[pb, bprm], writes=[barr["a"]])
                for tb in range(4):
                    co = 256 + tb * 512
                    tq, btq = (tmpb, btmpb) if tb % 2 == 0 else (tmpc, btmpc)
                    p.ts("pool", tq, arr["a"][:, co:co + 512], pc512[:, pr, 5:6], ALU.mult, omka[:, pr:pr + 1], ALU.add,
                         reads=[barr["a"], bprm, bAB], writes=[btq])
                    p.tt("pool", tq, tq, arr["k"][:, co:co + 512], ALU.mult, reads=[btq, barr["k"]], writes=[btq])
                    p.stt(tq, tq, pc512[:, pr, 6:7], arr["r"][:, co:co + 512], ALU.mult, ALU.mult,
                          reads=[btq, bprm, barr["r"]], writes=[btq])
                    ps, pb = bank()
                    p.mm(ps[:, :], blkones, tq, reads=[bcst, btq], writes=[pb])
                    if d == 0:
                        p.copy("act", bsum[:, tb * 512:(tb + 1) * 512], ps[:, :], reads=[pb], writes=[bbs])
                    else:
                        p.tt("dve", bsum[:, tb * 512:(tb + 1) * 512], ps[:, :], bsum[:, tb * 512:(tb + 1) * 512], ALU.add,
                             reads=[pb, bbs], writes=[bbs])
                if pr == 0 and d == 0:
                    dump("sig0", arr["sig"][:, 0:2304], [barr["sig"]])
                    dump("a0", arr["a"][:, 0:2304], [barr["a"]])

                order = list(range(18)) if d == 0 else [1, 0] + list(range(17, 1, -1))
                sgn = -LAM if d == 0 else LAM
                sig_, a_, k_, kk_, r_, v_ = arr["sig"], arr["a"], arr["k"], arr["kk"], arr["r"], arr["v"]

                def prepA(j):
                    W = wview(j)
                    co = 128 * order[j]
                    cs = slice(co, co + 128)
                    p.op("dve", lambda e: e.tensor_tensor_scan(out=W.cum, data0=sig_[:, cs], data1=sig_[:, cs], initial=0.0,
                                                               op0=ALU.add, op1=ALU.bypass), [barr["sig"]], [W.bcum])
                    p.ts("dve", W.Dm, W.cum, W.cum[:, 63:64], ALU.subtract, reads=[W.bcum], writes=[W.bDm])
                    p.tt("dve", W.Em, W.Dm, sig_[:, cs], ALU.subtract, reads=[W.bDm, barr["sig"]], writes=[W.bEm])
                    if d == 0:
                        p.act(W.eI, W.Dm, AF.Exp, scale=-LAM, reads=[W.bDm], writes=[W.beI])
                        p.act(W.eIn, W.Dm, AF.Exp, scale=LAM, reads=[W.bDm], writes=[W.beIn])
                        p.act(W.eE, W.Em, AF.Exp, scale=-LAM, reads=[W.bEm], writes=[W.beE])
                    else:
                        p.act(W.eI, W.Em, AF.Exp, scale=LAM, reads=[W.bEm], writes=[W.beI])
                        p.act(W.eIn, W.Em, AF.Exp, scale=-LAM, reads=[W.bEm], writes=[W.beIn])
                        p.act(W.eE, W.Dm, AF.Exp, scale=LAM, reads=[W.bDm], writes=[W.beE])
                    p.ts("pool", W.kdc, a_[:, cs], pc512[:, pr, 5:6], ALU.mult, omka[:, pr:pr + 1], ALU.add,
                         reads=[barr["a"], bprm, bAB], writes=[W.bkdc])
                    p.tt("pool", W.kdc, W.kdc, k_[:, cs], ALU.mult, reads=[W.bkdc, barr["k"]], writes=[W.bkdc])
                    p.tt("pool", W.bcc, kk_[:, cs], a_[:, cs], ALU.mult, reads=[barr["kk"], barr["a"]], writes=[W.bbcc])
                    p.tt("dve", W.QRf, kk_[:, cs], W.eE, ALU.mult, reads=[barr["kk"], W.beE], writes=[W.bQR])
                    p.tt("dve", W.QR[:, 0:128], kk_[:, cs], W.eE, ALU.mult, reads=[barr["kk"], W.beE], writes=[W.bQR])
                    p.tt("dve", W.QR[:, 128:256], r_[:, cs], W.eI, ALU.mult, reads=[barr["r"], W.beI, W.bQR], writes=[W.bQR])
                    for h in range(2):
                        hs = slice(64 * h, 64 * h + 64)
                        p.tt("pool", W.KmH[h][hs, :], W.kdc[hs, :], W.eIn[hs, :], ALU.mult, reads=[W.bkdc, W.beIn], writes=[W.bKm])
                        p.tt("pool", W.BmH[h][hs, :], W.bcc[hs, :], W.eIn[hs, :], ALU.mult, reads=[W.bbcc, W.beIn], writes=[W.bBm])

                def prepB(j):
                    W = wview(j)
                    N = wview(j + 1)
                    if d == 0:
                        p.tt("dve", W.cols[:, 0:1], W.Dm[:, 127:128], N.Em[:, 0:1], ALU.subtract, reads=[W.bDm, N.bEm], writes=[W.bcols])
                    else:
                        p.tt("dve", W.cols[:, 0:1], W.Em[:, 0:1], N.Dm[:, 127:128], ALU.subtract, reads=[W.bEm, N.bDm], writes=[W.bcols])
                    p.act(W.cols[:, 1:2], W.cols[:, 0:1], AF.Exp, scale=sgn, reads=[W.bcols], writes=[W.bcols])
                    gs = W.cols[:, 1:2]
                    p.stt(W.Kmg, W.kdc, gs, W.eIn, ALU.mult, ALU.mult, reads=[W.bkdc, W.beIn, W.bcols], writes=[W.bKmg])
                    p.stt(W.Bmg, W.bcc, gs, W.eIn, ALU.mult, ALU.mult, reads=[W.bbcc, W.beIn, W.bcols], writes=[W.bBmg])
                    p.ts("pool", W.diagG, idfold, gs, ALU.mult, reads=[bcst, W.bcols], writes=[W.bdiag])

                def front(j):
                    S_ = SLOT[j % GRP]
                    tok, Xb, gram, P2, GP, small = S_["tok"], S_["Xb"], S_["gram"], S_["P2"], S_["GP"], S_["small"]
                    btok, bX, bgram, bP2, bGP, bsmall = S_["btok"], S_["bX"], S_["bgram"], S_["bP2"], S_["bGP"], S_["bsmall"]
                    W = wview(j)
                    c = order[j]
                    co = 128 * c
                    cs = slice(co, co + 128)
                    last = (j == 17)
                    latent = (c >= 2)
                    nxt = 1 - cur
                    pt, pbt = bank()
                    p.tr(pt[:, 0:128], v_[:, cs], ident, reads=[barr["v"], bcst], writes=[pbt])
                    if not last:
                        p.tr(pt[:, 128:256], W.Kmg, ident, reads=[W.bKmg, bcst], writes=[pbt])
                        p.tr(pt[:, 256:384], W.Bmg, ident, reads=[W.bBmg, bcst], writes=[pbt])
                    p.tr(pt[:, 384:512], W.QRf, ident, reads=[W.bQR, bcst], writes=[pbt])
                    n = 128 if last else 384
                    p.copy("act", tok[:, 0:n], pt[:, 0:n], reads=[pbt], writes=[btok])
                    p.act(Xb[:, :, 0:64], pt[:, 384:512].rearrange("p (h k) -> p h k", h=2), AF.Copy, scale=-1.0, reads=[pbt], writes=[bX])
                    yield
                    pg = [bank(), bank()]
                    pp, pbp = bank()
                    for h in range(2):
                        hs = slice(64 * h, 64 * h + 64)
                        p.mm(pg[h][0][:, 0:256], W.BmH[h], W.QR, reads=[W.bBm, W.bQR], writes=[pg[h][1]])
                        p.mm(pg[h][0][:, 256:512], W.KmH[h], W.QR, reads=[W.bKm, W.bQR], writes=[pg[h][1]])
                        p.mm(pp[:, 128 * h:128 * h + 128], W.QR[:, 0:128], W.BmH[h], reads=[W.bBm, W.bQR], writes=[pbp])
                    for h in range(2):
                        p.tt("dve", gram[h], pg[h][0][:, :], mask4[d], ALU.mult, reads=[pg[h][1], bcst], writes=[bgram[h]])
                    p.tt("dve", P2, pp[:, 0:256], mP2[d], ALU.mult, reads=[pbp, bcst], writes=[bP2])
                    yield
                    pv, pbv = bank()
                    for h in range(2):
                        p.mm(pv[:, 64 * h:64 * h + 64], gram[h][:, 256:384], tok[:, 64 * h:64 * h + 64],
                             reads=[bgram[h], btok], writes=[pbv])
                    p.copy("act", Xb[:, :, 64:128], pv[:, 0:128].rearrange("p (h k) -> p h k", h=2), reads=[pbv], writes=[bX])
                    yield
                    Gk = [gram[h][:, 0:128] for h in range(2)]
                    Pk = [P2[:, 128 * h:128 * h + 128] for h in range(2)]
                    bG = [bgram[0], bgram[1]]
                    bP = [bP2, bP2]
                    Xf = Xb.rearrange("p h k -> p (h k)")
                    for k in range(7):
                        px, pbx = bank()
                        for h in range(2):
                            p.mm(px[:, 128 * h:128 * h + 128], Gk[h], Xb[:, h, :], reads=[bG[h], bX], writes=[pbx])
                        if k < 6:
                            pq, pbq = bank()
                            for h in range(2):
                                p.mm(pq[:, 256 * h:256 * h + 128], Pk[h], Gk[h], reads=[bP[h], bG[h]], writes=[pbq])
                                if k < 5:
                                    p.mm(pq[:, 256 * h + 128:256 * h + 256], Gk[h], Pk[h], reads=[bP[h], bG[h]], writes=[pbq])
                        p.tt("dve", Xf, px[:, 0:256], Xf, ALU.add, reads=[pbx, bX], writes=[bX])
                        if k < 6:
                            dst = GP[(k + 1) % 2]
                            bd = bGP[(k + 1) % 2]
                            if k < 5:
                                p.copy("act", dst[:, :], pq[:, :], reads=[pbq], writes=[bd])
                            else:
                                p.copy("act", dst.rearrange("p (h x) -> p h x", h=2)[:, :, 0:128],
                                       pq.rearrange("p (h x) -> p h x", h=2)[:, :, 0:128], reads=[pbq], writes=[bd])
                            Gk = [dst[:, 256 * h:256 * h + 128] for h in range(2)]
                            Pk = [dst[:, 256 * h + 128:256 * h + 256] for h in range(2)]
                            bG = [bd, bd]
                            bP = [bd, bd]
                        yield
                    if (not last) or latent:
                        psm, pbsm = bank()
                        lo, hi = (0 if not last else 128), (384 if latent else 128)
                        for h in range(2):
                            if not last:
                                p.mm(psm[0:64, 64 * h:64 * h + 64], selb[:, h, :], W.diagG, start=True, stop=False,
                                     reads=[bcst, W.bdiag], writes=[pbsm])
                                p.mm(psm[0:64, 64 * h:64 * h + 64], Xb[:, h, 0:64], tok[:, 256 + 64 * h:256 + 64 * h + 64],
                                     start=False, stop=True, reads=[bX, btok], writes=[pbsm])
                            if latent:
                                p.mm(psm[0:64, 128 + 128 * h:256 + 128 * h], selb[:, h, :], W.QR[:, 128:256], start=True, stop=False,
                                     reads=[bcst, W.bQR], writes=[pbsm])
                                p.mm(psm[0:64, 128 + 128 * h:256 + 128 * h], Xb[:, h, 0:64], gram[h][:, 128:256],
                                     start=False, stop=True, reads=[bX, bgram[h]], writes=[pbsm])
                        p.copy("act", small[0:64, lo:hi], psm[0:64, lo:hi], reads=[pbsm], writes=[bsmall])

                def back(j, cur):
                    S_ = SLOT[j % GRP]
                    tok, Xb, gram, small = S_["tok"], S_["Xb"], S_["gram"], S_["small"]
                    btok, bX, bgram, bsmall = S_["btok"], S_["bX"], S_["bgram"], S_["bsmall"]
                    c = order[j]
                    last = (j == 17)
                    latent = (c >= 2)
                    nxt = 1 - cur
                    if latent:
                        py, pby = bank()
                        for h in range(2):
                            p.mm(py[:, 64 * h:64 * h + 64], gram[h][:, 384:512], tok[:, 64 * h:64 * h + 64], start=True, stop=False,
                                 reads=[bgram[h], btok], writes=[pby])
                            p.mm(py[:, 64 * h:64 * h + 64], gram[h][:, 128:256], Xb[:, h, 64:128], start=False, stop=False,
                                 reads=[bgram[h], bX], writes=[pby])
                            p.mm(py[:, 64 * h:64 * h + 64], small[:, 128 + 128 * h:256 + 128 * h], Hs[cur][:, h, :],
                                 start=False, stop=True, reads=[bsmall, bHs[cur]], writes=[pby])
                        ti = c - 2
                        if d == 0:
                            p.copy("act", Ysum[:, ti, :], py[:, 0:128], reads=[pby], writes=[bY[ti]])
                        else:
                            p.tt("dve", Ysum[:, ti, :], py[:, 0:128], Ysum[:, ti, :], ALU.add, reads=[pby, bY[ti]], writes=[bY[ti]])
                    if not last:
                        ph, pbh = bank()
                        for h in range(2):
                            p.mm(ph[0:64, 64 * h:64 * h + 64], small[:, 64 * h:64 * h + 64], Hs[cur][:, h, :],
                                 start=True, stop=False, reads=[bsmall, bHs[cur]], writes=[pbh])
                            p.mm(ph[0:64, 64 * h:64 * h + 64], tok[:, 128 + 64 * h:128 + 64 * h + 64], tok[:, 64 * h:64 * h + 64],
                                 start=False, stop=False, reads=[btok], writes=[pbh])
                            p.mm(ph[0:64, 64 * h:64 * h + 64], tok[:, 256 + 64 * h:256 + 64 * h + 64], Xb[:, h, 64:128],
                                 start=False, stop=True, reads=[btok, bX], writes=[pbh])
                        p.copy("act", Hs[nxt][0:64].rearrange("p h k -> p (h k)"), ph[0:64, 0:128], reads=[pbh], writes=[bHs[nxt]])

                barrier()
                if d == 1:
                    p.tt("pool", bsum, bsum, arr["v"][:, 256:2304], ALU.mult, reads=[bbs, barr["v"]], writes=[bbs])
                p.memset("pool", Hs[0].rearrange("p h k -> p (h k)"), 0.0, writes=[bHs[0]])
                p.memset("pool", Hs[1].rearrange("p h k -> p (h k)"), 0.0, writes=[bHs[1]])
                for S_ in SLOT:
                    p.memset("pool", S_["small"], 0.0, writes=[S_["bsmall"]])
                cur = 0
                nsteps = 18 if stop_after > 3 else int(os.environ.get("NSTEPS", "18"))
                done_prep = set()

                def ensure_prep(jj):
                    if jj < 18 and jj not in done_prep:
                        prepA(jj)
                        done_prep.add(jj)
                        if jj >= 1:
                            prepB(jj - 1)
                j = 0
                while j < nsteps:
                    grp = list(range(j, min(j + GRP, nsteps)))
                    for jj in grp:
                        ensure_prep(jj)
                        ensure_prep(jj + 1)
                    gens = [front(jj) for jj in grp]
                    while gens:
                        for g_ in list(gens):
                            try:
                                next(g_)
                            except StopIteration:
                                gens.remove(g_)
                    for jj in grp:
                        back(jj, cur)
                        cur = 1 - cur
                        ensure_prep(jj + GRP + 1)
                    j += len(grp)
                barrier()
                if pr == 0:
                    dump("H_d%d" % d, Hs[cur][0:64].rearrange("p h k -> p (h k)"), [bHs[cur]])
                    dump("Y_d%d" % d, Ysum.rearrange("p i c -> p (i c)"), bY)

            Yv = Ysum.rearrange("p i (h n) -> p (i h) n", h=2)
            Yf = Ysum.rearrange("p i c -> p (i c)")
            sq = arr["sig"][:, 0:2048]
            p.op("dve", lambda e: e.tensor_reduce(out=st3[:, 0:32, 0], in_=Yv, axis=AX.X, op=ALU.add), bY, [bst3])
            p.act(sq, Yf, AF.Square, reads=bY, writes=[barr["sig"]])
            p.op("dve", lambda e: e.tensor_reduce(out=st3[:, 32:64, 0], in_=sq.rearrange("p (g n) -> p g n", n=64), axis=AX.X, op=ALU.add),
                 [barr["sig"], bst3], [bst3])
            p.ts("dve", st3[:, 64:96, 0], st3[:, 0:32, 0], 1.0 / 64.0, ALU.mult, reads=[bst3], writes=[bst3])
            p.tt("dve", st3[:, 0:32, 0], st3[:, 64:96, 0], st3[:, 64:96, 0], ALU.mult, reads=[bst3], writes=[bst3])
            p.stt(st3[:, 32:64, 0], st3[:, 32:64, 0], 1.0 / 64.0, st3[:, 0:32, 0], ALU.mult, ALU.subtract, reads=[bst3], writes=[bst3])
            p.act(st3[:, 32:64, 0], st3[:, 32:64, 0], AF.Sqrt, bias=GN_EPS, reads=[bst3], writes=[bst3])
            p.op("dve", lambda e: e.reciprocal(out=st3[:, 32:64, 0], in_=st3[:, 32:64, 0]), [bst3], [bst3])
            for tb in range(4):
                Yvb = Yv[:, tb * 8:(tb + 1) * 8, :]
                bYb = bY[tb * 4:(tb + 1) * 4]
                eng_n = "pool" if tb % 2 == 0 else "dve"
                p.tt(eng_n, Yvb, Yvb, st3[:, 64 + tb * 8:64 + (tb + 1) * 8, :].broadcast_to([128, 8, 64]), ALU.subtract, reads=bYb + [bst3], writes=bYb)
                p.tt(eng_n, Yvb, Yvb, st3[:, 32 + tb * 8:32 + (tb + 1) * 8, :].broadcast_to([128, 8, 64]), ALU.mult, reads=bYb + [bst3], writes=bYb)
                ps, pb = bank()
                for i in range(4):
                    p.tr(ps[:, i * 128:(i + 1) * 128], Ysum[:, tb * 4 + i, :], ident, reads=[bY[tb * 4 + i], bcst], writes=[pb])
                p.act(tmpb, ps[:, :], AF.Identity, bias=pc512[:, pr, 8:9], scale=pc512[:, pr, 7:8], reads=[pb, bprm], writes=[btmpb])
                p.tt("pool", tmpb, tmpb, bsum[:, tb * 512:(tb + 1) * 512], ALU.add, reads=[btmpb, bbs], writes=[btmpb])
                pg_, pbg_ = bank()
                p.mm(pg_[:, :], g2b[:, pr * 128:(pr + 1) * 128], sgd[:, tb * 512:(tb + 1) * 512], reads=[bw2, bsgd], writes=[pbg_])
                p.tt("dve", rwT[:, pr, tb * 512:(tb + 1) * 512], tmpb, pg_[:, :], ALU.mult, reads=[btmpb, pbg_], writes=[rwTb[pr]])
        dump("rwT", rwT[:, 0, :], rwTb)
        barrier()
        if stop_after <= 3:
            return _finish(nc, p, out_d, dbg_outs)
        al = Alloc(arena, 52, 180)
        YnT = al.bf16([128, 4, 2048])
        bYn = [Buf("Yn%d" % i) for i in range(4)]
        UfT = al.bf16([128, 4, 2048])
        bUf = [[Buf() for _ in range(4)] for _ in range(4)]
        Ucs = al.bf16([128, 16, 2, 512])
        bUcs = [Buf() for _ in range(16)]
        dft_lo = al.off
        dftb = al.bf16([128, 2, 16, 512])
        al2 = Alloc(arena, 0, 180)
        al2.off = dft_lo
        wf_st = al2.f32([128, 8, 512])
        bdft = Buf("dft")
        Yf = al.f32([128, 4, 512])
        Ysq = al.f32([128, 4, 512])
        rst = al.f32([128, 512])
        wf = al.bf16([128, 8, 512])
        bYf, bYsq, brst, bwf = Buf("Yf"), Buf("Ysq"), Buf("rst"), Buf("wf")
        p.dma(wf_st, win_v[:, :, 0:512], writes=[bdft])
        p.copy("pool", wf, wf_st, reads=[bdft], writes=[bwf])
        for cc in range(4):
            for tb in range(4):
                ps, pb = bank()
                for kc in range(8):
                    p.mm(ps[:, :], wf[:, kc, cc * 128:(cc + 1) * 128], hT[:, kc, tb * 512:(tb + 1) * 512],
                         start=(kc == 0), stop=(kc == 7), reads=[bwf] + hT_reads(tb), writes=[pb])
                p.copy("act", UfT[:, cc, tb * 512:(tb + 1) * 512], ps[:, :], reads=[pb], writes=[bUf[cc][tb]])
        for i in range(16):
            pc_, pbc_ = bank()
            ps_, pbs_ = bank()
            for cc in range(4):
                p.mm(pc_[:, cc * 128:(cc + 1) * 128], UfT[:, cc, i * 128:(i + 1) * 128], cgs[:, 0:128],
                     reads=[bUf[cc][i // 4], bcgs], writes=[pbc_])
                p.mm(ps_[:, cc * 128:(cc + 1) * 128], UfT[:, cc, i * 128:(i + 1) * 128], cgs[:, 128:256],
                     reads=[bUf[cc][i // 4], bcgs], writes=[pbs_])
            p.copy("act", Ucs[:, i, 0, :], pc_[:, :], reads=[pbc_], writes=[bUcs[i]])
            p.copy("dve", Ucs[:, i, 1, :], ps_[:, :], reads=[pbs_], writes=[bUcs[i]])
        dft_v = dft_d.rearrange("m (i p) t -> m p i t", p=128)
        bdft2 = [bdft, Buf("dft1")]
        for jb in range(4):
            accs = [bank() for _ in range(4)]
            for m in range(2):
                p.dma(dftb[:, m, :, :], dft_v[m][:, :, jb * 512:(jb + 1) * 512], reads=[bdft2[m]], writes=[bdft2[m]])
                for cc in range(4):
                    ps, pb = accs[cc]
                    for i in range(16):
                        p.mm(ps[:, :], Ucs[:, i, m, cc * 128:(cc + 1) * 128], dftb[:, m, i, :],
                             start=(m == 0 and i == 0), stop=(m == 1 and i == 15), reads=[bUcs[i], bdft2[m]], writes=[pb])
            for cc in range(4):
                ps, pb = accs[cc]
                p.copy("act", Yf[:, cc, :], ps[:, :], reads=[pb], writes=[bYf])
                p.act(Ysq[:, cc, :], ps[:, :], AF.Square, reads=[pb], writes=[bYsq])
            pss, pbss = bank()
            for cc in range(4):
                p.mm(pss[:, :], ones, Ysq[:, cc, :], start=(cc == 0), stop=(cc == 3), reads=[bcst, bYsq], writes=[pbss])
            p.act(rst, pss[:, :], AF.Ln, bias=NORM_EPS, scale=1.0 / 512.0, reads=[pbss], writes=[brst])
            p.act(rst, rst, AF.Exp, scale=-0.5, reads=[brst], writes=[brst])
            for cc in range(4):
                p.stt(YnT[:, cc, jb * 512:(jb + 1) * 512], Yf[:, cc, :], pc512[:, cc, 9:10], rst, ALU.mult, ALU.mult,
                      reads=[bYf, bprm, brst], writes=[bYn[jb]])
        dump("YnT", YnT.rearrange("p k t -> p (k t)"), bYn)
        barrier()
        if stop_after <= 4:
            return _finish(nc, p, out_d, dbg_outs)

        al = Alloc(arena, 68, 180)
        x1 = al.f32([128, 16, 1024])
        bx1 = [Buf("x1_%d" % i) for i in range(16)]
        wo = al.bf16([128, 8, 1024])
        st_lo = al.off
        wo_st = al.f32([128, 8, 512])
        xt2 = [al.f32([128, 1024]) for _ in range(2)]
        tmpy = [al.f32([128, 512]) for _ in range(2)]
        bwo, bwost = Buf("wo"), Buf("wost")
        bxt2 = [Buf(), Buf()]
        btmpy = [Buf(), Buf()]
        wout_v = wout_d.rearrange("(k p) n -> p k n", p=128)
        for nh in range(2):
            p.dma(wo_st, wout_v[:, :, nh * 512:(nh + 1) * 512], reads=[bwost], writes=[bwost])
            p.copy("dve" if nh == 0 else "act", wo[:, :, nh * 512:(nh + 1) * 512], wo_st, reads=[bwost], writes=[bwo])
        for i in range(16):
            p.dma(xt2[i % 2], xv[i], writes=[bxt2[i % 2]])
            for nh in range(2):
                ps, pb = bank()
                for kc in range(8):
                    if kc < 4:
                        lhs = YnT[:, kc, i * 128:(i + 1) * 128]
                        rd = [bYn[i // 4]]
                    else:
                        lhs = rwT[:, kc - 4, i * 128:(i + 1) * 128]
                        rd = [rwTb[kc - 4]]
                    p.mm(ps[:, :], lhs, wo[:, kc, nh * 512:(nh + 1) * 512], start=(kc == 0), stop=(kc == 7),
                         reads=rd + [bwo], writes=[pb])
                k2 = nh
                p.tt("dve", tmpy[k2], ps[:, :], bc[:, 0, nh * 512:(nh + 1) * 512], ALU.mult, reads=[pb, bbc], writes=[btmpy[k2]])
                p.tt("dve", x1[:, i, nh * 512:(nh + 1) * 512], tmpy[k2], xt2[i % 2][:, nh * 512:(nh + 1) * 512], ALU.add,
                     reads=[btmpy[k2], bxt2[i % 2]], writes=[bx1[i]])
        dump("x1", x1.rearrange("p i d -> p (i d)"), bx1)
        barrier()
        al = Alloc(arena, 0, 180)
        al.off = st_lo
        norm_to_T(al, lambda i: x1[:, i, :], 16, AB[:, 4, :], AB[:, 5, :], hT, lambda kc, i: hTb[kc][i],
                  src_is_sbuf=True, src_bufs=bx1)
        dump("h2T", hT.rearrange("p k t -> p (k t)"), [b for r in hTb for b in r])
        barrier()
        if stop_after <= 5:
            return _finish(nc, p, out_d, dbg_outs)

        alA = Alloc(arena, 32, 68)
        alB = Alloc(arena, 132, 180)
        actT = alA.bf16([128, 22, 512])
        bact = [Buf("act%d" % i) for i in range(22)]
        wdnb = [alA.bf16([128, 1024]) for _ in range(2)]
        wdn_st = [alA.f32([128, 1024]) for _ in range(2)]
        bwdn = [Buf(), Buf()]
        bwdst = [Buf(), Buf()]
        wup = [alB.bf16([128, 8, 2, 128]) for _ in range(2)]
        wup_st = alB.f32([128, 8, 2, 128])
        bwup = [Buf(), Buf()]
        bwupst = Buf()
        Upad = [[alB.bf16([128, 10, 66]) for _ in range(2)] for _ in range(2)]
        bUp = [[Buf(), Buf()], [Buf(), Buf()]]
        dg = [alB.bf16([128, 2, 9, 128]) for _ in range(2)]
        bdg = [Buf(), Buf()]
        gsb = [alB.f32([128, 512]) for _ in range(2)]
        bgsb = [Buf(), Buf()]
        x2t = [alB.f32([128, 1024]) for _ in range(2)]
        bx2 = [Buf(), Buf()]
        tmpf = [alB.f32([128, 512]) for _ in range(2)]
        btmpf = [Buf(), Buf()]
        junkf = alA.bf16([128, 1024])
        ssf = alB.f32([128, 8])
        identb = alB.bf16([128, 128])
        bjf, bssf, bidb = Buf(), Buf(), Buf()
        p.copy("pool", identb, ident, reads=[bcst], writes=[bidb])
        wdn_v = wdn_d.rearrange("(k p) n -> k p n", p=128)
        out_v = out_d.rearrange("(i p) d -> i p d", p=128)
        out_dmas = []
        def load_pair(q, i):
            s = i % 2
            p.dma(wup_st.rearrange("p k g n -> p (k g n)"), wup_d[i], reads=[bwupst], writes=[bwupst])
            p.copy("act", wup[s].rearrange("p k g n -> p (k g n)"), wup_st.rearrange("p k g n -> p (k g n)"),
                   reads=[bwupst], writes=[bwup[s]])
            for gv in range(2):
                ch = gv * 22 + i
                for tap in range(9):
                    p.ts("dve", dg[s][:, gv, tap, :], identb, pc5632[:, ch, tap:tap + 1], ALU.mult,
                         reads=[bidb, bprm], writes=[bdg[s]])

        def compute_pair(q, i):
            s = i % 2
            t0 = max(0, 512 * q - 64)
            t1 = min(2048, 512 * q + 576)
            for gv in range(2):
                ta = t0
                while ta < t1:
                    tb_ = min(ta + 512, t1)
                    n = tb_ - ta
                    ps, pb = bank()
                    for kc in range(8):
                        p.mm(ps[:, 0:n], wup[s][:, kc, gv, :], hT[:, kc, ta:tb_], start=(kc == 0), stop=(kc == 7),
                             reads=[bwup[s]] + [hTb[kc][ii] for ii in range(ta // 128, (tb_ + 127) // 128)], writes=[pb])
                    r0 = ta // 64 - (8 * q - 1)
                    nr = n // 64
                    p.copy("act", Upad[s][gv][:, r0:r0 + nr, 1:65], ps[:, 0:n].rearrange("p (r c) -> p r c", c=64),
                           reads=[pb], writes=[bUp[s][gv]])
                    ta = tb_
            pcv = []
            for gv in range(2):
                ps, pb = bank()
                for tap in range(9):
                    dr, dc = tap // 3 - 1, tap % 3 - 1
                    p.mm(ps[:, :], dg[s][:, gv, tap, :], Upad[s][gv][:, 1 + dr:9 + dr, 1 + dc:65 + dc],
                         start=(tap == 0), stop=(tap == 8), reads=[bdg[s], bUp[s][gv]], writes=[pb])
                pcv.append((ps, pb))
            p.act(gsb[s], pcv[0][0][:, :], AF.Silu, bias=pc5632[:, i, 9:10], reads=[pcv[0][1], bprm], writes=[bgsb[s]])
            p.stt(actT[:, i, :], pcv[1][0][:, :], pc5632[:, 22 + i, 9:10], gsb[s], ALU.add, ALU.mult,
                  reads=[pcv[1][1], bprm, bgsb[s]], writes=[bact[i]])

        load_pair(0, 0)
        for q in range(4):
            for s in range(2):
                for gv in range(2):
                    if q == 0:
                        p.memset("pool", Upad[s][gv].rearrange("p r c -> p (r c)"), 0.0, writes=[bUp[s][gv]])
                    elif q == 3:
                        p.memset("pool", Upad[s][gv][:, 9, :], 0.0, writes=[bUp[s][gv]])
            for i in range(22):
                if i + 1 < 22:
                    load_pair(q, i + 1)
                compute_pair(q, i)
            if q == 0:
                dump("actT", actT.rearrange("p k t -> p (k t)"), bact)
            for kc in range(22):
                s = kc % 2
                p.dma(wdn_st[s], wdn_v[kc], writes=[bwdst[s]], eng="act")
                p.copy("dve" if kc % 2 == 0 else "act", wdnb[s], wdn_st[s], reads=[bwdst[s]], writes=[bwdn[s]])
                for ti in range(4):
                    for nh in range(2):
                        b_ = ti * 2 + nh
                        p.mm(psum[b_][:, :], actT[:, kc, ti * 128:(ti + 1) * 128], wdnb[s][:, nh * 512:(nh + 1) * 512],
                             start=(kc == 0), stop=(kc == 21), reads=[bact[kc], bwdn[s]], writes=[pbuf[b_]])
            if q + 1 < 4:
                load_pair(q + 1, 0)
            for ti in range(4):
                gi = 4 * q + ti
                for nh in range(2):
                    b_ = ti * 2 + nh
                    k2 = (ti * 2 + nh) % 2
                    p.tt("dve", tmpf[k2], psum[b_][:, :], bc[:, 1, nh * 512:(nh + 1) * 512], ALU.mult,
                         reads=[pbuf[b_], bbc], writes=[btmpf[k2]])
                    p.tt("dve", x1[:, gi, nh * 512:(nh + 1) * 512], tmpf[k2], x1[:, gi, nh * 512:(nh + 1) * 512], ALU.add,
                         reads=[btmpf[k2], bx1[gi]], writes=[bx1[gi]])
            p.bank_rr = 0
            for ti in range(4):
                gi = 4 * q + ti
                s = ti % 2
                p.act(junkf, x1[:, gi, :], AF.Square, accum=ssf[:, ti:ti + 1], reads=[bx1[gi]], writes=[bjf, bssf])
                p.act(ssf[:, ti:ti + 1], ssf[:, ti:ti + 1], AF.Sqrt, bias=NORM_EPS, scale=1.0 / 1024.0, reads=[bssf], writes=[bssf])
                p.op("dve", lambda e, ti=ti: e.reciprocal(out=ssf[:, ti:ti + 1], in_=ssf[:, ti:ti + 1]), [bssf], [bssf])
                p.stt(x2t[s], x1[:, gi, :], ssf[:, ti:ti + 1], bc[:, 2, :], ALU.mult, ALU.mult, reads=[bx1[gi], bssf, bbc], writes=[bx2[s]])
                out_dmas.append(p.dma(out_v[gi], x2t[s], reads=[bx2[s]]))
        p.finish(out_dmas + list(dbg_outs.values()))
        p.emit()
    return nc


def _finish(nc, p, out_d, dbg_outs):
    p.finish(list(dbg_outs.values()))
    p.emit()
    return nc


def _const_tables():
    c = np.zeros((128, C_END), np.float32)
    idx = np.arange(128)
    c[:, C_ID:C_ID + 128] = np.eye(128, dtype=np.float32)
    c[:, C_ONE:C_ONE + 128] = 1.0
    blk = (idx[:, None] // 64 == idx[None, :] // 64).astype(np.float32)
    c[:, C_BLK:C_BLK + 128] = blk
    sel = np.zeros((128, 2, 64), np.float32)
    for h in range(2):
        sel[64 * h + np.arange(64), h, np.arange(64)] = 1.0
    c[:, C_SEL:C_SEL + 128] = sel.reshape(128, 128)
    c[:, C_IDF:C_IDF + 64] = (idx[:, None] % 64 == np.arange(64)[None, :]).astype(np.float32)
    s = idx[:, None]
    t = idx[None, :]
    for d in range(2):
        mG = (t > s) if d == 0 else (t < s)
        mL = (t >= s) if d == 0 else (t <= s)
        c[:, C_M4[d]:C_M4[d] + 512] = np.concatenate([-1.0 * mG, mL, -1.0 * mG, mL], 1).astype(np.float32)
        mP = (t < s) if d == 0 else (t > s)
        c[:, C_MP[d]:C_MP[d] + 256] = -np.concatenate([mP, mP], 1).astype(np.float32)
    k = np.arange(64)
    ang = 2.0 * np.pi * np.outer(k, k) / 64.0
    cg = np.zeros((128, 128), np.float64)
    sg = np.zeros((128, 128), np.float64)
    for g in range(2):
        cg[64 * g:64 * g + 64, 64 * g:64 * g + 64] = np.cos(ang) / 8.0
        sg[64 * g:64 * g + 64, 64 * g:64 * g + 64] = -np.sin(ang) / 8.0
    cgs = np.concatenate([cg, sg], 1).astype(ml_dtypes.bfloat16)
    tt = np.arange(2048)
    angT = 2.0 * np.pi * ((np.outer(tt, tt) % 2048).astype(np.float64)) / 2048.0
    dft = np.stack([np.cos(angT), np.sin(angT)], 0) / np.sqrt(2048.0)
    return c, cgs, dft.astype(ml_dtypes.bfloat16)


_CONSTS = None


def make_in_maps(inputs, cores):
    global _CONSTS
    if _CONSTS is None:
        _CONSTS = _const_tables()
    cst, cgs, dft = _CONSTS
    f = lambda a: np.ascontiguousarray(np.asarray(a, dtype=np.float32))
    i = {k: np.asarray(v) for k, v in inputs.items()}
    shared = {
        "ada_w": f(i["ada_w"][0]),
        "ada_b": f(i["ada_b"][0][None, :]),
        "v1024": f(np.stack([i["norm1_g"][0], i["norm2_g"][0]], 0)),
        "v1536": f(i["rwkv_conv_w"][0]),
        "v512": f(np.concatenate([i["decay_w0"][0], i["iclr_a0"][0], i["k_k"][0][None], i["k_a"][0][None],
                                  i["r_k"][0].reshape(1, 512), i["gn_g"][0][None], i["gn_b"][0][None],
                                  i["fourier_g"][0][None]], 0)),
        "v5632": f(np.concatenate([i["ffn_conv_w"][0].reshape(9, 5632), i["ffn_conv_b"][0][None]], 0)),
        "final_g": f(i["final_g"][None, :]),
        "w_in": f(i["w_in"][0]),
        "decay_w2": f(i["decay_w2"][0].reshape(128, 512)),
        "iclr_a2": f(i["iclr_a2"][0].reshape(128, 512)),
        "gate_g2": f(i["gate_g2"][0]),
        "w_out": f(i["w_out"][0]),
        "ffn_w_up": f(i["ffn_w_up"][0].reshape(8, 128, 2, 22, 128).transpose(3, 1, 0, 2, 4).reshape(22, 128, 2048)),
        "ffn_w_down": f(i["ffn_w_down"][0]),
        "cst": cst, "cgs": cgs, "dft": dft,
    }
    maps = []
    for b in cores:
        m = dict(shared)
        m["x"] = f(i["x"][b])
        m["ctx"] = f(i["ctx"][b])
        m["c2"] = f(np.stack([i["c"][b], i["c_ctx"]], 0))
        maps.append(m)
    return maps


def kernel(**inputs):
    nc = build_program()
    maps = make_in_maps(inputs, list(range(8)))
    res = run_bass_kernel_spmd(nc, maps, core_ids=list(range(8)))
    return np.stack([np.asarray(r["out"], dtype=np.float32) for r in res.results], 0)
```

```python
import contextlib
import os
import numpy as np
import ml_dtypes
import concourse.bass as bass
import concourse.mybir as mybir
from concourse.bass_utils import run_bass_kernel_spmd

F32 = mybir.dt.float32
BF16 = mybir.dt.bfloat16
F32R = mybir.dt.float32r
AF = mybir.ActivationFunctionType
ALU = mybir.AluOpType
AX = mybir.AxisListType

N_DMA_SEMS = 24
LAM = float(np.exp(-0.5))
NORM_EPS = 1e-6
GN_EPS = 64e-5
KK_EPS = 1e-12
SAME_ENGINE_SYNC = True
SAME_ENGINE_RAW_ONLY = True
SAME_ENGINE_RAW_ENGINES = ("dve", "act", "pool")


class Buf:
    __slots__ = ("name", "lw", "rd")

    def __init__(self, name=""):
        self.name = name
        self.lw = None
        self.rd = {}


class Op:
    __slots__ = ("eng", "fn", "deps", "signal", "sig", "is_dma", "slot", "target", "prev_on_slot")

    def __init__(self, eng, fn):
        self.eng = eng
        self.fn = fn
        self.deps = []
        self.signal = False
        self.sig = 0
        self.is_dma = False
        self.slot = 0
        self.target = 0
        self.prev_on_slot = None


class Prog:
    ENGS = ("pe", "dve", "act", "pool", "sp")

    def __init__(self, nc, same_engine_sync=True):
        self.nc = nc
        self.ops = {e: [] for e in self.ENGS}
        self.same_engine_sync = same_engine_sync
        self.dma_rr = 0
        self.dma_last = [None] * N_DMA_SEMS
        self.dma_cnt = [0] * N_DMA_SEMS
        self.bank_rr = 0

    def _add_deps(self, op, reads, writes):
        deps = {}

        def add(d, raw):
            if d is None or d is op:
                return
            if d.eng == op.eng and not d.is_dma and not op.is_dma:
                if not self.same_engine_sync:
                    return
                if op.eng == "pe":
                    return
                if not raw and SAME_ENGINE_RAW_ONLY and op.eng != "pool":
                    return
                if op.eng not in SAME_ENGINE_RAW_ENGINES:
                    return
            deps[id(d)] = d

        for b in reads:
            add(b.lw, True)
        for b in writes:
            add(b.lw, False)
            for r in b.rd.values():
                add(r, False)
        op.deps = list(deps.values())
        for d in op.deps:
            d.signal = True
        for b in reads:
            if b.name.startswith("ps") and b.rd:
                assert all(k == op.eng for k in b.rd), "two engines reading PSUM bank %s" % b.name
            b.rd[op.eng if not op.is_dma else ("dma", id(op))] = op
        for b in writes:
            b.lw = op
            b.rd = {}

    def op(self, eng, fn, reads=(), writes=()):
        o = Op(eng, fn)
        self._add_deps(o, reads, writes)
        self.ops[eng].append(o)
        return o

    def dma(self, out, in_, reads=(), writes=(), eng="sp", **kw):
        o = Op(eng, lambda e: e.dma_start(out=out, in_=in_, **kw))
        o.is_dma = True
        o.signal = True
        s = self.dma_rr
        self.dma_rr = (self.dma_rr + 1) % N_DMA_SEMS
        o.slot = s
        self.dma_cnt[s] += 16
        o.target = self.dma_cnt[s]
        o.prev_on_slot = self.dma_last[s]
        self.dma_last[s] = o
        self._add_deps(o, reads, writes)
        self.ops[eng].append(o)
        return o

    def barrier(self, scratch_out, scratch_in):
        o = self.dma(scratch_out, scratch_in)
        deps = {}
        for e in ("pe", "dve", "act", "pool"):
            for q in reversed(self.ops[e]):
                if not q.is_dma and q.fn is not None:
                    deps[id(q)] = q
                    q.signal = True
                    break
        for d in self.dma_last:
            if d is not None and d is not o:
                deps[id(d)] = d
        o.deps = list(deps.values())
        for e in ("pe", "dve", "act", "pool"):
            w = Op(e, None)
            w.deps = [o]
            self.ops[e].append(w)

    def finish(self, ops):
        o = Op("sp", None)
        o.deps = list(ops)
        self.ops["sp"].append(o)

    def mm(self, out, lhsT, rhs, start=True, stop=True, reads=(), writes=()):
        return self.op("pe", lambda e: e.matmul(out, lhsT, rhs, start=start, stop=stop), reads, writes)

    def tr(self, out, in_, ident, reads=(), writes=()):
        return self.op("pe", lambda e: e.transpose(out, in_, ident), reads, writes)

    def act(self, out, in_, func, bias=None, scale=None, accum=None, reads=(), writes=()):
        kw = {}
        if bias is not None:
            kw["bias"] = bias
        if scale is not None:
            kw["scale"] = scale
        if accum is not None:
            kw["accum_out"] = accum
        return self.op("act", lambda e: e.activation(out=out, in_=in_, func=func, **kw), reads, writes)

    def tt(self, eng, out, in0, in1, op, reads=(), writes=()):
        return self.op(eng, lambda e: e.tensor_tensor(out=out, in0=in0, in1=in1, op=op), reads, writes)

    def ts(self, eng, out, in0, s1, op0, s2=None, op1=None, reads=(), writes=()):
        if op1 is None and eng == "pool" and op0 == ALU.mult:
            return self.op(eng, lambda e: e.tensor_scalar(out=out, in0=in0, scalar1=s1, scalar2=1.0, op0=ALU.mult, op1=ALU.mult), reads, writes)
        if op1 is None:
            return self.op(eng, lambda e: e.tensor_scalar(out=out, in0=in0, scalar1=s1, scalar2=None, op0=op0), reads, writes)
        return self.op(eng, lambda e: e.tensor_scalar(out=out, in0=in0, scalar1=s1, scalar2=s2, op0=op0, op1=op1), reads, writes)

    def stt(self, out, in0, scalar, in1, op0, op1, reads=(), writes=()):
        return self.op("dve", lambda e: e.scalar_tensor_tensor(out=out, in0=in0, scalar=scalar, in1=in1, op0=op0, op1=op1), reads, writes)

    def copy(self, eng, out, in_, reads=(), writes=()):
        if eng == "act":
            return self.op("act", lambda e: e.copy(out=out, in_=in_), reads, writes)
        return self.op(eng, lambda e: e.tensor_copy(out=out, in_=in_), reads, writes)

    def memset(self, eng, ap, val, writes=()):
        return self.op(eng, lambda e: e.memset(ap, val), (), writes)

    def emit(self):
        nc = self.nc
        for e in self.ENGS:
            c = 0
            for o in self.ops[e]:
                if o.is_dma or o.fn is None:
                    continue
                if o.signal:
                    c += 1
                    o.sig = c
        with contextlib.ExitStack() as st:
            sems = {e: st.enter_context(nc.semaphore("s_" + e)) for e in self.ENGS}
            dsems = [st.enter_context(nc.semaphore("d%d" % i)) for i in range(N_DMA_SEMS)]
            block = st.enter_context(nc.Block())

            def run(e, engobj):
                waited = {}
                for o in self.ops[e]:
                    deps = list(o.deps)
                    if o.is_dma and o.prev_on_slot is not None:
                        deps.append(o.prev_on_slot)
                    for d in deps:
                        if d.is_dma:
                            key = ("d", d.slot)
                            if waited.get(key, 0) < d.target:
                                engobj.wait_ge(dsems[d.slot], d.target)
                                waited[key] = d.target
                        else:
                            key = d.eng
                            if waited.get(key, 0) < d.sig:
                                engobj.wait_ge(sems[d.eng], d.sig)
                                waited[key] = d.sig
                    if o.fn is None:
                        continue
                    ins = o.fn(engobj)
                    if o.is_dma:
                        ins.then_inc(dsems[o.slot], 16)
                    elif o.signal:
                        ins.then_inc(sems[e], 1)

            @block.tensor
            def _(eng):
                run("pe", eng)

            @block.vector
            def _(eng):
                run("dve", eng)

            @block.scalar
            def _(eng):
                run("act", eng)

            @block.gpsimd
            def _(eng):
                run("pool", eng)

            @block.sync
            def _(eng):
                run("sp", eng)


class Alloc:
    def __init__(self, arena, lo_kib, hi_kib):
        self.a = arena
        self.off = lo_kib * 256
        self.hi = hi_kib * 256

    def _take(self, words):
        o = self.off
        self.off += words
        assert self.off <= self.hi, "arena overflow %d > %d" % (self.off, self.hi)
        return o

    @staticmethod
    def _shape(ap, shape):
        if len(shape) == 2:
            return ap
        if len(shape) == 3:
            return ap.rearrange("p (a b) -> p a b", a=shape[1])
        if len(shape) == 4:
            return ap.rearrange("p (a b c) -> p a b c", a=shape[1], b=shape[2])
        raise ValueError

    def f32(self, shape):
        n = int(np.prod(shape[1:]))
        o = self._take(n)
        return self._shape(self.a[0:shape[0], o:o + n], shape)

    def bf16(self, shape):
        n = int(np.prod(shape[1:]))
        w = (n + 1) // 2
        o = self._take(w)
        ap = self.a[0:shape[0], o:o + w].bitcast(BF16)
        if w * 2 != n:
            ap = ap[:, 0:n]
        return self._shape(ap, shape)


C_ID, C_ONE, C_BLK, C_SEL, C_IDF = 0, 128, 256, 384, 512
C_M4 = (576, 1088)
C_MP = (1600, 1856)
C_END = 2112


def build_program(dbg=None, stop_after=99):
    dbg = dbg or []
    nc = bass.Bass("TRN2", target_bir_lowering=False)
    p = Prog(nc, same_engine_sync=SAME_ENGINE_SYNC)

    def din(name, shape, dt=F32):
        return nc.dram_tensor(name, list(shape), dt, kind="ExternalInput").ap()

    x_d = din("x", [2048, 1024])
    ctx_d = din("ctx", [256, 1024])
    c2_d = din("c2", [2, 1024])
    adaw_d = din("ada_w", [12, 128, 4096])
    adab_d = din("ada_b", [1, 6144])
    v1024_d = din("v1024", [2, 1024])
    v1536_d = din("v1536", [3, 1536])
    v512_d = din("v512", [10, 512])
    v5632_d = din("v5632", [10, 5632])
    fg_d = din("final_g", [1, 1024])
    win_d = din("w_in", [1024, 2432])
    w2_d = din("decay_w2", [128, 512])
    a2_d = din("iclr_a2", [128, 512])
    g2_d = din("gate_g2", [128, 512])
    wout_d = din("w_out", [2, 128, 4096])
    wup_d = din("ffn_w_up", [22, 128, 2048])
    wdn_d = din("ffn_w_down", [2816, 1024])
    cst_d = din("cst", [128, C_END])
    cgs_d = din("cgs", [128, 256], BF16)
    dft_d = din("dft", [2, 2048, 2048], BF16)
    out_d = nc.dram_tensor("out", [2048, 1024], F32, kind="ExternalOutput").ap()
    dbg_outs = {}

    with contextlib.ExitStack() as st:
        def sb(name, shape, dt=F32):
            return st.enter_context(nc.sbuf_tensor(name, list(shape), dt))

        cst = sb("cst_sb", [128, C_END])
        ident = cst[:, C_ID:C_ID + 128]
        ones = cst[:, C_ONE:C_ONE + 128]
        blkones = cst[:, C_BLK:C_BLK + 128]
        sel = cst[:, C_SEL:C_SEL + 128].rearrange("p (h k) -> p h k", h=2)
        idfold = cst[:, C_IDF:C_IDF + 64]
        mask4 = [cst[:, C_M4[d]:C_M4[d] + 512] for d in range(2)]
        mP2 = [cst[:, C_MP[d]:C_MP[d] + 256] for d in range(2)]
        cgs = sb("cgs_sb", [128, 256], BF16)
        pc1024 = sb("pc1024", [128, 8, 2])
        pc1536 = sb("pc1536", [128, 12, 3])
        pc512 = sb("pc512", [128, 4, 10])
        pc5632 = sb("pc5632", [128, 44, 10])
        modcol = sb("modcol", [128, 64])
        AB = sb("AB", [128, 6, 8])
        omka = sb("omka", [128, 4])
        bc = sb("bc", [128, 3, 1024])
        scratch = sb("scratch", [128, 16])
        arena = sb("arena", [128, 180 * 256])
        psum = [st.enter_context(nc.psum_tensor("ps%d" % i, [128, 512], F32)) for i in range(8)]
        pbuf = [Buf("ps%d" % i) for i in range(8)]
        bcst, bcgs, bprm, bmod, bAB, bbc = Buf("cst"), Buf("cgs"), Buf("prm"), Buf("modcol"), Buf("AB"), Buf("bc")

        def bank():
            i = p.bank_rr
            p.bank_rr = (p.bank_rr + 1) % 8
            return psum[i], pbuf[i]

        def barrier():
            p.barrier(scratch[0:1, 0:8], cst_d[0:1, 0:8])

        def dump(name, ap, reads):
            if name not in dbg:
                return
            shape = list(ap.shape)
            t = nc.dram_tensor("dbg_" + name, shape, ap.dtype, kind="ExternalOutput").ap()
            dbg_outs[name] = p.dma(t, ap, reads=reads)

        p.dma(cst[:], cst_d[:, :], writes=[bcst])
        p.dma(cgs[:], cgs_d[:, :], writes=[bcgs])

        hT_al = Alloc(arena, 0, 52)
        hT = hT_al.bf16([128, 8, 2048])
        hcT = hT_al.bf16([128, 8, 256])
        rwT = hT_al.bf16([128, 4, 2048])
        hTb = [[Buf("hT%d_%d" % (k, i)) for i in range(16)] for k in range(8)]
        hcTb = [Buf("hcT%d" % k) for k in range(8)]
        rwTb = [Buf("rwT%d" % i) for i in range(4)]

        al = Alloc(arena, 52, 176)
        stage = [al.f32([128, 8, 512]) for _ in range(3)]
        bstage = [Buf("st0"), Buf("st1"), Buf("st2")]
        modrow = al.f32([33, 6144])
        adab = al.f32([33, 6144])
        prm = al.f32([10, 5632])
        c2t = al.f32([128, 2, 8])
        scT = al.f32([128, 8, 33])
        bmodrow, badab, bc2, bsc = Buf("modrow"), Buf("adab"), Buf("c2"), Buf("sc")

        p.dma(c2t, c2_d.rearrange("r (p k) -> p r k", k=8), writes=[bc2])
        p.memset("pool", scT, 0.0, writes=[bsc])
        p.memset("pool", adab[0:33, :], 0.0, writes=[badab])
        p.dma(adab[0:1, :], adab_d[:, :], reads=(), writes=[badab])
        p.dma(adab[32:33, :], adab_d[:, :], reads=(), writes=[badab])
        p.act(scT[:, :, 0], c2t[:, 0, :], AF.Silu, reads=[bc2], writes=[bsc])
        p.act(scT[:, :, 32], c2t[:, 1, :], AF.Silu, reads=[bc2, bsc], writes=[bsc])
        for nb in range(12):
            s = nb % 3
            p.dma(stage[s].rearrange("p k n -> p (k n)"), adaw_d[nb], writes=[bstage[s]])
            ps, pb = bank()
            for kc in range(8):
                p.mm(ps[0:33, :], scT[:, kc, :], stage[s][:, kc, :], start=(kc == 0), stop=(kc == 7),
                     reads=[bsc, bstage[s]], writes=[pb])
            p.tt("dve", modrow[0:33, nb * 512:(nb + 1) * 512], ps[0:33, :], adab[0:33, nb * 512:(nb + 1) * 512],
                 ALU.add, reads=[pb, badab], writes=[bmodrow])
        dump("modrow", modrow[0:33, :], [bmodrow])
        ps, pb = bank()
        for v in range(6):
            for kc in range(8):
                j = v * 8 + kc
                p.mm(ps[:, 2 * j:2 * j + 2], modrow[0:1, v * 1024 + kc * 128:v * 1024 + (kc + 1) * 128],
                     ones[0:1, 0:2], reads=[bmodrow, bcst], writes=[pb])
        for v in range(2):
            for kc in range(8):
                j = 48 + v * 8 + kc
                p.mm(ps[:, 2 * j:2 * j + 2],
                     modrow[32:33, v * 1024 + kc * 128:v * 1024 + (kc + 1) * 128],
                     ones[32:33, 0:2], reads=[bmodrow, bcst], writes=[pb])
        p.copy("dve", modcol[:, :], ps[:, 0:128].rearrange("p (j t) -> p j t", t=2)[:, :, 0], reads=[pb], writes=[bmod])
        for (src, rows, width, dst) in ((v1024_d, 2, 1024, pc1024), (v1536_d, 3, 1536, pc1536),
                                        (v512_d, 10, 512, pc512), (v5632_d, 10, 5632, pc5632)):
            nch = width // 128
            re = rows + (rows % 2)
            if re != rows:
                p.memset("pool", prm[0:re, 0:width], 0.0, writes=[bprm])
            p.dma(prm[0:rows, 0:width], src[:, :], reads=[bprm], writes=[bprm])
            ps, pb = bank()
            for c in range(nch):
                p.tr(ps[:, c * re:(c + 1) * re], prm[0:re, c * 128:(c + 1) * 128], ident[0:re, 0:re],
                     reads=[bprm, bcst], writes=[pb])
            p.copy("dve", dst[:, :, :], ps[:, 0:nch * re].rearrange("p (c r) -> p c r", r=re)[:, :, 0:rows], reads=[pb], writes=[bprm])
        for (ai, sc_off, sh_off, gi) in ((0, 8, 0, 0), (2, 56, 48, 0), (4, 32, 24, 1)):
            p.ts("dve", AB[:, ai, :], modcol[:, sc_off:sc_off + 8], 1.0, ALU.add, reads=[bmod], writes=[bAB])
            p.tt("dve", AB[:, ai, :], AB[:, ai, :], pc1024[:, :, gi], ALU.mult, reads=[bAB, bprm], writes=[bAB])
            p.copy("dve", AB[:, ai + 1, :], modcol[:, sh_off:sh_off + 8], reads=[bmod], writes=[bAB])
        p.ts("dve", omka[:, :], pc512[:, :, 5], -1.0, ALU.mult, 1.0, ALU.add, reads=[bprm], writes=[bAB])
        p.dma(prm[0:1, 0:1024], fg_d[:, :], reads=[bprm], writes=[bprm])
        for (bi, row_ap) in ((0, modrow[0:1, 2048:3072]), (1, modrow[0:1, 5120:6144]), (2, prm[0:1, 0:1024])):
            for nh in range(2):
                ps, pb = bank()
                p.mm(ps[:, :], ones[0:1, 0:128], row_ap[:, nh * 512:(nh + 1) * 512],
                     reads=[bmodrow, bprm, bcst], writes=[pb])
                p.copy("act", bc[:, bi, nh * 512:(nh + 1) * 512], ps[:, :], reads=[pb], writes=[bbc])
        dump("AB", AB.rearrange("p a k -> p (a k)"), [bAB])
        dump("bc", bc.rearrange("p a k -> p (a k)"), [bbc])
        barrier()
        if stop_after <= 1:
            return _finish(nc, p, out_d, dbg_outs)

        def norm_to_T(al, src_rows, ntiles, A_ap, B_ap, dstT, dst_bufs_fn, src_is_sbuf=None, src_bufs=None):
            xt = [al.f32([128, 1024]) for _ in range(3)] if src_is_sbuf is None else None
            xs = [al.f32([128, 1024]) for _ in range(2)]
            junk = al.bf16([128, 1024])
            ss = al.f32([128, 32])
            bxt = [Buf() for _ in range(3)]
            bxs = [Buf() for _ in range(2)]
            bj = Buf()
            bss_l = [Buf() for _ in range(ntiles)]
            import os
            BIS = os.environ.get("BIS", "")
            for i in range(ntiles if "1" not in BIS else 1):
                bss = bss_l[i]
                if src_is_sbuf is None:
                    t = xt[i % 3]
                    bt = bxt[i % 3]
                    p.dma(t, src_rows(i), writes=[bt])
                else:
                    t = src_rows(i)
                    bt = src_bufs[i]
                p.act(junk, t, AF.Square, accum=ss[:, i:i + 1], reads=[bt], writes=[bj, bss])
                p.act(ss[:, i:i + 1], ss[:, i:i + 1], AF.Sqrt, bias=NORM_EPS, scale=1.0 / 1024.0, reads=[bss], writes=[bss])
                p.op("dve", lambda e, i=i: e.reciprocal(out=ss[:, i:i + 1], in_=ss[:, i:i + 1]), [bss], [bss])
                s2 = i % 2
                p.act(xs[s2], t, AF.Copy, scale=ss[:, i:i + 1], reads=[bt, bss], writes=[bxs[s2]])
                if "t" in BIS:
                    continue
                pa, pba = bank()
                pbk, pbb = bank()
                for kc in range(8):
                    pp, ppb = (pa, pba) if kc < 4 else (pbk, pbb)
                    p.tr(pp[:, (kc % 4) * 128:(kc % 4 + 1) * 128], xs[s2][:, kc * 128:(kc + 1) * 128], ident,
                         reads=[bxs[s2], bcst], writes=[ppb])
                for kc in range(8):
                    if "e" in BIS:
                        continue
                    pp, ppb = (pa, pba) if kc < 4 else (pbk, pbb)
                    src = pp[:, (kc % 4) * 128:(kc % 4 + 1) * 128]
                    dst = dstT[:, kc, i * 128:(i + 1) * 128]
                    if (kc < 4 and "D" not in BIS) or "A" in BIS:
                        p.act(dst, src, AF.Identity, bias=B_ap[:, kc:kc + 1], scale=A_ap[:, kc:kc + 1],
                              reads=[ppb, bAB], writes=[dst_bufs_fn(kc, i)])
                    else:
                        p.ts("dve", dst, src, A_ap[:, kc:kc + 1], ALU.mult, B_ap[:, kc:kc + 1], ALU.add,
                             reads=[ppb, bAB], writes=[dst_bufs_fn(kc, i)])

        al = Alloc(arena, 52, 176)
        xv = x_d.rearrange("(i p) d -> i p d", p=128)
        norm_to_T(al, lambda i: xv[i], 16, AB[:, 0, :], AB[:, 1, :], hT, lambda kc, i: hTb[kc][i])
        cv = ctx_d.rearrange("(i p) d -> i p d", p=128)
        al = Alloc(arena, 100, 176)
        norm_to_T(al, lambda i: cv[i], 2, AB[:, 2, :], AB[:, 3, :], hcT, lambda kc, i: hcTb[kc])
        dump("hT", hT.rearrange("p k t -> p (k t)"), [b for r in hTb for b in r])
        dump("hcT", hcT.rearrange("p k t -> p (k t)"), hcTb)
        barrier()
        if stop_after <= 2:
            return _finish(nc, p, out_d, dbg_outs)
        al = Alloc(arena, 52, 180)
        WID = 2320
        arr = {n: al.f32([128, WID]) for n in ("r", "k", "v", "kk", "sig", "a")}
        barr = {n: Buf(n) for n in arr}
        Ysum = al.f32([128, 16, 128])
        bY = [Buf("Y%d" % i) for i in range(16)]
        bsum = al.f32([128, 2048])
        bbs = Buf("bsum")
        twd = al.bf16([128, 2304])
        adT = al.bf16([128, 2304])
        sgd = al.bf16([128, 2048])
        btwd, badT, bsgd = Buf("twd"), Buf("adT"), Buf("sgd")
        w2b = al.bf16([128, 512])
        a2b = al.bf16([128, 512])
        g2b = al.bf16([128, 512])
        bw2 = Buf("w2")
        wrkv = arr["kk"][:, 0:1536].bitcast(BF16).rearrange("p (k x n) -> p k x n", k=8, x=3)
        bwrkv = Buf("wrkv")
        al_tmp = Alloc(arena, 52, 180)
        wlr = al_tmp.bf16([128, 8, 384])
        wlr_st = al_tmp.f32([128, 8, 384])
        assert al_tmp.off <= 52 * 256 + 2 * WID
        bwlr = Buf("wlr")

        class WK:
            pass
        GRP = int(os.environ.get("GRP", "3"))
        TS = []
        for s in range(2):
            w_ = WK()
            for nm in ("cum", "Dm", "Em", "eI", "eIn", "eE", "kdc", "bcc"):
                setattr(w_, nm, al.f32([128, 128]))
                setattr(w_, "b" + nm, Buf(nm + str(s)))
            TS.append(w_)
        PS = []
        for s in range(GRP + 1):
            w_ = WK()
            for nm in ("Kmg", "Bmg"):
                setattr(w_, nm, al.f32([128, 128]))
                setattr(w_, "b" + nm, Buf(nm + str(s)))
            w_.bKm = Buf("Km%d" % s)
            w_.bBm = Buf("Bm%d" % s)
            w_.KmH = [al.bf16([128, 128]) for _ in range(2)]
            w_.BmH = [al.bf16([128, 128]) for _ in range(2)]
            for t_ in w_.KmH + w_.BmH:
                p.memset("pool", t_, 0.0, writes=[w_.bKm, w_.bBm])
            w_.QR = al.bf16([128, 256])
            w_.QRf = al.f32([128, 128])
            w_.bQR = Buf("QR%d" % s)
            w_.cols = al.f32([128, 8])
            w_.bcols = Buf("cols%d" % s)
            w_.diagG = al.bf16([128, 64])
            w_.bdiag = Buf("diag%d" % s)
            PS.append(w_)

        def wview(j):
            w_ = WK()
            w_.__dict__.update(TS[j % 2].__dict__)
            w_.__dict__.update(PS[j % (GRP + 1)].__dict__)
            return w_
        slot_lo = al.off
        SLOT = []
        for s_ in range(GRP):
            SLOT.append(dict(tok=al.bf16([128, 384]), Xb=al.bf16([128, 2, 128]), gram=[al.bf16([128, 512]) for _ in range(2)],
                             P2=al.bf16([128, 256]), GP=[al.bf16([128, 512]) for _ in range(2)], small=al.bf16([128, 384]),
                             btok=Buf("tok%d" % s_), bX=Buf("X%d" % s_), bP2=Buf("P2%d" % s_), bsmall=Buf("small%d" % s_),
                             bgram=[Buf("gram0%d" % s_), Buf("gram1%d" % s_)], bGP=[Buf("GP0%d" % s_), Buf("GP1%d" % s_)]))
        Hs = [al.bf16([128, 2, 64]) for _ in range(2)]
        selb = al.bf16([128, 2, 64])
        p.copy("pool", selb.rearrange("p h k -> p (h k)"), cst[:, C_SEL:C_SEL + 128], reads=[bcst], writes=[bcst])
        identb3 = al.bf16([128, 128])
        p.copy("pool", identb3, ident, reads=[bcst], writes=[bcst])
        al_alias = Alloc(arena, 0, 180)
        al_alias.off = slot_lo
        tmpbc = al_alias.f32([128, 1024])
        tmpb = tmpbc[:, 0:512]
        tmpc = tmpbc[:, 512:1024]
        st3 = al_alias.f32([128, 96, 1])
        btmpb, btmpc, bst3 = Buf("tmpb"), Buf("tmpc"), Buf("st3")
        bHs = [Buf("H0"), Buf("H1")]

        def hT_reads(tb):
            return [hTb[kc][4 * tb + i] for kc in range(8) for i in range(4)]

        tblocks = [(0, 256, (lambda kc: hcT[:, kc, 0:256]), list(hcTb))]
        for tb in range(4):
            tblocks.append((256 + tb * 512, 512, (lambda kc, tb=tb: hT[:, kc, tb * 512:(tb + 1) * 512]), hT_reads(tb)))

        win_v = win_d.rearrange("(k p) n -> p k n", p=128)
        bwst = Buf("wlr_st")
        p.dma(wlr_st, win_v[:, :, 2048:2432], writes=[bwst])
        p.copy("pool", wlr, wlr_st, reads=[bwst], writes=[bwlr])
        for (dstw, srcw) in ((w2b, w2_d), (a2b, a2_d), (g2b, g2_d)):
            p.dma(wlr_st[:, 0, 0:512] if False else wlr_st.rearrange("p k n -> p (k n)")[:, 0:512], srcw[:, :], reads=[bwst], writes=[bwst])
            p.copy("pool", dstw, wlr_st.rearrange("p k n -> p (k n)")[:, 0:512], reads=[bwst], writes=[bw2])
        for (co, n, rhs_fn, rds) in tblocks:
            for jj in range(3):
                if jj == 2 and co == 0:
                    continue
                ps, pb = bank()
                for kc in range(8):
                    p.mm(ps[:, 0:n], wlr[:, kc, jj * 128:(jj + 1) * 128], rhs_fn(kc), start=(kc == 0), stop=(kc == 7),
                         reads=[bwlr] + rds, writes=[pb])
                if jj == 0:
                    p.act(twd[:, co:co + n], ps[:, 0:n], AF.Tanh, reads=[pb], writes=[btwd])
                elif jj == 1:
                    p.copy("act", adT[:, co:co + n], ps[:, 0:n], reads=[pb], writes=[badT])
                else:
                    p.act(sgd[:, co - 256:co - 256 + n], ps[:, 0:n], AF.Sigmoid, reads=[pb], writes=[bsgd])
        dump("twd", twd, [btwd])
        barrier()

        for pr in range(4 if stop_after > 3 else int(os.environ.get('NPAIRS', '1'))):
            for xi in range(3):
                sn = ("a", "a", "sig")[xi]
                so = (0, 1024, 0)[xi]
                stg = arr[sn][:, so:so + 1024].rearrange("p (k n) -> p k n", k=8)
                p.dma(stg, win_v[:, :, 512 + xi * 512 + pr * 128:512 + xi * 512 + (pr + 1) * 128],
                      writes=[barr[sn]])
                p.copy(("dve", "act", "pool")[xi], wrkv[:, :, xi, :], stg, reads=[barr[sn]], writes=[bwrkv, barr["kk"]])
            for xi, nm in enumerate(("r", "k", "v")):
                rawn = "sig" if xi != 1 else "a"
                raw = arr[rawn]
                braw = barr[rawn]
                p.memset("pool", raw[:, 0:1], 0.0, writes=[braw])
                p.memset("pool", raw[:, 257:259], 0.0, writes=[braw])
                p.memset("pool", raw[:, 2307:2308], 0.0, writes=[braw])
                for (co, n, rhs_fn, rds) in tblocks:
                    ps, pb = bank()
                    for kc in range(8):
                        p.mm(ps[:, 0:n], wrkv[:, kc, xi, :], rhs_fn(kc), start=(kc == 0), stop=(kc == 7),
                             reads=[bwrkv] + rds, writes=[pb])
                    ro = 1 if co == 0 else 259 + (co - 256)
                    p.copy("act", raw[:, ro:ro + n], ps[:, 0:n], reads=[pb], writes=[braw])
                cw = pc1536[:, xi * 4 + pr, :]
                o = arr[nm]
                bo = barr[nm]
                for (oo, n, rb) in ((0, 256, 1), (256, 2048, 259)):
                    p.ts("pool", o[:, oo:oo + n], raw[:, rb:rb + n], cw[:, 1:2], ALU.mult, reads=[braw, bprm], writes=[bo])
                    p.stt(o[:, oo:oo + n], raw[:, rb - 1:rb - 1 + n], cw[:, 0:1], o[:, oo:oo + n], ALU.mult, ALU.add,
                          reads=[braw, bprm, bo], writes=[bo])
                    p.stt(o[:, oo:oo + n], raw[:, rb + 1:rb + 1 + n], cw[:, 2:3], o[:, oo:oo + n], ALU.mult, ALU.add,
                          reads=[braw, bprm, bo], writes=[bo])
            p.ts("pool", arr["kk"][:, 0:2304], arr["k"][:, 0:2304], pc512[:, pr, 4:5], ALU.mult,
                 reads=[barr["k"], bprm], writes=[barr["kk"], bwrkv])
            p.act(arr["a"][:, 0:2304], arr["kk"][:, 0:2304], AF.Square, reads=[barr["kk"]], writes=[barr["a"]])
            for (co, n, _, _) in tblocks:
                ps, pb = bank()
                p.mm(ps[:, 0:n], blkones, arr["a"][:, co:co + n], reads=[bcst, barr["a"]], writes=[pb])
                p.act(arr["a"][:, co:co + n], ps[:, 0:n], AF.Ln, bias=KK_EPS, reads=[pb], writes=[barr["a"]])
            p.act(arr["a"][:, 0:2304], arr["a"][:, 0:2304], AF.Exp, scale=-0.5, reads=[barr["a"]], writes=[barr["a"]])
            p.tt("dve", arr["kk"][:, 0:2304], arr["kk"][:, 0:2304], arr["a"][:, 0:2304], ALU.mult,
                 reads=[barr["kk"], barr["a"]], writes=[barr["kk"]])
            if pr == 0:
                dump("r", arr["r"][:, 0:2304], [barr["r"]])
                dump("kk", arr["kk"][:, 0:2304], [barr["kk"]])
                dump("v", arr["v"][:, 0:2304], [barr["v"]])

            for d in range(2):
                hs_ = slice(64 * d, 64 * d + 64)
                for (co, n, _, _) in tblocks:
                    ps, pb = bank()
                    p.mm(ps[:, 0:n], w2b[hs_, pr * 128:(pr + 1) * 128], twd[hs_, co:co + n], reads=[bw2, btwd], writes=[pb])
                    p.act(arr["sig"][:, co:co + n], ps[:, 0:n], AF.Sigmoid, bias=pc512[:, pr, d:d + 1],
                          reads=[pb, bprm], writes=[barr["sig"]])
                    ps, pb = bank()
                    p.mm(ps[:, 0:n], a2b[hs_, pr * 128:(pr + 1) * 128], adT[hs_, co:co + n], reads=[bw2, badT], writes=[pb])
                    p.act(arr["a"][:, co:co + n], ps[:, 0:n], AF.Sigmoid, bias=pc512[:, pr, 2 + d:3 + d],
                          reads=[pb, bprm], writes=[barr["a"]])
                for tb in range(4):
                    co = 256 + tb * 512
                    tq, btq = (tmpb, btmpb) if tb % 2 == 0 else (tmpc, btmpc)
                    p.ts("pool", tq, arr["a"][:, co:co + 512], pc512[:, pr, 5:6], ALU.mult, omka[:, pr:pr + 1], ALU.add,
                         reads=[barr["a"], bprm, bAB], writes=[btq])
                    p.tt("pool", tq, tq, arr["k"][:, co:co + 512], ALU.mult, reads=[btq, barr["k"]], writes=[btq])
                    p.stt(tq, tq, pc512[:, pr, 6:7], arr["r"][:, co:co + 512], ALU.mult, ALU.mult,
                          reads=[btq, bprm, barr["r"]], writes=[btq])
                    ps, pb = bank()
                    p.mm(ps[:, :], blkones, tq, reads=[bcst, btq], writes=[pb])
                    if d == 0:
                        p.copy("act", bsum[:, tb * 512:(tb + 1) * 512], ps[:, :], reads=[pb], writes=[bbs])
                    else:
                        p.tt("dve", bsum[:, tb * 512:(tb + 1) * 512], ps[:, :], bsum[:, tb * 512:(tb + 1) * 512], ALU.add,
                             reads=[pb, bbs], writes=[bbs])
                if pr == 0 and d == 0:
                    dump("sig0", arr["sig"][:, 0:2304], [barr["sig"]])
                    dump("a0", arr["a"][:, 0:2304], [barr["a"]])

                order = list(range(18)) if d == 0 else [1, 0] + list(range(17, 1, -1))
                sgn = -LAM if d == 0 else LAM
                sig_, a_, k_, kk_, r_, v_ = arr["sig"], arr["a"], arr["k"], arr["kk"], arr["r"], arr["v"]

                def prepA(j):
                    W = wview(j)
                    co = 128 * order[j]
                    cs = slice(co, co + 128)
                    p.op("dve", lambda e: e.tensor_tensor_scan(out=W.cum, data0=sig_[:, cs], data1=sig_[:, cs], initial=0.0,
                                                               op0=ALU.add, op1=ALU.bypass), [barr["sig"]], [W.bcum])
                    p.ts("dve", W.Dm, W.cum, W.cum[:, 63:64], ALU.subtract, reads=[W.bcum], writes=[W.bDm])
                    p.tt("dve", W.Em, W.Dm, sig_[:, cs], ALU.subtract, reads=[W.bDm, barr["sig"]], writes=[W.bEm])
                    if d == 0:
                        p.act(W.eI, W.Dm, AF.Exp, scale=-LAM, reads=[W.bDm], writes=[W.beI])
                        p.act(W.eIn, W.Dm, AF.Exp, scale=LAM, reads=[W.bDm], writes=[W.beIn])
                        p.act(W.eE, W.Em, AF.Exp, scale=-LAM, reads=[W.bEm], writes=[W.beE])
                    else:
                        p.act(W.eI, W.Em, AF.Exp, scale=LAM, reads=[W.bEm], writes=[W.beI])
                        p.act(W.eIn, W.Em, AF.Exp, scale=-LAM, reads=[W.bEm], writes=[W.beIn])
                        p.act(W.eE, W.Dm, AF.Exp, scale=LAM, reads=[W.bDm], writes=[W.beE])
                    p.ts("pool", W.kdc, a_[:, cs], pc512[:, pr, 5:6], ALU.mult, omka[:, pr:pr + 1], ALU.add,
                         reads=[barr["a"], bprm, bAB], writes=[W.bkdc])
                    p.tt("pool", W.kdc, W.kdc, k_[:, cs], ALU.mult, reads=[W.bkdc, barr["k"]], writes=[W.bkdc])
                    p.tt("pool", W.bcc, kk_[:, cs], a_[:, cs], ALU.mult, reads=[barr["kk"], barr["a"]], writes=[W.bbcc])
                    p.tt("dve", W.QRf, kk_[:, cs], W.eE, ALU.mult, reads=[barr["kk"], W.beE], writes=[W.bQR])
                    p.tt("dve", W.QR[:, 0:128], kk_[:, cs], W.eE, ALU.mult, reads=[barr["kk"], W.beE], writes=[W.bQR])
                    p.tt("dve", W.QR[:, 128:256], r_[:, cs], W.eI, ALU.mult, reads=[barr["r"], W.beI, W.bQR], writes=[W.bQR])
                    for h in range(2):
                        hs = slice(64 * h, 64 * h + 64)
                        p.tt("pool", W.KmH[h][hs, :], W.kdc[hs, :], W.eIn[hs, :], ALU.mult, reads=[W.bkdc, W.beIn], writes=[W.bKm])
                        p.tt("pool", W.BmH[h][hs, :], W.bcc[hs, :], W.eIn[hs, :], ALU.mult, reads=[W.bbcc, W.beIn], writes=[W.bBm])

                def prepB(j):
                    W = wview(j)
                    N = wview(j + 1)
                    if d == 0:
                        p.tt("dve", W.cols[:, 0:1], W.Dm[:, 127:128], N.Em[:, 0:1], ALU.subtract, reads=[W.bDm, N.bEm], writes=[W.bcols])
                    else:
                        p.tt("dve", W.cols[:, 0:1], W.Em[:, 0:1], N.Dm[:, 127:128], ALU.subtract, reads=[W.bEm, N.bDm], writes=[W.bcols])
                    p.act(W.cols[:, 1:2], W.cols[:, 0:1], AF.Exp, scale=sgn, reads=[W.bcols], writes=[W.bcols])
                    gs = W.cols[:, 1:2]
                    p.stt(W.Kmg, W.kdc, gs, W.eIn, ALU.mult, ALU.mult, reads=[W.bkdc, W.beIn, W.bcols], writes=[W.bKmg])
                    p.stt(W.Bmg, W.bcc, gs, W.eIn, ALU.mult, ALU.mult, reads=[W.bbcc, W.beIn, W.bcols], writes=[W.bBmg])
                    p.ts("pool", W.diagG, idfold, gs, ALU.mult, reads=[bcst, W.bcols], writes=[W.bdiag])

                def front(j):
                    S_ = SLOT[j % GRP]
                    tok, Xb, gram, P2, GP, small = S_["tok"], S_["Xb"], S_["gram"], S_["P2"], S_["GP"], S_["small"]
                    btok, bX, bgram, bP2, bGP, bsmall = S_["btok"], S_["bX"], S_["bgram"], S_["bP2"], S_["bGP"], S_["bsmall"]
                    W = wview(j)
                    c = order[j]
                    co = 128 * c
                    cs = slice(co, co + 128)
                    last = (j == 17)
                    latent = (c >= 2)
                    nxt = 1 - cur
                    pt, pbt = bank()
                    p.tr(pt[:, 0:128], v_[:, cs], ident, reads=[barr["v"], bcst], writes=[pbt])
                    if not last:
                        p.tr(pt[:, 128:256], W.Kmg, ident, reads=[W.bKmg, bcst], writes=[pbt])
                        p.tr(pt[:, 256:384], W.Bmg, ident, reads=[W.bBmg, bcst], writes=[pbt])
                    p.tr(pt[:, 384:512], W.QRf, ident, reads=[W.bQR, bcst], writes=[pbt])
                    n = 128 if last else 384
                    p.copy("act", tok[:, 0:n], pt[:, 0:n], reads=[pbt], writes=[btok])
                    p.act(Xb[:, :, 0:64], pt[:, 384:512].rearrange("p (h k) -> p h k", h=2), AF.Copy, scale=-1.0, reads=[pbt], writes=[bX])
                    yield
                    pg = [bank(), bank()]
                    pp, pbp = bank()
                    for h in range(2):
                        hs = slice(64 * h, 64 * h + 64)
                        p.mm(pg[h][0][:, 0:256], W.BmH[h], W.QR, reads=[W.bBm, W.bQR], writes=[pg[h][1]])
                        p.mm(pg[h][0][:, 256:512], W.KmH[h], W.QR, reads=[W.bKm, W.bQR], writes=[pg[h][1]])
                        p.mm(pp[:, 128 * h:128 * h + 128], W.QR[:, 0:128], W.BmH[h], reads=[W.bBm, W.bQR], writes=[pbp])
                    for h in range(2):
                        p.tt("dve", gram[h], pg[h][0][:, :], mask4[d], ALU.mult, reads=[pg[h][1], bcst], writes=[bgram[h]])
                    p.tt("dve", P2, pp[:, 0:256], mP2[d], ALU.mult, reads=[pbp, bcst], writes=[bP2])
                    yield
                    pv, pbv = bank()
                    for h in range(2):
                        p.mm(pv[:, 64 * h:64 * h + 64], gram[h][:, 256:384], tok[:, 64 * h:64 * h + 64],
                             reads=[bgram[h], btok], writes=[pbv])
                    p.copy("act", Xb[:, :, 64:128], pv[:, 0:128].rearrange("p (h k) -> p h k", h=2), reads=[pbv], writes=[bX])
                    yield
                    Gk = [gram[h][:, 0:128] for h in range(2)]
                    Pk = [P2[:, 128 * h:128 * h + 128] for h in range(2)]
                    bG = [bgram[0], bgram[1]]
                    bP = [bP2, bP2]
                    Xf = Xb.rearrange("p h k -> p (h k)")
                    for k in range(7):
                        px, pbx = bank()
                        for h in range(2):
                            p.mm(px[:, 128 * h:128 * h + 128], Gk[h], Xb[:, h, :], reads=[bG[h], bX], writes=[pbx])
                        if k < 6:
                            pq, pbq = bank()
                            for h in range(2):
                                p.mm(pq[:, 256 * h:256 * h + 128], Pk[h], Gk[h], reads=[bP[h], bG[h]], writes=[pbq])
                                if k < 5:
                                    p.mm(pq[:, 256 * h + 128:256 * h + 256], Gk[h], Pk[h], reads=[bP[h], bG[h]], writes=[pbq])
                        p.tt("dve", Xf, px[:, 0:256], Xf, ALU.add, reads=[pbx, bX], writes=[bX])
                        if k < 6:
                            dst = GP[(k + 1) % 2]
                            bd = bGP[(k + 1) % 2]
                            if k < 5:
                                p.copy("act", dst[:, :], pq[:, :], reads=[pbq], writes=[bd])
                            else:
                                p.copy("act", dst.rearrange("p (h x) -> p h x", h=2)[:, :, 0:128],
                                       pq.rearrange("p (h x) -> p h x", h=2)[:, :, 0:128], reads=[pbq], writes=[bd])
                            Gk = [dst[:, 256 * h:256 * h + 128] for h in range(2)]
                            Pk = [dst[:, 256 * h + 128:256 * h + 256] for h in range(2)]
                            bG = [bd, bd]
                            bP = [bd, bd]
                        yield
                    if (not last) or latent:
                        psm, pbsm = bank()
                        lo, hi = (0 if not last else 128), (384 if latent else 128)
                        for h in range(2):
                            if not last:
                                p.mm(psm[0:64, 64 * h:64 * h + 64], selb[:, h, :], W.diagG, start=True, stop=False,
                                     reads=[bcst, W.bdiag], writes=[pbsm])
                                p.mm(psm[0:64, 64 * h:64 * h + 64], Xb[:, h, 0:64], tok[:, 256 + 64 * h:256 + 64 * h + 64],
                                     start=False, stop=True, reads=[bX, btok], writes=[pbsm])
                            if latent:
                                p.mm(psm[0:64, 128 + 128 * h:256 + 128 * h], selb[:, h, :], W.QR[:, 128:256], start=True, stop=False,
                                     reads=[bcst, W.bQR], writes=[pbsm])
                                p.mm(psm[0:64, 128 + 128 * h:256 + 128 * h], Xb[:, h, 0:64], gram[h][:, 128:256],
                                     start=False, stop=True, reads=[bX, bgram[h]], writes=[pbsm])
                        p.copy("act", small[0:64, lo:hi], psm[0:64, lo:hi], reads=[pbsm], writes=[bsmall])

                def back(j, cur):
                    S_ = SLOT[j % GRP]
                    tok, Xb, gram, small = S_["tok"], S_["Xb"], S_["gram"], S_["small"]
                    btok, bX, bgram, bsmall = S_["btok"], S_["bX"], S_["bgram"], S_["bsmall"]
                    c = order[j]
                    last = (j == 17)
                    latent = (c >= 2)
                    nxt = 1 - cur
                    if latent:
                        py, pby = bank()
                        for h in range(2):
                            p.mm(py[:, 64 * h:64 * h + 64], gram[h][:, 384:512], tok[:, 64 * h:64 * h + 64], start=True, stop=False,
                                 reads=[bgram[h], btok], writes=[pby])
                            p.mm(py[:, 64 * h:64 * h + 64], gram[h][:, 128:256], Xb[:, h, 64:128], start=False, stop=False,
                                 reads=[bgram[h], bX], writes=[pby])
                            p.mm(py[:, 64 * h:64 * h + 64], small[:, 128 + 128 * h:256 + 128 * h], Hs[cur][:, h, :],
                                 start=False, stop=True, reads=[bsmall, bHs[cur]], writes=[pby])
                        ti = c - 2
                        if d == 0:
                            p.copy("act", Ysum[:, ti, :], py[:, 0:128], reads=[pby], writes=[bY[ti]])
                        else:
                            p.tt("dve", Ysum[:, ti, :], py[:, 0:128], Ysum[:, ti, :], ALU.add, reads=[pby, bY[ti]], writes=[bY[ti]])
                    if not last:
                        ph, pbh = bank()
                        for h in range(2):
                            p.mm(ph[0:64, 64 * h:64 * h + 64], small[:, 64 * h:64 * h + 64], Hs[cur][:, h, :],
                                 start=True, stop=False, reads=[bsmall, bHs[cur]], writes=[pbh])
                            p.mm(ph[0:64, 64 * h:64 * h + 64], tok[:, 128 + 64 * h:128 + 64 * h + 64], tok[:, 64 * h:64 * h + 64],
                                 start=False, stop=False, reads=[btok], writes=[pbh])
                            p.mm(ph[0:64, 64 * h:64 * h + 64], tok[:, 256 + 64 * h:256 + 64 * h + 64], Xb[:, h, 64:128],
                                 start=False, stop=True, reads=[btok, bX], writes=[pbh])
                        p.copy("act", Hs[nxt][0:64].rearrange("p h k -> p (h k)"), ph[0:64, 0:128], reads=[pbh], writes=[bHs[nxt]])

                barrier()
                if d == 1:
                    p.tt("pool", bsum, bsum, arr["v"][:, 256:2304], ALU.mult, reads=[bbs, barr["v"]], writes=[bbs])
                p.memset("pool", Hs[0].rearrange("p h k -> p (h k)"), 0.0, writes=[bHs[0]])
                p.memset("pool", Hs[1].rearrange("p h k -> p (h k)"), 0.0, writes=[bHs[1]])
                for S_ in SLOT:
                    p.memset("pool", S_["small"], 0.0, writes=[S_["bsmall"]])
                cur = 0
                nsteps = 18 if stop_after > 3 else int(os.environ.get("NSTEPS", "18"))
                done_prep = set()

                def ensure_prep(jj):
                    if jj < 18 and jj not in done_prep:
                        prepA(jj)
                        done_prep.add(jj)
                        if jj >= 1:
                            prepB(jj - 1)
                j = 0
                while j < nsteps:
                    grp = list(range(j, min(j + GRP, nsteps)))
                    for jj in grp:
                        ensure_prep(jj)
                        ensure_prep(jj + 1)
                    gens = [front(jj) for jj in grp]
                    while gens:
                        for g_ in list(gens):
                            try:
                                next(g_)
                            except StopIteration:
                                gens.remove(g_)
                    for jj in grp:
                        back(jj, cur)
                        cur = 1 - cur
                        ensure_prep(jj + GRP + 1)
                    j += len(grp)
                barrier()
                if pr == 0:
                    dump("H_d%d" % d, Hs[cur][0:64].rearrange("p h k -> p (h k)"), [bHs[cur]])
                    dump("Y_d%d" % d, Ysum.rearrange("p i c -> p (i c)"), bY)

            Yv = Ysum.rearrange("p i (h n) -> p (i h) n", h=2)
            Yf = Ysum.rearrange("p i c -> p (i c)")
            sq = arr["sig"][:, 0:2048]
            p.op("dve", lambda e: e.tensor_reduce(out=st3[:, 0:32, 0], in_=Yv, axis=AX.X, op=ALU.add), bY, [bst3])
            p.act(sq, Yf, AF.Square, reads=bY, writes=[barr["sig"]])
            p.op("dve", lambda e: e.tensor_reduce(out=st3[:, 32:64, 0], in_=sq.rearrange("p (g n) -> p g n", n=64), axis=AX.X, op=ALU.add),
                 [barr["sig"], bst3], [bst3])
            p.ts("dve", st3[:, 64:96, 0], st3[:, 0:32, 0], 1.0 / 64.0, ALU.mult, reads=[bst3], writes=[bst3])
            p.tt("dve", st3[:, 0:32, 0], st3[:, 64:96, 0], st3[:, 64:96, 0], ALU.mult, reads=[bst3], writes=[bst3])
            p.stt(st3[:, 32:64, 0], st3[:, 32:64, 0], 1.0 / 64.0, st3[:, 0:32, 0], ALU.mult, ALU.subtract, reads=[bst3], writes=[bst3])
            p.act(st3[:, 32:64, 0], st3[:, 32:64, 0], AF.Sqrt, bias=GN_EPS, reads=[bst3], writes=[bst3])
            p.op("dve", lambda e: e.reciprocal(out=st3[:, 32:64, 0], in_=st3[:, 32:64, 0]), [bst3], [bst3])
            for tb in range(4):
                Yvb = Yv[:, tb * 8:(tb + 1) * 8, :]
                bYb = bY[tb * 4:(tb + 1) * 4]
                eng_n = "pool" if tb % 2 == 0 else "dve"
                p.tt(eng_n, Yvb, Yvb, st3[:, 64 + tb * 8:64 + (tb + 1) * 8, :].broadcast_to([128, 8, 64]), ALU.subtract, reads=bYb + [bst3], writes=bYb)
                p.tt(eng_n, Yvb, Yvb, st3[:, 32 + tb * 8:32 + (tb + 1) * 8, :].broadcast_to([128, 8, 64]), ALU.mult, reads=bYb + [bst3], writes=bYb)
                ps, pb = bank()
                for i in range(4):
                    p.tr(ps[:, i * 128:(i + 1) * 128], Ysum[:, tb * 4 + i, :], ident, reads=[bY[tb * 4 + i], bcst], writes=[pb])
                p.act(tmpb, ps[:, :], AF.Identity, bias=pc512[:, pr, 8:9], scale=pc512[:, pr, 7:8], reads=[pb, bprm], writes=[btmpb])
                p.tt("pool", tmpb, tmpb, bsum[:, tb * 512:(tb + 1) * 512], ALU.add, reads=[btmpb, bbs], writes=[btmpb])
                pg_, pbg_ = bank()
                p.mm(pg_[:, :], g2b[:, pr * 128:(pr + 1) * 128], sgd[:, tb * 512:(tb + 1) * 512], reads=[bw2, bsgd], writes=[pbg_])
                p.tt("dve", rwT[:, pr, tb * 512:(tb + 1) * 512], tmpb, pg_[:, :], ALU.mult, reads=[btmpb, pbg_], writes=[rwTb[pr]])
        dump("rwT", rwT[:, 0, :], rwTb)
        barrier()
        if stop_after <= 3:
            return _finish(nc, p, out_d, dbg_outs)
        al = Alloc(arena, 52, 180)
        YnT = al.bf16([128, 4, 2048])
        bYn = [Buf("Yn%d" % i) for i in range(4)]
        UfT = al.bf16([128, 4, 2048])
        bUf = [[Buf() for _ in range(4)] for _ in range(4)]
        Ucs = al.bf16([128, 16, 2, 512])
        bUcs = [Buf() for _ in range(16)]
        dft_lo = al.off
        dftb = al.bf16([128, 2, 16, 512])
        al2 = Alloc(arena, 0, 180)
        al2.off = dft_lo
        wf_st = al2.f32([128, 8, 512])
        bdft = Buf("dft")
        Yf = al.f32([128, 4, 512])
        Ysq = al.f32([128, 4, 512])
        rst = al.f32([128, 512])
        wf = al.bf16([128, 8, 512])
        bYf, bYsq, brst, bwf = Buf("Yf"), Buf("Ysq"), Buf("rst"), Buf("wf")
        p.dma(wf_st, win_v[:, :, 0:512], writes=[bdft])
        p.copy("pool", wf, wf_st, reads=[bdft], writes=[bwf])
        for cc in range(4):
            for tb in range(4):
                ps, pb = bank()
                for kc in range(8):
                    p.mm(ps[:, :], wf[:, kc, cc * 128:(cc + 1) * 128], hT[:, kc, tb * 512:(tb + 1) * 512],
                         start=(kc == 0), stop=(kc == 7), reads=[bwf] + hT_reads(tb), writes=[pb])
                p.copy("act", UfT[:, cc, tb * 512:(tb + 1) * 512], ps[:, :], reads=[pb], writes=[bUf[cc][tb]])
        for i in range(16):
            pc_, pbc_ = bank()
            ps_, pbs_ = bank()
            for cc in range(4):
                p.mm(pc_[:, cc * 128:(cc + 1) * 128], UfT[:, cc, i * 128:(i + 1) * 128], cgs[:, 0:128],
                     reads=[bUf[cc][i // 4], bcgs], writes=[pbc_])
                p.mm(ps_[:, cc * 128:(cc + 1) * 128], UfT[:, cc, i * 128:(i + 1) * 128], cgs[:, 128:256],
                     reads=[bUf[cc][i // 4], bcgs], writes=[pbs_])
            p.copy("act", Ucs[:, i, 0, :], pc_[:, :], reads=[pbc_], writes=[bUcs[i]])
            p.copy("dve", Ucs[:, i, 1, :], ps_[:, :], reads=[pbs_], writes=[bUcs[i]])
        dft_v = dft_d.rearrange("m (i p) t -> m p i t", p=128)
        bdft2 = [bdft, Buf("dft1")]
        for jb in range(4):
            accs = [bank() for _ in range(4)]
            for m in range(2):
                p.dma(dftb[:, m, :, :], dft_v[m][:, :, jb * 512:(jb + 1) * 512], reads=[bdft2[m]], writes=[bdft2[m]])
                for cc in range(4):
                    ps, pb = accs[cc]
                    for i in range(16):
                        p.mm(ps[:, :], Ucs[:, i, m, cc * 128:(cc + 1) * 128], dftb[:, m, i, :],
                             start=(m == 0 and i == 0), stop=(m == 1 and i == 15), reads=[bUcs[i], bdft2[m]], writes=[pb])
            for cc in range(4):
                ps, pb = accs[cc]
                p.copy("act", Yf[:, cc, :], ps[:, :], reads=[pb], writes=[bYf])
                p.act(Ysq[:, cc, :], ps[:, :], AF.Square, reads=[pb], writes=[bYsq])
            pss, pbss = bank()
            for cc in range(4):
                p.mm(pss[:, :], ones, Ysq[:, cc, :], start=(cc == 0), stop=(cc == 3), reads=[bcst, bYsq], writes=[pbss])
            p.act(rst, pss[:, :], AF.Ln, bias=NORM_EPS, scale=1.0 / 512.0, reads=[pbss], writes=[brst])
            p.act(rst, rst, AF.Exp, scale=-0.5, reads=[brst], writes=[brst])
            for cc in range(4):
                p.stt(YnT[:, cc, jb * 512:(jb + 1) * 512], Yf[:, cc, :], pc512[:, cc, 9:10], rst, ALU.mult, ALU.mult,
                      reads=[bYf, bprm, brst], writes=[bYn[jb]])
        dump("YnT", YnT.rearrange("p k t -> p (k t)"), bYn)
        barrier()
        if stop_after <= 4:
            return _finish(nc, p, out_d, dbg_outs)

        al = Alloc(arena, 68, 180)
        x1 = al.f32([128, 16, 1024])
        bx1 = [Buf("x1_%d" % i) for i in range(16)]
        wo = al.bf16([128, 8, 1024])
        st_lo = al.off
        wo_st = al.f32([128, 8, 512])
        xt2 = [al.f32([128, 1024]) for _ in range(2)]
        tmpy = [al.f32([128, 512]) for _ in range(2)]
        bwo, bwost = Buf("wo"), Buf("wost")
        bxt2 = [Buf(), Buf()]
        btmpy = [Buf(), Buf()]
        for nh in range(2):
            p.dma(wo_st.rearrange("p k n -> p (k n)"), wout_d[nh], reads=[bwost], writes=[bwost])
            p.copy("dve" if nh == 0 else "act", wo[:, :, nh * 512:(nh + 1) * 512], wo_st, reads=[bwost], writes=[bwo])
        for i in range(16):
            p.dma(xt2[i % 2], xv[i], writes=[bxt2[i % 2]])
            for nh in range(2):
                ps, pb = bank()
                for kc in range(8):
                    if kc < 4:
                        lhs = YnT[:, kc, i * 128:(i + 1) * 128]
                        rd = [bYn[i // 4]]
                    else:
                        lhs = rwT[:, kc - 4, i * 128:(i + 1) * 128]
                        rd = [rwTb[kc - 4]]
                    p.mm(ps[:, :], lhs, wo[:, kc, nh * 512:(nh + 1) * 512], start=(kc == 0), stop=(kc == 7),
                         reads=rd + [bwo], writes=[pb])
                k2 = nh
                p.tt("dve", tmpy[k2], ps[:, :], bc[:, 0, nh * 512:(nh + 1) * 512], ALU.mult, reads=[pb, bbc], writes=[btmpy[k2]])
                p.tt("dve", x1[:, i, nh * 512:(nh + 1) * 512], tmpy[k2], xt2[i % 2][:, nh * 512:(nh + 1) * 512], ALU.add,
                     reads=[btmpy[k2], bxt2[i % 2]], writes=[bx1[i]])
        dump("x1", x1.rearrange("p i d -> p (i d)"), bx1)
        barrier()
        al = Alloc(arena, 0, 180)
        al.off = st_lo
        norm_to_T(al, lambda i: x1[:, i, :], 16, AB[:, 4, :], AB[:, 5, :], hT, lambda kc, i: hTb[kc][i],
                  src_is_sbuf=True, src_bufs=bx1)
        dump("h2T", hT.rearrange("p k t -> p (k t)"), [b for r in hTb for b in r])
        barrier()
        if stop_after <= 5:
            return _finish(nc, p, out_d, dbg_outs)

        alA = Alloc(arena, 32, 68)
        alB = Alloc(arena, 132, 180)
        actT = alA.bf16([128, 22, 512])
        bact = [Buf("act%d" % i) for i in range(22)]
        wdnb = [alA.bf16([128, 1024]) for _ in range(2)]
        wdn_st = [alA.f32([128, 1024]) for _ in range(2)]
        bwdn = [Buf(), Buf()]
        bwdst = [Buf(), Buf()]
        wup = [alB.bf16([128, 8, 2, 128]) for _ in range(2)]
        wup_st = alB.f32([128, 8, 2, 128])
        bwup = [Buf(), Buf()]
        bwupst = Buf()
        Upad = [[alB.bf16([128, 10, 66]) for _ in range(2)] for _ in range(2)]
        bUp = [[Buf(), Buf()], [Buf(), Buf()]]
        dg = [alB.bf16([128, 2, 9, 128]) for _ in range(2)]
        bdg = [Buf(), Buf()]
        gsb = [alB.f32([128, 512]) for _ in range(2)]
        bgsb = [Buf(), Buf()]
        x2t = [alB.f32([128, 1024]) for _ in range(2)]
        bx2 = [Buf(), Buf()]
        tmpf = [alB.f32([128, 512]) for _ in range(2)]
        btmpf = [Buf(), Buf()]
        junkf = alA.bf16([128, 1024])
        ssf = alB.f32([128, 8])
        identb = alB.bf16([128, 128])
        bjf, bssf, bidb = Buf(), Buf(), Buf()
        p.copy("pool", identb, ident, reads=[bcst], writes=[bidb])
        wdn_v = wdn_d.rearrange("(k p) n -> k p n", p=128)
        out_v = out_d.rearrange("(i p) d -> i p d", p=128)
        out_dmas = []
        def load_pair(q, i):
            s = i % 2
            p.dma(wup_st.rearrange("p k g n -> p (k g n)"), wup_d[i], reads=[bwupst], writes=[bwupst])
            p.copy("act", wup[s].rearrange("p k g n -> p (k g n)"), wup_st.rearrange("p k g n -> p (k g n)"),
                   reads=[bwupst], writes=[bwup[s]])
            for gv in range(2):
                ch = gv * 22 + i
                for tap in range(9):
                    p.ts("dve", dg[s][:, gv, tap, :], identb, pc5632[:, ch, tap:tap + 1], ALU.mult,
                         reads=[bidb, bprm], writes=[bdg[s]])

        def compute_pair(q, i):
            s = i % 2
            t0 = max(0, 512 * q - 64)
            t1 = min(2048, 512 * q + 576)
            for gv in range(2):
                ta = t0
                while ta < t1:
                    tb_ = min(ta + 512, t1)
                    n = tb_ - ta
                    ps, pb = bank()
                    for kc in range(8):
                        p.mm(ps[:, 0:n], wup[s][:, kc, gv, :], hT[:, kc, ta:tb_], start=(kc == 0), stop=(kc == 7),
                             reads=[bwup[s]] + [hTb[kc][ii] for ii in range(ta // 128, (tb_ + 127) // 128)], writes=[pb])
                    r0 = ta // 64 - (8 * q - 1)
                    nr = n // 64
                    p.copy("act", Upad[s][gv][:, r0:r0 + nr, 1:65], ps[:, 0:n].rearrange("p (r c) -> p r c", c=64),
                           reads=[pb], writes=[bUp[s][gv]])
                    ta = tb_
            pcv = []
            for gv in range(2):
                ps, pb = bank()
                for tap in range(9):
                    dr, dc = tap // 3 - 1, tap % 3 - 1
                    p.mm(ps[:, :], dg[s][:, gv, tap, :], Upad[s][gv][:, 1 + dr:9 + dr, 1 + dc:65 + dc],
                         start=(tap == 0), stop=(tap == 8), reads=[bdg[s], bUp[s][gv]], writes=[pb])
                pcv.append((ps, pb))
            p.act(gsb[s], pcv[0][0][:, :], AF.Silu, bias=pc5632[:, i, 9:10], reads=[pcv[0][1], bprm], writes=[bgsb[s]])
            p.stt(actT[:, i, :], pcv[1][0][:, :], pc5632[:, 22 + i, 9:10], gsb[s], ALU.add, ALU.mult,
                  reads=[pcv[1][1], bprm, bgsb[s]], writes=[bact[i]])

        load_pair(0, 0)
        for q in range(4):
            for s in range(2):
                for gv in range(2):
                    if q == 0:
                        p.memset("pool", Upad[s][gv].rearrange("p r c -> p (r c)"), 0.0, writes=[bUp[s][gv]])
                    elif q == 3:
                        p.memset("pool", Upad[s][gv][:, 9, :], 0.0, writes=[bUp[s][gv]])
            for i in range(22):
                if i + 1 < 22:
                    load_pair(q, i + 1)
                compute_pair(q, i)
            if q == 0:
                dump("actT", actT.rearrange("p k t -> p (k t)"), bact)
            for kc in range(22):
                s = kc % 2
                p.dma(wdn_st[s], wdn_v[kc], writes=[bwdst[s]])
                p.copy("dve" if kc % 2 == 0 else "act", wdnb[s], wdn_st[s], reads=[bwdst[s]], writes=[bwdn[s]])
                for ti in range(4):
                    for nh in range(2):
                        b_ = ti * 2 + nh
                        p.mm(psum[b_][:, :], actT[:, kc, ti * 128:(ti + 1) * 128], wdnb[s][:, nh * 512:(nh + 1) * 512],
                             start=(kc == 0), stop=(kc == 21), reads=[bact[kc], bwdn[s]], writes=[pbuf[b_]])
            if q + 1 < 4:
                load_pair(q + 1, 0)
            for ti in range(4):
                gi = 4 * q + ti
                for nh in range(2):
                    b_ = ti * 2 + nh
                    k2 = (ti * 2 + nh) % 2
                    p.tt("dve", tmpf[k2], psum[b_][:, :], bc[:, 1, nh * 512:(nh + 1) * 512], ALU.mult,
                         reads=[pbuf[b_], bbc], writes=[btmpf[k2]])
                    p.tt("dve", x1[:, gi, nh * 512:(nh + 1) * 512], tmpf[k2], x1[:, gi, nh * 512:(nh + 1) * 512], ALU.add,
                         reads=[btmpf[k2], bx1[gi]], writes=[bx1[gi]])
            p.bank_rr = 0
            for ti in range(4):
                gi = 4 * q + ti
                s = ti % 2
                p.act(junkf, x1[:, gi, :], AF.Square, accum=ssf[:, ti:ti + 1], reads=[bx1[gi]], writes=[bjf, bssf])
                p.act(ssf[:, ti:ti + 1], ssf[:, ti:ti + 1], AF.Sqrt, bias=NORM_EPS, scale=1.0 / 1024.0, reads=[bssf], writes=[bssf])
                p.op("dve", lambda e, ti=ti: e.reciprocal(out=ssf[:, ti:ti + 1], in_=ssf[:, ti:ti + 1]), [bssf], [bssf])
                p.stt(x2t[s], x1[:, gi, :], ssf[:, ti:ti + 1], bc[:, 2, :], ALU.mult, ALU.mult, reads=[bx1[gi], bssf, bbc], writes=[bx2[s]])
                out_dmas.append(p.dma(out_v[gi], x2t[s], reads=[bx2[s]]))
        p.finish(out_dmas + list(dbg_outs.values()))
        p.emit()
    return nc


def _finish(nc, p, out_d, dbg_outs):
    p.finish(list(dbg_outs.values()))
    p.emit()
    return nc


def _const_tables():
    c = np.zeros((128, C_END), np.float32)
    idx = np.arange(128)
    c[:, C_ID:C_ID + 128] = np.eye(128, dtype=np.float32)
    c[:, C_ONE:C_ONE + 128] = 1.0
    blk = (idx[:, None] // 64 == idx[None, :] // 64).astype(np.float32)
    c[:, C_BLK:C_BLK + 128] = blk
    sel = np.zeros((128, 2, 64), np.float32)
    for h in range(2):
        sel[64 * h + np.arange(64), h, np.arange(64)] = 1.0
    c[:, C_SEL:C_SEL + 128] = sel.reshape(128, 128)
    c[:, C_IDF:C_IDF + 64] = (idx[:, None] % 64 == np.arange(64)[None, :]).astype(np.float32)
    s = idx[:, None]
    t = idx[None, :]
    for d in range(2):
        mG = (t > s) if d == 0 else (t < s)
        mL = (t >= s) if d == 0 else (t <= s)
        c[:, C_M4[d]:C_M4[d] + 512] = np.concatenate([-1.0 * mG, mL, -1.0 * mG, mL], 1).astype(np.float32)
        mP = (t < s) if d == 0 else (t > s)
        c[:, C_MP[d]:C_MP[d] + 256] = -np.concatenate([mP, mP], 1).astype(np.float32)
    k = np.arange(64)
    ang = 2.0 * np.pi * np.outer(k, k) / 64.0
    cg = np.zeros((128, 128), np.float64)
    sg = np.zeros((128, 128), np.float64)
    for g in range(2):
        cg[64 * g:64 * g + 64, 64 * g:64 * g + 64] = np.cos(ang) / 8.0
        sg[64 * g:64 * g + 64, 64 * g:64 * g + 64] = -np.sin(ang) / 8.0
    cgs = np.concatenate([cg, sg], 1).astype(ml_dtypes.bfloat16)
    tt = np.arange(2048)
    angT = 2.0 * np.pi * ((np.outer(tt, tt) % 2048).astype(np.float64)) / 2048.0
    dft = np.stack([np.cos(angT), np.sin(angT)], 0) / np.sqrt(2048.0)
    return c, cgs, dft.astype(ml_dtypes.bfloat16)


_CONSTS = None


def make_in_maps(inputs, cores):
    global _CONSTS
    if _CONSTS is None:
        _CONSTS = _const_tables()
    cst, cgs, dft = _CONSTS
    f = lambda a: np.ascontiguousarray(np.asarray(a, dtype=np.float32))
    i = {k: np.asarray(v) for k, v in inputs.items()}
    shared = {
        "ada_w": f(i["ada_w"][0].reshape(128, 8, 12, 512).transpose(2, 0, 1, 3).reshape(12, 128, 4096)),
        "ada_b": f(i["ada_b"][0][None, :]),
        "v1024": f(np.stack([i["norm1_g"][0], i["norm2_g"][0]], 0)),
        "v1536": f(i["rwkv_conv_w"][0]),
        "v512": f(np.concatenate([i["decay_w0"][0], i["iclr_a0"][0], i["k_k"][0][None], i["k_a"][0][None],
                                  i["r_k"][0].reshape(1, 512), i["gn_g"][0][None], i["gn_b"][0][None],
                                  i["fourier_g"][0][None]], 0)),
        "v5632": f(np.concatenate([i["ffn_conv_w"][0].reshape(9, 5632), i["ffn_conv_b"][0][None]], 0)),
        "final_g": f(i["final_g"][None, :]),
        "w_in": f(i["w_in"][0]),
        "decay_w2": f(i["decay_w2"][0].reshape(128, 512)),
        "iclr_a2": f(i["iclr_a2"][0].reshape(128, 512)),
        "gate_g2": f(i["gate_g2"][0]),
        "w_out": f(i["w_out"][0].reshape(8, 128, 2, 512).transpose(2, 1, 0, 3).reshape(2, 128, 4096)),
        "ffn_w_up": f(i["ffn_w_up"][0].reshape(8, 128, 2, 22, 128).transpose(3, 1, 0, 2, 4).reshape(22, 128, 2048)),
        "ffn_w_down": f(i["ffn_w_down"][0]),
        "cst": cst, "cgs": cgs, "dft": dft,
    }
    maps = []
    for b in cores:
        m = dict(shared)
        m["x"] = f(i["x"][b])
        m["ctx"] = f(i["ctx"][b])
        m["c2"] = f(np.stack([i["c"][b], i["c_ctx"]], 0))
        maps.append(m)
    return maps


def kernel(**inputs):
    nc = build_program()
    maps = make_in_maps(inputs, list(range(8)))
    res = run_bass_kernel_spmd(nc, maps, core_ids=list(range(8)))
    return np.stack([np.asarray(r["out"], dtype=np.float32) for r in res.results], 0)
```

```python
import contextlib
import os
import numpy as np
import ml_dtypes
import concourse.bass as bass
import concourse.mybir as mybir
from concourse.bass_utils import run_bass_kernel_spmd

F32 = mybir.dt.float32
BF16 = mybir.dt.bfloat16
F32R = mybir.dt.float32r
AF = mybir.ActivationFunctionType
ALU = mybir.AluOpType
AX = mybir.AxisListType

N_DMA_SEMS = 24
LAM = float(np.exp(-0.5))
NORM_EPS = 1e-6
GN_EPS = 64e-5
KK_EPS = 1e-12
SAME_ENGINE_SYNC = True
SAME_ENGINE_RAW_ONLY = True
SAME_ENGINE_RAW_ENGINES = ("dve", "act", "pool")


class Buf:
    __slots__ = ("name", "lw", "rd")

    def __init__(self, name=""):
        self.name = name
        self.lw = None
        self.rd = {}


class Op:
    __slots__ = ("eng", "fn", "deps", "signal", "sig", "is_dma", "slot", "target", "prev_on_slot")

    def __init__(self, eng, fn):
        self.eng = eng
        self.fn = fn
        self.deps = []
        self.signal = False
        self.sig = 0
        self.is_dma = False
        self.slot = 0
        self.target = 0
        self.prev_on_slot = None


class Prog:
    ENGS = ("pe", "dve", "act", "pool", "sp")

    def __init__(self, nc, same_engine_sync=True):
        self.nc = nc
        self.ops = {e: [] for e in self.ENGS}
        self.same_engine_sync = same_engine_sync
        self.dma_rr = 0
        self.dma_last = [None] * N_DMA_SEMS
        self.dma_cnt = [0] * N_DMA_SEMS
        self.bank_rr = 0

    def _add_deps(self, op, reads, writes):
        deps = {}

        def add(d, raw):
            if d is None or d is op:
                return
            if d.eng == op.eng and not d.is_dma and not op.is_dma:
                if not self.same_engine_sync:
                    return
                if op.eng == "pe":
                    return
                if not raw and SAME_ENGINE_RAW_ONLY and op.eng != "pool":
                    return
                if op.eng not in SAME_ENGINE_RAW_ENGINES:
                    return
            deps[id(d)] = d

        for b in reads:
            add(b.lw, True)
        for b in writes:
            add(b.lw, False)
            for r in b.rd.values():
                add(r, False)
        op.deps = list(deps.values())
        for d in op.deps:
            d.signal = True
        for b in reads:
            if b.name.startswith("ps") and b.rd:
                assert all(k == op.eng for k in b.rd), "two engines reading PSUM bank %s" % b.name
            b.rd[op.eng if not op.is_dma else ("dma", id(op))] = op
        for b in writes:
            b.lw = op
            b.rd = {}

    def op(self, eng, fn, reads=(), writes=()):
        o = Op(eng, fn)
        self._add_deps(o, reads, writes)
        self.ops[eng].append(o)
        return o

    def dma(self, out, in_, reads=(), writes=(), eng="sp", **kw):
        o = Op(eng, lambda e: e.dma_start(out=out, in_=in_, **kw))
        o.is_dma = True
        o.signal = True
        s = self.dma_rr
        self.dma_rr = (self.dma_rr + 1) % N_DMA_SEMS
        o.slot = s
        self.dma_cnt[s] += 16
        o.target = self.dma_cnt[s]
        o.prev_on_slot = self.dma_last[s]
        self.dma_last[s] = o
        self._add_deps(o, reads, writes)
        self.ops[eng].append(o)
        return o

    def barrier(self, scratch_out, scratch_in):
        o = self.dma(scratch_out, scratch_in)
        deps = {}
        for e in ("pe", "dve", "act", "pool"):
            for q in reversed(self.ops[e]):
                if not q.is_dma and q.fn is not None:
                    deps[id(q)] = q
                    q.signal = True
                    break
        for d in self.dma_last:
            if d is not None and d is not o:
                deps[id(d)] = d
        o.deps = list(deps.values())
        for e in ("pe", "dve", "act", "pool"):
            w = Op(e, None)
            w.deps = [o]
            self.ops[e].append(w)

    def finish(self, ops):
        o = Op("sp", None)
        o.deps = list(ops)
        self.ops["sp"].append(o)

    def mm(self, out, lhsT, rhs, start=True, stop=True, reads=(), writes=()):
        return self.op("pe", lambda e: e.matmul(out, lhsT, rhs, start=start, stop=stop), reads, writes)

    def tr(self, out, in_, ident, reads=(), writes=()):
        return self.op("pe", lambda e: e.transpose(out, in_, ident), reads, writes)

    def act(self, out, in_, func, bias=None, scale=None, accum=None, reads=(), writes=()):
        kw = {}
        if bias is not None:
            kw["bias"] = bias
        if scale is not None:
            kw["scale"] = scale
        if accum is not None:
            kw["accum_out"] = accum
        return self.op("act", lambda e: e.activation(out=out, in_=in_, func=func, **kw), reads, writes)

    def tt(self, eng, out, in0, in1, op, reads=(), writes=()):
        return self.op(eng, lambda e: e.tensor_tensor(out=out, in0=in0, in1=in1, op=op), reads, writes)

    def ts(self, eng, out, in0, s1, op0, s2=None, op1=None, reads=(), writes=()):
        if op1 is None and eng == "pool" and op0 == ALU.mult:
            return self.op(eng, lambda e: e.tensor_scalar(out=out, in0=in0, scalar1=s1, scalar2=1.0, op0=ALU.mult, op1=ALU.mult), reads, writes)
        if op1 is None:
            return self.op(eng, lambda e: e.tensor_scalar(out=out, in0=in0, scalar1=s1, scalar2=None, op0=op0), reads, writes)
        return self.op(eng, lambda e: e.tensor_scalar(out=out, in0=in0, scalar1=s1, scalar2=s2, op0=op0, op1=op1), reads, writes)

    def stt(self, out, in0, scalar, in1, op0, op1, reads=(), writes=()):
        return self.op("dve", lambda e: e.scalar_tensor_tensor(out=out, in0=in0, scalar=scalar, in1=in1, op0=op0, op1=op1), reads, writes)

    def copy(self, eng, out, in_, reads=(), writes=()):
        if eng == "act":
            return self.op("act", lambda e: e.copy(out=out, in_=in_), reads, writes)
        return self.op(eng, lambda e: e.tensor_copy(out=out, in_=in_), reads, writes)

    def memset(self, eng, ap, val, writes=()):
        return self.op(eng, lambda e: e.memset(ap, val), (), writes)

    def emit(self):
        nc = self.nc
        for e in self.ENGS:
            c = 0
            for o in self.ops[e]:
                if o.is_dma or o.fn is None:
                    continue
                if o.signal:
                    c += 1
                    o.sig = c
        with contextlib.ExitStack() as st:
            sems = {e: st.enter_context(nc.semaphore("s_" + e)) for e in self.ENGS}
            dsems = [st.enter_context(nc.semaphore("d%d" % i)) for i in range(N_DMA_SEMS)]
            block = st.enter_context(nc.Block())

            def run(e, engobj):
                waited = {}
                for o in self.ops[e]:
                    deps = list(o.deps)
                    if o.is_dma and o.prev_on_slot is not None:
                        deps.append(o.prev_on_slot)
                    for d in deps:
                        if d.is_dma:
                            key = ("d", d.slot)
                            if waited.get(key, 0) < d.target:
                                engobj.wait_ge(dsems[d.slot], d.target)
                                waited[key] = d.target
                        else:
                            key = d.eng
                            if waited.get(key, 0) < d.sig:
                                engobj.wait_ge(sems[d.eng], d.sig)
                                waited[key] = d.sig
                    if o.fn is None:
                        continue
                    ins = o.fn(engobj)
                    if o.is_dma:
                        ins.then_inc(dsems[o.slot], 16)
                    elif o.signal:
                        ins.then_inc(sems[e], 1)

            @block.tensor
            def _(eng):
                run("pe", eng)

            @block.vector
            def _(eng):
                run("dve", eng)

            @block.scalar
            def _(eng):
                run("act", eng)

            @block.gpsimd
            def _(eng):
                run("pool", eng)

            @block.sync
            def _(eng):
                run("sp", eng)


class Alloc:
    def __init__(self, arena, lo_kib, hi_kib):
        self.a = arena
        self.off = lo_kib * 256
        self.hi = hi_kib * 256

    def _take(self, words):
        o = self.off
        self.off += words
        assert self.off <= self.hi, "arena overflow %d > %d" % (self.off, self.hi)
        return o

    @staticmethod
    def _shape(ap, shape):
        if len(shape) == 2:
            return ap
        if len(shape) == 3:
            return ap.rearrange("p (a b) -> p a b", a=shape[1])
        if len(shape) == 4:
            return ap.rearrange("p (a b c) -> p a b c", a=shape[1], b=shape[2])
        raise ValueError

    def f32(self, shape):
        n = int(np.prod(shape[1:]))
        o = self._take(n)
        return self._shape(self.a[0:shape[0], o:o + n], shape)

    def bf16(self, shape):
        n = int(np.prod(shape[1:]))
        w = (n + 1) // 2
        o = self._take(w)
        ap = self.a[0:shape[0], o:o + w].bitcast(BF16)
        if w * 2 != n:
            ap = ap[:, 0:n]
        return self._shape(ap, shape)


C_ID, C_ONE, C_BLK, C_SEL, C_IDF = 0, 128, 256, 384, 512
C_M4 = (576, 1088)
C_MP = (1600, 1856)
C_END = 2112


def build_program(dbg=None, stop_after=99):
    dbg = dbg or []
    nc = bass.Bass("TRN2", target_bir_lowering=False)
    p = Prog(nc, same_engine_sync=SAME_ENGINE_SYNC)

    def din(name, shape, dt=F32):
        return nc.dram_tensor(name, list(shape), dt, kind="ExternalInput").ap()

    x_d = din("x", [2048, 1024])
    ctx_d = din("ctx", [256, 1024])
    c2_d = din("c2", [2, 1024])
    adaw_d = din("ada_w", [1024, 6144])
    adab_d = din("ada_b", [1, 6144])
    v1024_d = din("v1024", [2, 1024])
    v1536_d = din("v1536", [3, 1536])
    v512_d = din("v512", [10, 512])
    v5632_d = din("v5632", [10, 5632])
    fg_d = din("final_g", [1, 1024])
    win_d = din("w_in", [1024, 2432])
    w2_d = din("decay_w2", [128, 512])
    a2_d = din("iclr_a2", [128, 512])
    g2_d = din("gate_g2", [128, 512])
    wout_d = din("w_out", [1024, 1024])
    wup_d = din("ffn_w_up", [22, 128, 2048])
    wdn_d = din("ffn_w_down", [2816, 1024])
    cst_d = din("cst", [128, C_END])
    cgs_d = din("cgs", [128, 256], BF16)
    dft_d = din("dft", [2, 2048, 2048], BF16)
    out_d = nc.dram_tensor("out", [2048, 1024], F32, kind="ExternalOutput").ap()
    dbg_outs = {}

    with contextlib.ExitStack() as st:
        def sb(name, shape, dt=F32):
            return st.enter_context(nc.sbuf_tensor(name, list(shape), dt))

        cst = sb("cst_sb", [128, C_END])
        ident = cst[:, C_ID:C_ID + 128]
        ones = cst[:, C_ONE:C_ONE + 128]
        blkones = cst[:, C_BLK:C_BLK + 128]
        sel = cst[:, C_SEL:C_SEL + 128].rearrange("p (h k) -> p h k", h=2)
        idfold = cst[:, C_IDF:C_IDF + 64]
        mask4 = [cst[:, C_M4[d]:C_M4[d] + 512] for d in range(2)]
        mP2 = [cst[:, C_MP[d]:C_MP[d] + 256] for d in range(2)]
        cgs = sb("cgs_sb", [128, 256], BF16)
        pc1024 = sb("pc1024", [128, 8, 2])
        pc1536 = sb("pc1536", [128, 12, 3])
        pc512 = sb("pc512", [128, 4, 10])
        pc5632 = sb("pc5632", [128, 44, 10])
        modcol = sb("modcol", [128, 64])
        AB = sb("AB", [128, 6, 8])
        omka = sb("omka", [128, 4])
        bc = sb("bc", [128, 3, 1024])
        scratch = sb("scratch", [128, 16])
        arena = sb("arena", [128, 180 * 256])
        psum = [st.enter_context(nc.psum_tensor("ps%d" % i, [128, 512], F32)) for i in range(8)]
        pbuf = [Buf("ps%d" % i) for i in range(8)]
        bcst, bcgs, bprm, bmod, bAB, bbc = Buf("cst"), Buf("cgs"), Buf("prm"), Buf("modcol"), Buf("AB"), Buf("bc")

        def bank():
            i = p.bank_rr
            p.bank_rr = (p.bank_rr + 1) % 8
            return psum[i], pbuf[i]

        def barrier():
            p.barrier(scratch[0:1, 0:8], cst_d[0:1, 0:8])

        def dump(name, ap, reads):
            if name not in dbg:
                return
            shape = list(ap.shape)
            t = nc.dram_tensor("dbg_" + name, shape, ap.dtype, kind="ExternalOutput").ap()
            dbg_outs[name] = p.dma(t, ap, reads=reads)

        p.dma(cst[:], cst_d[:, :], writes=[bcst])
        p.dma(cgs[:], cgs_d[:, :], writes=[bcgs])

        hT_al = Alloc(arena, 0, 52)
        hT = hT_al.bf16([128, 8, 2048])
        hcT = hT_al.bf16([128, 8, 256])
        rwT = hT_al.bf16([128, 4, 2048])
        hTb = [[Buf("hT%d_%d" % (k, i)) for i in range(16)] for k in range(8)]
        hcTb = [Buf("hcT%d" % k) for k in range(8)]
        rwTb = [Buf("rwT%d" % i) for i in range(4)]

        al = Alloc(arena, 52, 176)
        stage = [al.f32([128, 8, 512]) for _ in range(3)]
        bstage = [Buf("st0"), Buf("st1"), Buf("st2")]
        modrow = al.f32([33, 6144])
        adab = al.f32([33, 6144])
        prm = al.f32([10, 5632])
        c2t = al.f32([128, 2, 8])
        scT = al.f32([128, 8, 33])
        bmodrow, badab, bc2, bsc = Buf("modrow"), Buf("adab"), Buf("c2"), Buf("sc")

        p.dma(c2t, c2_d.rearrange("r (p k) -> p r k", k=8), writes=[bc2])
        p.memset("pool", scT, 0.0, writes=[bsc])
        p.memset("pool", adab[0:33, :], 0.0, writes=[badab])
        p.dma(adab[0:1, :], adab_d[:, :], reads=(), writes=[badab])
        p.dma(adab[32:33, :], adab_d[:, :], reads=(), writes=[badab])
        p.act(scT[:, :, 0], c2t[:, 0, :], AF.Silu, reads=[bc2], writes=[bsc])
        p.act(scT[:, :, 32], c2t[:, 1, :], AF.Silu, reads=[bc2, bsc], writes=[bsc])
        adaw_v = adaw_d.rearrange("(p k) n -> p k n", k=8)
        for nb in range(12):
            s = nb % 3
            p.dma(stage[s], adaw_v[:, :, nb * 512:(nb + 1) * 512], writes=[bstage[s]])
            ps, pb = bank()
            for kc in range(8):
                p.mm(ps[0:33, :], scT[:, kc, :], stage[s][:, kc, :], start=(kc == 0), stop=(kc == 7),
                     reads=[bsc, bstage[s]], writes=[pb])
            p.tt("dve", modrow[0:33, nb * 512:(nb + 1) * 512], ps[0:33, :], adab[0:33, nb * 512:(nb + 1) * 512],
                 ALU.add, reads=[pb, badab], writes=[bmodrow])
        dump("modrow", modrow[0:33, :], [bmodrow])
        ps, pb = bank()
        for v in range(6):
            for kc in range(8):
                j = v * 8 + kc
                p.mm(ps[:, 2 * j:2 * j + 2], modrow[0:1, v * 1024 + kc * 128:v * 1024 + (kc + 1) * 128],
                     ones[0:1, 0:2], reads=[bmodrow, bcst], writes=[pb])
        for v in range(2):
            for kc in range(8):
                j = 48 + v * 8 + kc
                p.mm(ps[:, 2 * j:2 * j + 2],
                     modrow[32:33, v * 1024 + kc * 128:v * 1024 + (kc + 1) * 128],
                     ones[32:33, 0:2], reads=[bmodrow, bcst], writes=[pb])
        p.copy("dve", modcol[:, :], ps[:, 0:128].rearrange("p (j t) -> p j t", t=2)[:, :, 0], reads=[pb], writes=[bmod])
        for (src, rows, width, dst) in ((v1024_d, 2, 1024, pc1024), (v1536_d, 3, 1536, pc1536),
                                        (v512_d, 10, 512, pc512), (v5632_d, 10, 5632, pc5632)):
            nch = width // 128
            re = rows + (rows % 2)
            if re != rows:
                p.memset("pool", prm[0:re, 0:width], 0.0, writes=[bprm])
            p.dma(prm[0:rows, 0:width], src[:, :], reads=[bprm], writes=[bprm])
            ps, pb = bank()
            for c in range(nch):
                p.tr(ps[:, c * re:(c + 1) * re], prm[0:re, c * 128:(c + 1) * 128], ident[0:re, 0:re],
                     reads=[bprm, bcst], writes=[pb])
            p.copy("dve", dst[:, :, :], ps[:, 0:nch * re].rearrange("p (c r) -> p c r", r=re)[:, :, 0:rows], reads=[pb], writes=[bprm])
        for (ai, sc_off, sh_off, gi) in ((0, 8, 0, 0), (2, 56, 48, 0), (4, 32, 24, 1)):
            p.ts("dve", AB[:, ai, :], modcol[:, sc_off:sc_off + 8], 1.0, ALU.add, reads=[bmod], writes=[bAB])
            p.tt("dve", AB[:, ai, :], AB[:, ai, :], pc1024[:, :, gi], ALU.mult, reads=[bAB, bprm], writes=[bAB])
            p.copy("dve", AB[:, ai + 1, :], modcol[:, sh_off:sh_off + 8], reads=[bmod], writes=[bAB])
        p.ts("dve", omka[:, :], pc512[:, :, 5], -1.0, ALU.mult, 1.0, ALU.add, reads=[bprm], writes=[bAB])
        p.dma(prm[0:1, 0:1024], fg_d[:, :], reads=[bprm], writes=[bprm])
        for (bi, row_ap) in ((0, modrow[0:1, 2048:3072]), (1, modrow[0:1, 5120:6144]), (2, prm[0:1, 0:1024])):
            for nh in range(2):
                ps, pb = bank()
                p.mm(ps[:, :], ones[0:1, 0:128], row_ap[:, nh * 512:(nh + 1) * 512],
                     reads=[bmodrow, bprm, bcst], writes=[pb])
                p.copy("act", bc[:, bi, nh * 512:(nh + 1) * 512], ps[:, :], reads=[pb], writes=[bbc])
        dump("AB", AB.rearrange("p a k -> p (a k)"), [bAB])
        dump("bc", bc.rearrange("p a k -> p (a k)"), [bbc])
        barrier()
        if stop_after <= 1:
            return _finish(nc, p, out_d, dbg_outs)

        def norm_to_T(al, src_rows, ntiles, A_ap, B_ap, dstT, dst_bufs_fn, src_is_sbuf=None, src_bufs=None):
            xt = [al.f32([128, 1024]) for _ in range(3)] if src_is_sbuf is None else None
            xs = [al.f32([128, 1024]) for _ in range(2)]
            junk = al.bf16([128, 1024])
            ss = al.f32([128, 32])
            bxt = [Buf() for _ in range(3)]
            bxs = [Buf() for _ in range(2)]
            bj = Buf()
            bss_l = [Buf() for _ in range(ntiles)]
            import os
            BIS = os.environ.get("BIS", "")
            for i in range(ntiles if "1" not in BIS else 1):
                bss = bss_l[i]
                if src_is_sbuf is None:
                    t = xt[i % 3]
                    bt = bxt[i % 3]
                    p.dma(t, src_rows(i), writes=[bt])
                else:
                    t = src_rows(i)
                    bt = src_bufs[i]
                p.act(junk, t, AF.Square, accum=ss[:, i:i + 1], reads=[bt], writes=[bj, bss])
                p.act(ss[:, i:i + 1], ss[:, i:i + 1], AF.Sqrt, bias=NORM_EPS, scale=1.0 / 1024.0, reads=[bss], writes=[bss])
                p.op("dve", lambda e, i=i: e.reciprocal(out=ss[:, i:i + 1], in_=ss[:, i:i + 1]), [bss], [bss])
                s2 = i % 2
                p.act(xs[s2], t, AF.Copy, scale=ss[:, i:i + 1], reads=[bt, bss], writes=[bxs[s2]])
                if "t" in BIS:
                    continue
                pa, pba = bank()
                pbk, pbb = bank()
                for kc in range(8):
                    pp, ppb = (pa, pba) if kc < 4 else (pbk, pbb)
                    p.tr(pp[:, (kc % 4) * 128:(kc % 4 + 1) * 128], xs[s2][:, kc * 128:(kc + 1) * 128], ident,
                         reads=[bxs[s2], bcst], writes=[ppb])
                for kc in range(8):
                    if "e" in BIS:
                        continue
                    pp, ppb = (pa, pba) if kc < 4 else (pbk, pbb)
                    src = pp[:, (kc % 4) * 128:(kc % 4 + 1) * 128]
                    dst = dstT[:, kc, i * 128:(i + 1) * 128]
                    if (kc < 4 and "D" not in BIS) or "A" in BIS:
                        p.act(dst, src, AF.Identity, bias=B_ap[:, kc:kc + 1], scale=A_ap[:, kc:kc + 1],
                              reads=[ppb, bAB], writes=[dst_bufs_fn(kc, i)])
                    else:
                        p.ts("dve", dst, src, A_ap[:, kc:kc + 1], ALU.mult, B_ap[:, kc:kc + 1], ALU.add,
                             reads=[ppb, bAB], writes=[dst_bufs_fn(kc, i)])

        al = Alloc(arena, 52, 176)
        xv = x_d.rearrange("(i p) d -> i p d", p=128)
        norm_to_T(al, lambda i: xv[i], 16, AB[:, 0, :], AB[:, 1, :], hT, lambda kc, i: hTb[kc][i])
        cv = ctx_d.rearrange("(i p) d -> i p d", p=128)
        al = Alloc(arena, 100, 176)
        norm_to_T(al, lambda i: cv[i], 2, AB[:, 2, :], AB[:, 3, :], hcT, lambda kc, i: hcTb[kc])
        dump("hT", hT.rearrange("p k t -> p (k t)"), [b for r in hTb for b in r])
        dump("hcT", hcT.rearrange("p k t -> p (k t)"), hcTb)
        barrier()
        if stop_after <= 2:
            return _finish(nc, p, out_d, dbg_outs)
        al = Alloc(arena, 52, 180)
        WID = 2320
        arr = {n: al.f32([128, WID]) for n in ("r", "k", "v", "kk", "sig", "a")}
        barr = {n: Buf(n) for n in arr}
        Ysum = al.f32([128, 16, 128])
        bY = [Buf("Y%d" % i) for i in range(16)]
        bsum = al.f32([128, 2048])
        bbs = Buf("bsum")
        twd = al.bf16([128, 2304])
        adT = al.bf16([128, 2304])
        sgd = al.bf16([128, 2048])
        btwd, badT, bsgd = Buf("twd"), Buf("adT"), Buf("sgd")
        w2b = al.bf16([128, 512])
        a2b = al.bf16([128, 512])
        g2b = al.bf16([128, 512])
        bw2 = Buf("w2")
        wrkv = arr["kk"][:, 0:1536].bitcast(BF16).rearrange("p (k x n) -> p k x n", k=8, x=3)
        bwrkv = Buf("wrkv")
        al_tmp = Alloc(arena, 52, 180)
        wlr = al_tmp.bf16([128, 8, 384])
        wlr_st = al_tmp.f32([128, 8, 384])
        assert al_tmp.off <= 52 * 256 + 2 * WID
        bwlr = Buf("wlr")

        class WK:
            pass
        GRP = int(os.environ.get("GRP", "3"))
        TS = []
        for s in range(2):
            w_ = WK()
            for nm in ("cum", "Dm", "Em", "eI", "eIn", "eE", "kdc", "bcc"):
                setattr(w_, nm, al.f32([128, 128]))
                setattr(w_, "b" + nm, Buf(nm + str(s)))
            TS.append(w_)
        PS = []
        for s in range(GRP + 1):
            w_ = WK()
            for nm in ("Kmg", "Bmg"):
                setattr(w_, nm, al.f32([128, 128]))
                setattr(w_, "b" + nm, Buf(nm + str(s)))
            w_.bKm = Buf("Km%d" % s)
            w_.bBm = Buf("Bm%d" % s)
            w_.KmH = [al.bf16([128, 128]) for _ in range(2)]
            w_.BmH = [al.bf16([128, 128]) for _ in range(2)]
            for t_ in w_.KmH + w_.BmH:
                p.memset("pool", t_, 0.0, writes=[w_.bKm, w_.bBm])
            w_.QR = al.bf16([128, 256])
            w_.QRf = al.f32([128, 128])
            w_.bQR = Buf("QR%d" % s)
            w_.cols = al.f32([128, 8])
            w_.bcols = Buf("cols%d" % s)
            w_.diagG = al.bf16([128, 64])
            w_.bdiag = Buf("diag%d" % s)
            PS.append(w_)

        def wview(j):
            w_ = WK()
            w_.__dict__.update(TS[j % 2].__dict__)
            w_.__dict__.update(PS[j % (GRP + 1)].__dict__)
            return w_
        slot_lo = al.off
        SLOT = []
        for s_ in range(GRP):
            SLOT.append(dict(tok=al.bf16([128, 384]), Xb=al.bf16([128, 2, 128]), gram=[al.bf16([128, 512]) for _ in range(2)],
                             P2=al.bf16([128, 256]), GP=[al.bf16([128, 512]) for _ in range(2)], small=al.bf16([128, 384]),
                             btok=Buf("tok%d" % s_), bX=Buf("X%d" % s_), bP2=Buf("P2%d" % s_), bsmall=Buf("small%d" % s_),
                             bgram=[Buf("gram0%d" % s_), Buf("gram1%d" % s_)], bGP=[Buf("GP0%d" % s_), Buf("GP1%d" % s_)]))
        Hs = [al.bf16([128, 2, 64]) for _ in range(2)]
        selb = al.bf16([128, 2, 64])
        p.copy("pool", selb.rearrange("p h k -> p (h k)"), cst[:, C_SEL:C_SEL + 128], reads=[bcst], writes=[bcst])
        identb3 = al.bf16([128, 128])
        p.copy("pool", identb3, ident, reads=[bcst], writes=[bcst])
        al_alias = Alloc(arena, 0, 180)
        al_alias.off = slot_lo
        tmpbc = al_alias.f32([128, 1024])
        tmpb = tmpbc[:, 0:512]
        tmpc = tmpbc[:, 512:1024]
        st3 = al_alias.f32([128, 96, 1])
        btmpb, btmpc, bst3 = Buf("tmpb"), Buf("tmpc"), Buf("st3")
        bHs = [Buf("H0"), Buf("H1")]

        def hT_reads(tb):
            return [hTb[kc][4 * tb + i] for kc in range(8) for i in range(4)]

        tblocks = [(0, 256, (lambda kc: hcT[:, kc, 0:256]), list(hcTb))]
        for tb in range(4):
            tblocks.append((256 + tb * 512, 512, (lambda kc, tb=tb: hT[:, kc, tb * 512:(tb + 1) * 512]), hT_reads(tb)))

        win_v = win_d.rearrange("(k p) n -> p k n", p=128)
        bwst = Buf("wlr_st")
        p.dma(wlr_st, win_v[:, :, 2048:2432], writes=[bwst])
        p.copy("pool", wlr, wlr_st, reads=[bwst], writes=[bwlr])
        for (dstw, srcw) in ((w2b, w2_d), (a2b, a2_d), (g2b, g2_d)):
            p.dma(wlr_st[:, 0, 0:512] if False else wlr_st.rearrange("p k n -> p (k n)")[:, 0:512], srcw[:, :], reads=[bwst], writes=[bwst])
            p.copy("pool", dstw, wlr_st.rearrange("p k n -> p (k n)")[:, 0:512], reads=[bwst], writes=[bw2])
        for (co, n, rhs_fn, rds) in tblocks:
            for jj in range(3):
                if jj == 2 and co == 0:
                    continue
                ps, pb = bank()
                for kc in range(8):
                    p.mm(ps[:, 0:n], wlr[:, kc, jj * 128:(jj + 1) * 128], rhs_fn(kc), start=(kc == 0), stop=(kc == 7),
                         reads=[bwlr] + rds, writes=[pb])
                if jj == 0:
                    p.act(twd[:, co:co + n], ps[:, 0:n], AF.Tanh, reads=[pb], writes=[btwd])
                elif jj == 1:
                    p.copy("act", adT[:, co:co + n], ps[:, 0:n], reads=[pb], writes=[badT])
                else:
                    p.act(sgd[:, co - 256:co - 256 + n], ps[:, 0:n], AF.Sigmoid, reads=[pb], writes=[bsgd])
        dump("twd", twd, [btwd])
        barrier()

        for pr in range(4 if stop_after > 3 else int(os.environ.get('NPAIRS', '1'))):
            for xi in range(3):
                sn = ("a", "a", "sig")[xi]
                so = (0, 1024, 0)[xi]
                stg = arr[sn][:, so:so + 1024].rearrange("p (k n) -> p k n", k=8)
                p.dma(stg, win_v[:, :, 512 + xi * 512 + pr * 128:512 + xi * 512 + (pr + 1) * 128],
                      writes=[barr[sn]])
                p.copy(("dve", "act", "pool")[xi], wrkv[:, :, xi, :], stg, reads=[barr[sn]], writes=[bwrkv, barr["kk"]])
            for xi, nm in enumerate(("r", "k", "v")):
                rawn = "sig" if xi != 1 else "a"
                raw = arr[rawn]
                braw = barr[rawn]
                p.memset("pool", raw[:, 0:1], 0.0, writes=[braw])
                p.memset("pool", raw[:, 257:259], 0.0, writes=[braw])
                p.memset("pool", raw[:, 2307:2308], 0.0, writes=[braw])
                for (co, n, rhs_fn, rds) in tblocks:
                    ps, pb = bank()
                    for kc in range(8):
                        p.mm(ps[:, 0:n], wrkv[:, kc, xi, :], rhs_fn(kc), start=(kc == 0), stop=(kc == 7),
                             reads=[bwrkv] + rds, writes=[pb])
                    ro = 1 if co == 0 else 259 + (co - 256)
                    p.copy("act", raw[:, ro:ro + n], ps[:, 0:n], reads=[pb], writes=[braw])
                cw = pc1536[:, xi * 4 + pr, :]
                o = arr[nm]
                bo = barr[nm]
                for (oo, n, rb) in ((0, 256, 1), (256, 2048, 259)):
                    p.ts("pool", o[:, oo:oo + n], raw[:, rb:rb + n], cw[:, 1:2], ALU.mult, reads=[braw, bprm], writes=[bo])
                    p.stt(o[:, oo:oo + n], raw[:, rb - 1:rb - 1 + n], cw[:, 0:1], o[:, oo:oo + n], ALU.mult, ALU.add,
                          reads=[braw, bprm, bo], writes=[bo])
                    p.stt(o[:, oo:oo + n], raw[:, rb + 1:rb + 1 + n], cw[:, 2:3], o[:, oo:oo + n], ALU.mult, ALU.add,
                          reads=[braw, bprm, bo], writes=[bo])
            p.ts("pool", arr["kk"][:, 0:2304], arr["k"][:, 0:2304], pc512[:, pr, 4:5], ALU.mult,
                 reads=[barr["k"], bprm], writes=[barr["kk"], bwrkv])
            p.act(arr["a"][:, 0:2304], arr["kk"][:, 0:2304], AF.Square, reads=[barr["kk"]], writes=[barr["a"]])
            for (co, n, _, _) in tblocks:
                ps, pb = bank()
                p.mm(ps[:, 0:n], blkones, arr["a"][:, co:co + n], reads=[bcst, barr["a"]], writes=[pb])
                p.act(arr["a"][:, co:co + n], ps[:, 0:n], AF.Ln, bias=KK_EPS, reads=[pb], writes=[barr["a"]])
            p.act(arr["a"][:, 0:2304], arr["a"][:, 0:2304], AF.Exp, scale=-0.5, reads=[barr["a"]], writes=[barr["a"]])
            p.tt("dve", arr["kk"][:, 0:2304], arr["kk"][:, 0:2304], arr["a"][:, 0:2304], ALU.mult,
                 reads=[barr["kk"], barr["a"]], writes=[barr["kk"]])
            if pr == 0:
                dump("r", arr["r"][:, 0:2304], [barr["r"]])
                dump("kk", arr["kk"][:, 0:2304], [barr["kk"]])
                dump("v", arr["v"][:, 0:2304], [barr["v"]])

            for d in range(2):
                hs_ = slice(64 * d, 64 * d + 64)
                for (co, n, _, _) in tblocks:
                    ps, pb = bank()
                    p.mm(ps[:, 0:n], w2b[hs_, pr * 128:(pr + 1) * 128], twd[hs_, co:co + n], reads=[bw2, btwd], writes=[pb])
                    p.act(arr["sig"][:, co:co + n], ps[:, 0:n], AF.Sigmoid, bias=pc512[:, pr, d:d + 1],
                          reads=[pb, bprm], writes=[barr["sig"]])
                    ps, pb = bank()
                    p.mm(ps[:, 0:n], a2b[hs_, pr * 128:(pr + 1) * 128], adT[hs_, co:co + n], reads=[bw2, badT], writes=[pb])
                    p.act(arr["a"][:, co:co + n], ps[:, 0:n], AF.Sigmoid, bias=pc512[:, pr, 2 + d:3 + d],
                          reads=[pb, bprm], writes=[barr["a"]])
                for tb in range(4):
                    co = 256 + tb * 512
                    tq, btq = (tmpb, btmpb) if tb % 2 == 0 else (tmpc, btmpc)
                    p.ts("pool", tq, arr["a"][:, co:co + 512], pc512[:, pr, 5:6], ALU.mult, omka[:, pr:pr + 1], ALU.add,
                         reads=[barr["a"], bprm, bAB], writes=[btq])
                    p.tt("pool", tq, tq, arr["k"][:, co:co + 512], ALU.mult, reads=[btq, barr["k"]], writes=[btq])
                    p.stt(tq, tq, pc512[:, pr, 6:7], arr["r"][:, co:co + 512], ALU.mult, ALU.mult,
                          reads=[btq, bprm, barr["r"]], writes=[btq])
                    ps, pb = bank()
                    p.mm(ps[:, :], blkones, tq, reads=[bcst, btq], writes=[pb])
                    if d == 0:
                        p.copy("act", bsum[:, tb * 512:(tb + 1) * 512], ps[:, :], reads=[pb], writes=[bbs])
                    else:
                        p.tt("dve", bsum[:, tb * 512:(tb + 1) * 512], ps[:, :], bsum[:, tb * 512:(tb + 1) * 512], ALU.add,
                             reads=[pb, bbs], writes=[bbs])
                if pr == 0 and d == 0:
                    dump("sig0", arr["sig"][:, 0:2304], [barr["sig"]])
                    dump("a0", arr["a"][:, 0:2304], [barr["a"]])

                order = list(range(18)) if d == 0 else [1, 0] + list(range(17, 1, -1))
                sgn = -LAM if d == 0 else LAM
                sig_, a_, k_, kk_, r_, v_ = arr["sig"], arr["a"], arr["k"], arr["kk"], arr["r"], arr["v"]

                def prepA(j):
                    W = wview(j)
                    co = 128 * order[j]
                    cs = slice(co, co + 128)
                    p.op("dve", lambda e: e.tensor_tensor_scan(out=W.cum, data0=sig_[:, cs], data1=sig_[:, cs], initial=0.0,
                                                               op0=ALU.add, op1=ALU.bypass), [barr["sig"]], [W.bcum])
                    p.ts("dve", W.Dm, W.cum, W.cum[:, 63:64], ALU.subtract, reads=[W.bcum], writes=[W.bDm])
                    p.stt(W.Em, W.cum, W.cum[:, 63:64], sig_[:, cs], ALU.subtract, ALU.subtract, reads=[W.bcum, barr["sig"]], writes=[W.bEm])
                    if d == 0:
                        p.act(W.eI, W.Dm, AF.Exp, scale=-LAM, reads=[W.bDm], writes=[W.beI])
                        p.act(W.eIn, W.Dm, AF.Exp, scale=LAM, reads=[W.bDm], writes=[W.beIn])
                        p.act(W.eE, W.Em, AF.Exp, scale=-LAM, reads=[W.bEm], writes=[W.beE])
                    else:
                        p.act(W.eI, W.Em, AF.Exp, scale=LAM, reads=[W.bEm], writes=[W.beI])
                        p.act(W.eIn, W.Em, AF.Exp, scale=-LAM, reads=[W.bEm], writes=[W.beIn])
                        p.act(W.eE, W.Dm, AF.Exp, scale=LAM, reads=[W.bDm], writes=[W.beE])
                    p.ts("pool", W.kdc, a_[:, cs], pc512[:, pr, 5:6], ALU.mult, omka[:, pr:pr + 1], ALU.add,
                         reads=[barr["a"], bprm, bAB], writes=[W.bkdc])
                    p.tt("pool", W.kdc, W.kdc, k_[:, cs], ALU.mult, reads=[W.bkdc, barr["k"]], writes=[W.bkdc])
                    p.tt("pool", W.bcc, kk_[:, cs], a_[:, cs], ALU.mult, reads=[barr["kk"], barr["a"]], writes=[W.bbcc])
                    p.tt("dve", W.QRf, kk_[:, cs], W.eE, ALU.mult, reads=[barr["kk"], W.beE], writes=[W.bQR])
                    p.tt("dve", W.QR[:, 0:128], kk_[:, cs], W.eE, ALU.mult, reads=[barr["kk"], W.beE], writes=[W.bQR])
                    p.tt("dve", W.QR[:, 128:256], r_[:, cs], W.eI, ALU.mult, reads=[barr["r"], W.beI, W.bQR], writes=[W.bQR])
                    for h in range(2):
                        hs = slice(64 * h, 64 * h + 64)
                        p.tt("pool", W.KmH[h][hs, :], W.kdc[hs, :], W.eIn[hs, :], ALU.mult, reads=[W.bkdc, W.beIn], writes=[W.bKm])
                        p.tt("pool", W.BmH[h][hs, :], W.bcc[hs, :], W.eIn[hs, :], ALU.mult, reads=[W.bbcc, W.beIn], writes=[W.bBm])

                def prepB(j):
                    W = wview(j)
                    N = wview(j + 1)
                    if d == 0:
                        p.tt("dve", W.cols[:, 0:1], W.Dm[:, 127:128], N.Em[:, 0:1], ALU.subtract, reads=[W.bDm, N.bEm], writes=[W.bcols])
                    else:
                        p.tt("dve", W.cols[:, 0:1], W.Em[:, 0:1], N.Dm[:, 127:128], ALU.subtract, reads=[W.bEm, N.bDm], writes=[W.bcols])
                    p.act(W.cols[:, 1:2], W.cols[:, 0:1], AF.Exp, scale=sgn, reads=[W.bcols], writes=[W.bcols])
                    gs = W.cols[:, 1:2]
                    p.stt(W.Kmg, W.kdc, gs, W.eIn, ALU.mult, ALU.mult, reads=[W.bkdc, W.beIn, W.bcols], writes=[W.bKmg])
                    p.stt(W.Bmg, W.bcc, gs, W.eIn, ALU.mult, ALU.mult, reads=[W.bbcc, W.beIn, W.bcols], writes=[W.bBmg])
                    p.ts("pool", W.diagG, idfold, gs, ALU.mult, reads=[bcst, W.bcols], writes=[W.bdiag])

                def front(j):
                    S_ = SLOT[j % GRP]
                    tok, Xb, gram, P2, GP, small = S_["tok"], S_["Xb"], S_["gram"], S_["P2"], S_["GP"], S_["small"]
                    btok, bX, bgram, bP2, bGP, bsmall = S_["btok"], S_["bX"], S_["bgram"], S_["bP2"], S_["bGP"], S_["bsmall"]
                    W = wview(j)
                    c = order[j]
                    co = 128 * c
                    cs = slice(co, co + 128)
                    last = (j == 17)
                    latent = (c >= 2)
                    nxt = 1 - cur
                    pt, pbt = bank()
                    p.tr(pt[:, 0:128], v_[:, cs], ident, reads=[barr["v"], bcst], writes=[pbt])
                    if not last:
                        p.tr(pt[:, 128:256], W.Kmg, ident, reads=[W.bKmg, bcst], writes=[pbt])
                        p.tr(pt[:, 256:384], W.Bmg, ident, reads=[W.bBmg, bcst], writes=[pbt])
                    p.tr(pt[:, 384:512], W.QRf, ident, reads=[W.bQR, bcst], writes=[pbt])
                    n = 128 if last else 384
                    p.copy("act", tok[:, 0:n], pt[:, 0:n], reads=[pbt], writes=[btok])
                    p.act(Xb[:, :, 0:64], pt[:, 384:512].rearrange("p (h k) -> p h k", h=2), AF.Copy, scale=-1.0, reads=[pbt], writes=[bX])
                    yield
                    pg = [bank(), bank()]
                    pp, pbp = bank()
                    for h in range(2):
                        hs = slice(64 * h, 64 * h + 64)
                        p.mm(pg[h][0][:, 0:256], W.BmH[h], W.QR, reads=[W.bBm, W.bQR], writes=[pg[h][1]])
                        p.mm(pg[h][0][:, 256:512], W.KmH[h], W.QR, reads=[W.bKm, W.bQR], writes=[pg[h][1]])
                        p.mm(pp[:, 128 * h:128 * h + 128], W.QR[:, 0:128], W.BmH[h], reads=[W.bBm, W.bQR], writes=[pbp])
                    for h in range(2):
                        p.tt("dve", gram[h], pg[h][0][:, :], mask4[d], ALU.mult, reads=[pg[h][1], bcst], writes=[bgram[h]])
                    p.tt("dve", P2, pp[:, 0:256], mP2[d], ALU.mult, reads=[pbp, bcst], writes=[bP2])
                    yield
                    pv, pbv = bank()
                    for h in range(2):
                        p.mm(pv[:, 64 * h:64 * h + 64], gram[h][:, 256:384], tok[:, 64 * h:64 * h + 64],
                             reads=[bgram[h], btok], writes=[pbv])
                    p.copy("act", Xb[:, :, 64:128], pv[:, 0:128].rearrange("p (h k) -> p h k", h=2), reads=[pbv], writes=[bX])
                    yield
                    Gk = [gram[h][:, 0:128] for h in range(2)]
                    Pk = [P2[:, 128 * h:128 * h + 128] for h in range(2)]
                    bG = [bgram[0], bgram[1]]
                    bP = [bP2, bP2]
                    Xf = Xb.rearrange("p h k -> p (h k)")
                    for k in range(7):
                        px, pbx = bank()
                        for h in range(2):
                            p.mm(px[:, 128 * h:128 * h + 128], Gk[h], Xb[:, h, :], reads=[bG[h], bX], writes=[pbx])
                        if k < 6:
                            pq, pbq = bank()
                            for h in range(2):
                                p.mm(pq[:, 256 * h:256 * h + 128], Pk[h], Gk[h], reads=[bP[h], bG[h]], writes=[pbq])
                                if k < 5:
                                    p.mm(pq[:, 256 * h + 128:256 * h + 256], Gk[h], Pk[h], reads=[bP[h], bG[h]], writes=[pbq])
                        p.tt("dve", Xf, px[:, 0:256], Xf, ALU.add, reads=[pbx, bX], writes=[bX])
                        if k < 6:
                            dst = GP[(k + 1) % 2]
                            bd = bGP[(k + 1) % 2]
                            if k < 5:
                                p.copy("act", dst[:, :], pq[:, :], reads=[pbq], writes=[bd])
                            else:
                                p.copy("act", dst.rearrange("p (h x) -> p h x", h=2)[:, :, 0:128],
                                       pq.rearrange("p (h x) -> p h x", h=2)[:, :, 0:128], reads=[pbq], writes=[bd])
                            Gk = [dst[:, 256 * h:256 * h + 128] for h in range(2)]
                            Pk = [dst[:, 256 * h + 128:256 * h + 256] for h in range(2)]
                            bG = [bd, bd]
                            bP = [bd, bd]
                        yield
                    if (not last) or latent:
                        psm, pbsm = bank()
                        lo, hi = (0 if not last else 128), (384 if latent else 128)
                        for h in range(2):
                            if not last:
                                p.mm(psm[0:64, 64 * h:64 * h + 64], selb[:, h, :], W.diagG, start=True, stop=False,
                                     reads=[bcst, W.bdiag], writes=[pbsm])
                                p.mm(psm[0:64, 64 * h:64 * h + 64], Xb[:, h, 0:64], tok[:, 256 + 64 * h:256 + 64 * h + 64],
                                     start=False, stop=True, reads=[bX, btok], writes=[pbsm])
                            if latent:
                                p.mm(psm[0:64, 128 + 128 * h:256 + 128 * h], selb[:, h, :], W.QR[:, 128:256], start=True, stop=False,
                                     reads=[bcst, W.bQR], writes=[pbsm])
                                p.mm(psm[0:64, 128 + 128 * h:256 + 128 * h], Xb[:, h, 0:64], gram[h][:, 128:256],
                                     start=False, stop=True, reads=[bX, bgram[h]], writes=[pbsm])
                        p.copy("act", small[0:64, lo:hi], psm[0:64, lo:hi], reads=[pbsm], writes=[bsmall])

                def back(j, cur):
                    S_ = SLOT[j % GRP]
                    tok, Xb, gram, small = S_["tok"], S_["Xb"], S_["gram"], S_["small"]
                    btok, bX, bgram, bsmall = S_["btok"], S_["bX"], S_["bgram"], S_["bsmall"]
                    c = order[j]
                    last = (j == 17)
                    latent = (c >= 2)
                    nxt = 1 - cur
                    if latent:
                        py, pby = bank()
                        for h in range(2):
                            p.mm(py[:, 64 * h:64 * h + 64], gram[h][:, 384:512], tok[:, 64 * h:64 * h + 64], start=True, stop=False,
                                 reads=[bgram[h], btok], writes=[pby])
                            p.mm(py[:, 64 * h:64 * h + 64], gram[h][:, 128:256], Xb[:, h, 64:128], start=False, stop=False,
                                 reads=[bgram[h], bX], writes=[pby])
                            p.mm(py[:, 64 * h:64 * h + 64], small[:, 128 + 128 * h:256 + 128 * h], Hs[cur][:, h, :],
                                 start=False, stop=True, reads=[bsmall, bHs[cur]], writes=[pby])
                        ti = c - 2
                        if d == 0:
                            p.copy("act", Ysum[:, ti, :], py[:, 0:128], reads=[pby], writes=[bY[ti]])
                        else:
                            p.tt("dve", Ysum[:, ti, :], py[:, 0:128], Ysum[:, ti, :], ALU.add, reads=[pby, bY[ti]], writes=[bY[ti]])
                    if not last:
                        ph, pbh = bank()
                        for h in range(2):
                            p.mm(ph[0:64, 64 * h:64 * h + 64], small[:, 64 * h:64 * h + 64], Hs[cur][:, h, :],
                                 start=True, stop=False, reads=[bsmall, bHs[cur]], writes=[pbh])
                            p.mm(ph[0:64, 64 * h:64 * h + 64], tok[:, 128 + 64 * h:128 + 64 * h + 64], tok[:, 64 * h:64 * h + 64],
                                 start=False, stop=False, reads=[btok], writes=[pbh])
                            p.mm(ph[0:64, 64 * h:64 * h + 64], tok[:, 256 + 64 * h:256 + 64 * h + 64], Xb[:, h, 64:128],
                                 start=False, stop=True, reads=[btok, bX], writes=[pbh])
                        p.copy("act", Hs[nxt][0:64].rearrange("p h k -> p (h k)"), ph[0:64, 0:128], reads=[pbh], writes=[bHs[nxt]])

                barrier()
                if d == 1:
                    p.tt("pool", bsum, bsum, arr["v"][:, 256:2304], ALU.mult, reads=[bbs, barr["v"]], writes=[bbs])
                p.memset("pool", Hs[0].rearrange("p h k -> p (h k)"), 0.0, writes=[bHs[0]])
                p.memset("pool", Hs[1].rearrange("p h k -> p (h k)"), 0.0, writes=[bHs[1]])
                for S_ in SLOT:
                    p.memset("pool", S_["small"], 0.0, writes=[S_["bsmall"]])
                cur = 0
                nsteps = 18 if stop_after > 3 else int(os.environ.get("NSTEPS", "18"))
                done_prep = set()

                def ensure_prep(jj):
                    if jj < 18 and jj not in done_prep:
                        prepA(jj)
                        done_prep.add(jj)
                        if jj >= 1:
                            prepB(jj - 1)
                j = 0
                while j < nsteps:
                    grp = list(range(j, min(j + GRP, nsteps)))
                    for jj in grp:
                        ensure_prep(jj)
                        ensure_prep(jj + 1)
                    gens = [front(jj) for jj in grp]
                    while gens:
                        for g_ in list(gens):
                            try:
                                next(g_)
                            except StopIteration:
                                gens.remove(g_)
                    for jj in grp:
                        back(jj, cur)
                        cur = 1 - cur
                        ensure_prep(jj + GRP + 1)
                    j += len(grp)
                barrier()
                if pr == 0:
                    dump("H_d%d" % d, Hs[cur][0:64].rearrange("p h k -> p (h k)"), [bHs[cur]])
                    dump("Y_d%d" % d, Ysum.rearrange("p i c -> p (i c)"), bY)

            Yv = Ysum.rearrange("p i (h n) -> p (i h) n", h=2)
            Yf = Ysum.rearrange("p i c -> p (i c)")
            sq = arr["sig"][:, 0:2048]
            p.op("dve", lambda e: e.tensor_reduce(out=st3[:, 0:32, 0], in_=Yv, axis=AX.X, op=ALU.add), bY, [bst3])
            p.act(sq, Yf, AF.Square, reads=bY, writes=[barr["sig"]])
            p.op("dve", lambda e: e.tensor_reduce(out=st3[:, 32:64, 0], in_=sq.rearrange("p (g n) -> p g n", n=64), axis=AX.X, op=ALU.add),
                 [barr["sig"], bst3], [bst3])
            p.ts("dve", st3[:, 64:96, 0], st3[:, 0:32, 0], 1.0 / 64.0, ALU.mult, reads=[bst3], writes=[bst3])
            p.tt("dve", st3[:, 0:32, 0], st3[:, 64:96, 0], st3[:, 64:96, 0], ALU.mult, reads=[bst3], writes=[bst3])
            p.stt(st3[:, 32:64, 0], st3[:, 32:64, 0], 1.0 / 64.0, st3[:, 0:32, 0], ALU.mult, ALU.subtract, reads=[bst3], writes=[bst3])
            p.act(st3[:, 32:64, 0], st3[:, 32:64, 0], AF.Sqrt, bias=GN_EPS, reads=[bst3], writes=[bst3])
            p.op("dve", lambda e: e.reciprocal(out=st3[:, 32:64, 0], in_=st3[:, 32:64, 0]), [bst3], [bst3])
            for tb in range(4):
                Yvb = Yv[:, tb * 8:(tb + 1) * 8, :]
                bYb = bY[tb * 4:(tb + 1) * 4]
                eng_n = "pool" if tb % 2 == 0 else "dve"
                p.tt(eng_n, Yvb, Yvb, st3[:, 64 + tb * 8:64 + (tb + 1) * 8, :].broadcast_to([128, 8, 64]), ALU.subtract, reads=bYb + [bst3], writes=bYb)
                p.tt(eng_n, Yvb, Yvb, st3[:, 32 + tb * 8:32 + (tb + 1) * 8, :].broadcast_to([128, 8, 64]), ALU.mult, reads=bYb + [bst3], writes=bYb)
                ps, pb = bank()
                for i in range(4):
                    p.tr(ps[:, i * 128:(i + 1) * 128], Ysum[:, tb * 4 + i, :], ident, reads=[bY[tb * 4 + i], bcst], writes=[pb])
                p.act(tmpb, ps[:, :], AF.Identity, bias=pc512[:, pr, 8:9], scale=pc512[:, pr, 7:8], reads=[pb, bprm], writes=[btmpb])
                p.tt("pool", tmpb, tmpb, bsum[:, tb * 512:(tb + 1) * 512], ALU.add, reads=[btmpb, bbs], writes=[btmpb])
                pg_, pbg_ = bank()
                p.mm(pg_[:, :], g2b[:, pr * 128:(pr + 1) * 128], sgd[:, tb * 512:(tb + 1) * 512], reads=[bw2, bsgd], writes=[pbg_])
                p.tt("dve", rwT[:, pr, tb * 512:(tb + 1) * 512], tmpb, pg_[:, :], ALU.mult, reads=[btmpb, pbg_], writes=[rwTb[pr]])
        dump("rwT", rwT[:, 0, :], rwTb)
        barrier()
        if stop_after <= 3:
            return _finish(nc, p, out_d, dbg_outs)
        al = Alloc(arena, 52, 180)
        YnT = al.bf16([128, 4, 2048])
        bYn = [Buf("Yn%d" % i) for i in range(4)]
        UfT = al.bf16([128, 4, 2048])
        bUf = [[Buf() for _ in range(4)] for _ in range(4)]
        Ucs = al.bf16([128, 16, 2, 512])
        bUcs = [Buf() for _ in range(16)]
        dft_lo = al.off
        dftb = al.bf16([128, 2, 16, 512])
        al2 = Alloc(arena, 0, 180)
        al2.off = dft_lo
        wf_st = al2.f32([128, 8, 512])
        bdft = Buf("dft")
        Yf = al.f32([128, 4, 512])
        Ysq = al.f32([128, 4, 512])
        rst = al.f32([128, 512])
        wf = al.bf16([128, 8, 512])
        bYf, bYsq, brst, bwf = Buf("Yf"), Buf("Ysq"), Buf("rst"), Buf("wf")
        p.dma(wf_st, win_v[:, :, 0:512], writes=[bdft])
        p.copy("pool", wf, wf_st, reads=[bdft], writes=[bwf])
        for cc in range(4):
            for tb in range(4):
                ps, pb = bank()
                for kc in range(8):
                    p.mm(ps[:, :], wf[:, kc, cc * 128:(cc + 1) * 128], hT[:, kc, tb * 512:(tb + 1) * 512],
                         start=(kc == 0), stop=(kc == 7), reads=[bwf] + hT_reads(tb), writes=[pb])
                p.copy("act", UfT[:, cc, tb * 512:(tb + 1) * 512], ps[:, :], reads=[pb], writes=[bUf[cc][tb]])
        for i in range(16):
            pc_, pbc_ = bank()
            ps_, pbs_ = bank()
            for cc in range(4):
                p.mm(pc_[:, cc * 128:(cc + 1) * 128], UfT[:, cc, i * 128:(i + 1) * 128], cgs[:, 0:128],
                     reads=[bUf[cc][i // 4], bcgs], writes=[pbc_])
                p.mm(ps_[:, cc * 128:(cc + 1) * 128], UfT[:, cc, i * 128:(i + 1) * 128], cgs[:, 128:256],
                     reads=[bUf[cc][i // 4], bcgs], writes=[pbs_])
            p.copy("act", Ucs[:, i, 0, :], pc_[:, :], reads=[pbc_], writes=[bUcs[i]])
            p.copy("dve", Ucs[:, i, 1, :], ps_[:, :], reads=[pbs_], writes=[bUcs[i]])
        dft_v = dft_d.rearrange("m (i p) t -> m p i t", p=128)
        bdft2 = [bdft, Buf("dft1")]
        for jb in range(4):
            accs = [bank() for _ in range(4)]
            for m in range(2):
                p.dma(dftb[:, m, :, :], dft_v[m][:, :, jb * 512:(jb + 1) * 512], reads=[bdft2[m]], writes=[bdft2[m]])
                for cc in range(4):
                    ps, pb = accs[cc]
                    for i in range(16):
                        p.mm(ps[:, :], Ucs[:, i, m, cc * 128:(cc + 1) * 128], dftb[:, m, i, :],
                             start=(m == 0 and i == 0), stop=(m == 1 and i == 15), reads=[bUcs[i], bdft2[m]], writes=[pb])
            for cc in range(4):
                ps, pb = accs[cc]
                p.copy("act", Yf[:, cc, :], ps[:, :], reads=[pb], writes=[bYf])
                p.act(Ysq[:, cc, :], ps[:, :], AF.Square, reads=[pb], writes=[bYsq])
            pss, pbss = bank()
            for cc in range(4):
                p.mm(pss[:, :], ones, Ysq[:, cc, :], start=(cc == 0), stop=(cc == 3), reads=[bcst, bYsq], writes=[pbss])
            p.act(rst, pss[:, :], AF.Ln, bias=NORM_EPS, scale=1.0 / 512.0, reads=[pbss], writes=[brst])
            p.act(rst, rst, AF.Exp, scale=-0.5, reads=[brst], writes=[brst])
            for cc in range(4):
                p.stt(YnT[:, cc, jb * 512:(jb + 1) * 512], Yf[:, cc, :], pc512[:, cc, 9:10], rst, ALU.mult, ALU.mult,
                      reads=[bYf, bprm, brst], writes=[bYn[jb]])
        dump("YnT", YnT.rearrange("p k t -> p (k t)"), bYn)
        barrier()
        if stop_after <= 4:
            return _finish(nc, p, out_d, dbg_outs)

        al = Alloc(arena, 68, 180)
        x1 = al.f32([128, 16, 1024])
        bx1 = [Buf("x1_%d" % i) for i in range(16)]
        wo = al.bf16([128, 8, 1024])
        st_lo = al.off
        wo_st = al.f32([128, 8, 512])
        xt2 = [al.f32([128, 1024]) for _ in range(2)]
        tmpy = [al.f32([128, 512]) for _ in range(2)]
        bwo, bwost = Buf("wo"), Buf("wost")
        bxt2 = [Buf(), Buf()]
        btmpy = [Buf(), Buf()]
        wout_v = wout_d.rearrange("(k p) n -> p k n", p=128)
        for nh in range(2):
            p.dma(wo_st, wout_v[:, :, nh * 512:(nh + 1) * 512], reads=[bwost], writes=[bwost])
            p.copy("dve" if nh == 0 else "act", wo[:, :, nh * 512:(nh + 1) * 512], wo_st, reads=[bwost], writes=[bwo])
        for i in range(16):
            p.dma(xt2[i % 2], xv[i], writes=[bxt2[i % 2]])
            for nh in range(2):
                ps, pb = bank()
                for kc in range(8):
                    if kc < 4:
                        lhs = YnT[:, kc, i * 128:(i + 1) * 128]
                        rd = [bYn[i // 4]]
                    else:
                        lhs = rwT[:, kc - 4, i * 128:(i + 1) * 128]
                        rd = [rwTb[kc - 4]]
                    p.mm(ps[:, :], lhs, wo[:, kc, nh * 512:(nh + 1) * 512], start=(kc == 0), stop=(kc == 7),
                         reads=rd + [bwo], writes=[pb])
                k2 = nh
                p.tt("dve", tmpy[k2], ps[:, :], bc[:, 0, nh * 512:(nh + 1) * 512], ALU.mult, reads=[pb, bbc], writes=[btmpy[k2]])
                p.tt("dve", x1[:, i, nh * 512:(nh + 1) * 512], tmpy[k2], xt2[i % 2][:, nh * 512:(nh + 1) * 512], ALU.add,
                     reads=[btmpy[k2], bxt2[i % 2]], writes=[bx1[i]])
        dump("x1", x1.rearrange("p i d -> p (i d)"), bx1)
        barrier()
        al = Alloc(arena, 0, 180)
        al.off = st_lo
        norm_to_T(al, lambda i: x1[:, i, :], 16, AB[:, 4, :], AB[:, 5, :], hT, lambda kc, i: hTb[kc][i],
                  src_is_sbuf=True, src_bufs=bx1)
        dump("h2T", hT.rearrange("p k t -> p (k t)"), [b for r in hTb for b in r])
        barrier()
        if stop_after <= 5:
            return _finish(nc, p, out_d, dbg_outs)

        alA = Alloc(arena, 32, 68)
        alB = Alloc(arena, 132, 180)
        actT = alA.bf16([128, 22, 512])
        bact = [Buf("act%d" % i) for i in range(22)]
        wdnb = [alA.bf16([128, 1024]) for _ in range(2)]
        wdn_st = [alA.f32([128, 1024]) for _ in range(2)]
        bwdn = [Buf(), Buf()]
        bwdst = [Buf(), Buf()]
        wup = [alB.bf16([128, 8, 2, 128]) for _ in range(2)]
        wup_st = alB.f32([128, 8, 2, 128])
        bwup = [Buf(), Buf()]
        bwupst = Buf()
        Upad = [[alB.bf16([128, 10, 66]) for _ in range(2)] for _ in range(2)]
        bUp = [[Buf(), Buf()], [Buf(), Buf()]]
        dg = [alB.bf16([128, 2, 9, 128]) for _ in range(2)]
        bdg = [Buf(), Buf()]
        gsb = [alB.f32([128, 512]) for _ in range(2)]
        bgsb = [Buf(), Buf()]
        x2t = [alB.f32([128, 1024]) for _ in range(2)]
        bx2 = [Buf(), Buf()]
        tmpf = [alB.f32([128, 512]) for _ in range(2)]
        btmpf = [Buf(), Buf()]
        junkf = alA.bf16([128, 1024])
        ssf = alB.f32([128, 8])
        identb = alB.bf16([128, 128])
        bjf, bssf, bidb = Buf(), Buf(), Buf()
        p.copy("pool", identb, ident, reads=[bcst], writes=[bidb])
        wdn_v = wdn_d.rearrange("(k p) n -> k p n", p=128)
        out_v = out_d.rearrange("(i p) d -> i p d", p=128)
        out_dmas = []
        def load_pair(q, i):
            s = i % 2
            p.dma(wup_st.rearrange("p k g n -> p (k g n)"), wup_d[i], reads=[bwupst], writes=[bwupst])
            p.copy("act", wup[s].rearrange("p k g n -> p (k g n)"), wup_st.rearrange("p k g n -> p (k g n)"),
                   reads=[bwupst], writes=[bwup[s]])
            for gv in range(2):
                ch = gv * 22 + i
                for tap in range(9):
                    p.ts("dve", dg[s][:, gv, tap, :], identb, pc5632[:, ch, tap:tap + 1], ALU.mult,
                         reads=[bidb, bprm], writes=[bdg[s]])

        def compute_pair(q, i):
            s = i % 2
            t0 = max(0, 512 * q - 64)
            t1 = min(2048, 512 * q + 576)
            for gv in range(2):
                ta = t0
                while ta < t1:
                    tb_ = min(ta + 512, t1)
                    n = tb_ - ta
                    ps, pb = bank()
                    for kc in range(8):
                        p.mm(ps[:, 0:n], wup[s][:, kc, gv, :], hT[:, kc, ta:tb_], start=(kc == 0), stop=(kc == 7),
                             reads=[bwup[s]] + [hTb[kc][ii] for ii in range(ta // 128, (tb_ + 127) // 128)], writes=[pb])
                    r0 = ta // 64 - (8 * q - 1)
                    nr = n // 64
                    p.copy("act", Upad[s][gv][:, r0:r0 + nr, 1:65], ps[:, 0:n].rearrange("p (r c) -> p r c", c=64),
                           reads=[pb], writes=[bUp[s][gv]])
                    ta = tb_
            pcv = []
            for gv in range(2):
                ps, pb = bank()
                for tap in range(9):
                    dr, dc = tap // 3 - 1, tap % 3 - 1
                    p.mm(ps[:, :], dg[s][:, gv, tap, :], Upad[s][gv][:, 1 + dr:9 + dr, 1 + dc:65 + dc],
                         start=(tap == 0), stop=(tap == 8), reads=[bdg[s], bUp[s][gv]], writes=[pb])
                pcv.append((ps, pb))
            p.act(gsb[s], pcv[0][0][:, :], AF.Silu, bias=pc5632[:, i, 9:10], reads=[pcv[0][1], bprm], writes=[bgsb[s]])
            p.stt(actT[:, i, :], pcv[1][0][:, :], pc5632[:, 22 + i, 9:10], gsb[s], ALU.add, ALU.mult,
                  reads=[pcv[1][1], bprm, bgsb[s]], writes=[bact[i]])

        load_pair(0, 0)
        for q in range(4):
            for s in range(2):
                for gv in range(2):
                    if q == 0:
                        p.memset("pool", Upad[s][gv].rearrange("p r c -> p (r c)"), 0.0, writes=[bUp[s][gv]])
                    elif q == 3:
                        p.memset("pool", Upad[s][gv][:, 9, :], 0.0, writes=[bUp[s][gv]])
            for i in range(22):
                if i + 1 < 22:
                    load_pair(q, i + 1)
                compute_pair(q, i)
            if q == 0:
                dump("actT", actT.rearrange("p k t -> p (k t)"), bact)
            for kc in range(22):
                s = kc % 2
                p.dma(wdn_st[s], wdn_v[kc], writes=[bwdst[s]])
                p.copy("dve" if kc % 2 == 0 else "act", wdnb[s], wdn_st[s], reads=[bwdst[s]], writes=[bwdn[s]])
                for ti in range(4):
                    for nh in range(2):
                        b_ = ti * 2 + nh
                        p.mm(psum[b_][:, :], actT[:, kc, ti * 128:(ti + 1) * 128], wdnb[s][:, nh * 512:(nh + 1) * 512],
                             start=(kc == 0), stop=(kc == 21), reads=[bact[kc], bwdn[s]], writes=[pbuf[b_]])
            if q + 1 < 4:
                load_pair(q + 1, 0)
            for ti in range(4):
                gi = 4 * q + ti
                for nh in range(2):
                    b_ = ti * 2 + nh
                    k2 = (ti * 2 + nh) % 2
                    p.tt("dve", tmpf[k2], psum[b_][:, :], bc[:, 1, nh * 512:(nh + 1) * 512], ALU.mult,
                         reads=[pbuf[b_], bbc], writes=[btmpf[k2]])
                    p.tt("dve", x1[:, gi, nh * 512:(nh + 1) * 512], tmpf[k2], x1[:, gi, nh * 512:(nh + 1) * 512], ALU.add,
                         reads=[btmpf[k2], bx1[gi]], writes=[bx1[gi]])
            p.bank_rr = 0
            for ti in range(4):
                gi = 4 * q + ti
                s = ti % 2
                p.act(junkf, x1[:, gi, :], AF.Square, accum=ssf[:, ti:ti + 1], reads=[bx1[gi]], writes=[bjf, bssf])
                p.act(ssf[:, ti:ti + 1], ssf[:, ti:ti + 1], AF.Sqrt, bias=NORM_EPS, scale=1.0 / 1024.0, reads=[bssf], writes=[bssf])
                p.op("dve", lambda e, ti=ti: e.reciprocal(out=ssf[:, ti:ti + 1], in_=ssf[:, ti:ti + 1]), [bssf], [bssf])
                p.stt(x2t[s], x1[:, gi, :], ssf[:, ti:ti + 1], bc[:, 2, :], ALU.mult, ALU.mult, reads=[bx1[gi], bssf, bbc], writes=[bx2[s]])
                out_dmas.append(p.dma(out_v[gi], x2t[s], reads=[bx2[s]]))
        p.finish(out_dmas + list(dbg_outs.values()))
        p.emit()
    return nc


def _finish(nc, p, out_d, dbg_outs):
    p.finish(list(dbg_outs.values()))
    p.emit()
    return nc


def _const_tables():
    c = np.zeros((128, C_END), np.float32)
    idx = np.arange(128)
    c[:, C_ID:C_ID + 128] = np.eye(128, dtype=np.float32)
    c[:, C_ONE:C_ONE + 128] = 1.0
    blk = (idx[:, None] // 64 == idx[None, :] // 64).astype(np.float32)
    c[:, C_BLK:C_BLK + 128] = blk
    sel = np.zeros((128, 2, 64), np.float32)
    for h in range(2):
        sel[64 * h + np.arange(64), h, np.arange(64)] = 1.0
    c[:, C_SEL:C_SEL + 128] = sel.reshape(128, 128)
    c[:, C_IDF:C_IDF + 64] = (idx[:, None] % 64 == np.arange(64)[None, :]).astype(np.float32)
    s = idx[:, None]
    t = idx[None, :]
    for d in range(2):
        mG = (t > s) if d == 0 else (t < s)
        mL = (t >= s) if d == 0 else (t <= s)
        c[:, C_M4[d]:C_M4[d] + 512] = np.concatenate([-1.0 * mG, mL, -1.0 * mG, mL], 1).astype(np.float32)
        mP = (t < s) if d == 0 else (t > s)
        c[:, C_MP[d]:C_MP[d] + 256] = -np.concatenate([mP, mP], 1).astype(np.float32)
    k = np.arange(64)
    ang = 2.0 * np.pi * np.outer(k, k) / 64.0
    cg = np.zeros((128, 128), np.float64)
    sg = np.zeros((128, 128), np.float64)
    for g in range(2):
        cg[64 * g:64 * g + 64, 64 * g:64 * g + 64] = np.cos(ang) / 8.0
        sg[64 * g:64 * g + 64, 64 * g:64 * g + 64] = -np.sin(ang) / 8.0
    cgs = np.concatenate([cg, sg], 1).astype(ml_dtypes.bfloat16)
    tt = np.arange(2048)
    angT = 2.0 * np.pi * ((np.outer(tt, tt) % 2048).astype(np.float64)) / 2048.0
    dft = np.stack([np.cos(angT), np.sin(angT)], 0) / np.sqrt(2048.0)
    return c, cgs, dft.astype(ml_dtypes.bfloat16)


_CONSTS = None


def make_in_maps(inputs, cores):
    global _CONSTS
    if _CONSTS is None:
        _CONSTS = _const_tables()
    cst, cgs, dft = _CONSTS
    f = lambda a: np.ascontiguousarray(np.asarray(a, dtype=np.float32))
    i = {k: np.asarray(v) for k, v in inputs.items()}
    shared = {
        "ada_w": f(i["ada_w"][0]),
        "ada_b": f(i["ada_b"][0][None, :]),
        "v1024": f(np.stack([i["norm1_g"][0], i["norm2_g"][0]], 0)),
        "v1536": f(i["rwkv_conv_w"][0]),
        "v512": f(np.concatenate([i["decay_w0"][0], i["iclr_a0"][0], i["k_k"][0][None], i["k_a"][0][None],
                                  i["r_k"][0].reshape(1, 512), i["gn_g"][0][None], i["gn_b"][0][None],
                                  i["fourier_g"][0][None]], 0)),
        "v5632": f(np.concatenate([i["ffn_conv_w"][0].reshape(9, 5632), i["ffn_conv_b"][0][None]], 0)),
        "final_g": f(i["final_g"][None, :]),
        "w_in": f(i["w_in"][0]),
        "decay_w2": f(i["decay_w2"][0].reshape(128, 512)),
        "iclr_a2": f(i["iclr_a2"][0].reshape(128, 512)),
        "gate_g2": f(i["gate_g2"][0]),
        "w_out": f(i["w_out"][0]),
        "ffn_w_up": f(i["ffn_w_up"][0].reshape(8, 128, 2, 22, 128).transpose(3, 1, 0, 2, 4).reshape(22, 128, 2048)),
        "ffn_w_down": f(i["ffn_w_down"][0]),
        "cst": cst, "cgs": cgs, "dft": dft,
    }
    maps = []
    for b in cores:
        m = dict(shared)
        m["x"] = f(i["x"][b])
        m["ctx"] = f(i["ctx"][b])
        m["c2"] = f(np.stack([i["c"][b], i["c_ctx"]], 0))
        maps.append(m)
    return maps


def kernel(**inputs):
    nc = build_program()
    maps = make_in_maps(inputs, list(range(8)))
    res = run_bass_kernel_spmd(nc, maps, core_ids=list(range(8)))
    return np.stack([np.asarray(r["out"], dtype=np.float32) for r in res.results], 0)
```
